# Optimizing a Trainium2 kernel written in Bass

```python
import jax, jax.numpy as jnp
from jax import lax
import numpy as np

D_MODEL = 2048
BATCH = 1
SEQ = 16384
DEPTH = 2

D_MIX = D_MODEL
SB_HEADS = 8
SB_HEAD_DIM = 128
SB_WIDTH = SB_HEADS * SB_HEAD_DIM
ML_HEADS = 4
ML_QK_DIM = 128
ML_V_DIM = 256
ML_QK_WIDTH = ML_HEADS * ML_QK_DIM
ML_V_WIDTH = ML_HEADS * ML_V_DIM
CONV_WIDTH = 4
ML_CHUNK = 64
Q_BLOCK = 128
D_FF = 5632
N_IN = 3 * SB_WIDTH + 2 * ML_QK_WIDTH + 2 * ML_V_WIDTH + 2 * ML_HEADS
N_SUB = 3
N_MOD = 3
ALPHA = (2 * DEPTH) ** 0.25
BETA = (8 * DEPTH) ** -0.25
LN_EPS = 1e-5

kernel_name = "hymba_style_stickbreak_mlstm_macaron_deepnorm_adaln"


def layer_norm(x, g, b):
    xf = x.astype(jnp.float32)
    mu = jnp.mean(xf, -1, keepdims=True)
    var = jnp.mean(jnp.square(xf - mu), -1, keepdims=True)
    return ((xf - mu) * lax.rsqrt(var + LN_EPS) * g + b).astype(x.dtype)


def head_norm(y, g):
    mu = jnp.mean(y, -1, keepdims=True)
    var = jnp.mean(jnp.square(y - mu), -1, keepdims=True)
    return (y - mu) * lax.rsqrt(var + LN_EPS) * g.reshape(y.shape[2:])


def swiglu_ffn(h, w1, w2):
    a, u = jnp.split(h @ w1, 2, axis=-1)
    return (jax.nn.silu(a) * u) @ w2


def causal_depthwise_conv(u, w, b):
    K = w.shape[0]
    S = u.shape[1]
    up = jnp.pad(u, ((0, 0), (K - 1, 0), (0, 0)))
    return b + sum(up[:, k:k + S, :] * w[k] for k in range(K))


def stick_breaking_attention(q, k, v):
    B, S, H, d = q.shape
    nb = S // Q_BLOCK
    qf = q.astype(jnp.float32) * (d ** -0.5)
    kf = k.astype(jnp.float32)
    vf = v.astype(jnp.float32)
    q_blocks = qf.reshape(B, nb, Q_BLOCK, H, d).transpose(1, 0, 3, 2, 4)
    key_pos = jnp.arange(S)

    def block(args):
        i, qb = args
        z = jnp.einsum('bhqd,bshd->bhqs', qb, kf)
        q_pos = i * Q_BLOCK + jnp.arange(Q_BLOCK)
        mask = key_pos[None, :] < q_pos[:, None]
        log_1mb = jnp.where(mask, jax.nn.log_sigmoid(-z), 0.0)
        log_stick = lax.cumsum(log_1mb, axis=log_1mb.ndim - 1, reverse=True) - log_1mb
        a = jnp.where(mask, jnp.exp(jax.nn.log_sigmoid(z) + log_stick), 0.0)
        return jnp.einsum('bhqs,bshd->bqhd', a, vf)

    out = lax.map(block, (jnp.arange(nb), q_blocks))
    return out.transpose(1, 0, 2, 3, 4).reshape(B, S, H, d)


def mlstm_chunkwise(q, k, v, log_i, log_f):
    B, S, H, dk = q.shape
    dv = v.shape[-1]
    L = ML_CHUNK
    nc = S // L

    def to_chunks(t):
        t = t.reshape((B, nc, L, H) + t.shape[3:])
        return jnp.moveaxis(t, (1, 3), (0, 2))

    causal = jnp.tril(jnp.ones((L, L), dtype=bool))

    def step(carry, xs):
        C, n, m = carry
        qc, kc, vc, ic, fc = xs
        b = jnp.cumsum(fc, axis=-1)
        log_d = jnp.where(causal, b[..., :, None] - b[..., None, :] + ic[..., None, :], -jnp.inf)
        log_g = b + m[..., None]
        m_t = jnp.maximum(jnp.max(log_d, -1), log_g)
        p = jnp.exp(log_d - m_t[..., None])
        g = jnp.exp(log_g - m_t)
        s = jnp.einsum('bhtd,bhsd->bhts', qc, kc) * p
        num = jnp.einsum('bhts,bhsv->bhtv', s, vc) + g[..., None] * jnp.einsum('bhtd,bhdv->bhtv', qc, C)
        den = jnp.sum(s, -1) + g * jnp.einsum('bhtd,bhd->bht', qc, n)
        h = num / jnp.maximum(jnp.abs(den), jnp.exp(-m_t))[..., None]
        b_last = b[..., -1]
        log_w = b_last[..., None] - b + ic
        m_new = jnp.maximum(b_last + m, jnp.max(log_w, -1))
        decay = jnp.exp(b_last + m - m_new)
        w = jnp.exp(log_w - m_new[..., None])
        C = decay[..., None, None] * C + jnp.einsum('bhs,bhsd,bhsv->bhdv', w, kc, vc)
        n = decay[..., None] * n + jnp.einsum('bhs,bhsd->bhd', w, kc)
        return (C, n, m_new), h

    init = (jnp.zeros((B, H, dk, dv), jnp.float32),
            jnp.zeros((B, H, dk), jnp.float32),
            jnp.zeros((B, H), jnp.float32))
    xs = (to_chunks(q), to_chunks(k), to_chunks(v), to_chunks(log_i), to_chunks(log_f))
    _, hs = lax.scan(step, init, xs)
    return jnp.moveaxis(hs, (0, 2), (1, 3)).reshape(B, S, H, dv)


def hybrid_mixer(h, w_in, conv_w, conv_b, gate_b, norm_g, w_out):
    B, S, _ = h.shape
    z = (h @ w_in).astype(jnp.float32)
    sizes = [SB_WIDTH, SB_WIDTH, SB_WIDTH, ML_QK_WIDTH, ML_QK_WIDTH,
             ML_V_WIDTH, ML_V_WIDTH, ML_HEADS, ML_HEADS]
    idx = np.cumsum(sizes)[:-1].tolist()
    q_sb, k_sb, v_sb, q_ml, k_ml, v_ml, o_ml, i_ml, f_ml = jnp.split(z, idx, axis=-1)

    y_sb = stick_breaking_attention(q_sb.reshape(B, S, SB_HEADS, SB_HEAD_DIM),
                                    k_sb.reshape(B, S, SB_HEADS, SB_HEAD_DIM),
                                    v_sb.reshape(B, S, SB_HEADS, SB_HEAD_DIM))
    y_sb = head_norm(y_sb, norm_g[:SB_WIDTH]).reshape(B, S, SB_WIDTH)

    qk = jax.nn.silu(causal_depthwise_conv(jnp.concatenate([q_ml, k_ml], -1), conv_w, conv_b))
    q_ml, k_ml = jnp.split(qk, 2, axis=-1)
    q_ml = q_ml * (ML_QK_DIM ** -0.5)
    log_i = i_ml + gate_b[:ML_HEADS]
    log_f = jax.nn.log_sigmoid(f_ml + gate_b[ML_HEADS:])
    y_ml = mlstm_chunkwise(q_ml.reshape(B, S, ML_HEADS, ML_QK_DIM),
                           k_ml.reshape(B, S, ML_HEADS, ML_QK_DIM),
                           v_ml.reshape(B, S, ML_HEADS, ML_V_DIM),
                           log_i, log_f)
    y_ml = jax.nn.sigmoid(o_ml) * head_norm(y_ml, norm_g[SB_WIDTH:]).reshape(B, S, ML_V_WIDTH)

    y = jnp.concatenate([y_sb, y_ml], axis=-1).astype(h.dtype)
    return y @ w_out


def setup_inputs(seed: int = 0) -> dict:
    key = jax.random.key(seed)
    ks = jax.random.split(key, 18)
    nrm = jax.random.normal
    f32 = jnp.float32
    x = nrm(ks[0], (BATCH, SEQ, D_MODEL), f32)
    c = nrm(ks[1], (BATCH, D_MODEL), f32)
    ada_w = nrm(ks[2], (DEPTH, D_MODEL, N_SUB * N_MOD * D_MODEL), f32) * (0.1 * D_MODEL ** -0.5)
    ada_b = nrm(ks[3], (DEPTH, N_SUB * N_MOD * D_MODEL), f32) * 0.01
    ln_g = 1.0 + 0.01 * nrm(ks[4], (DEPTH, N_SUB, D_MODEL), f32)
    ln_b = 0.01 * nrm(ks[5], (DEPTH, N_SUB, D_MODEL), f32)
    ffn1_w1 = nrm(ks[6], (DEPTH, D_MODEL, 2 * D_FF), f32) * D_MODEL ** -0.5
    ffn1_w2 = nrm(ks[7], (DEPTH, D_FF, D_MODEL), f32) * (BETA * D_FF ** -0.5)
    col_scale = jnp.concatenate([
        jnp.ones((2 * SB_WIDTH,), f32), jnp.full((SB_WIDTH,), BETA, f32),
        jnp.ones((2 * ML_QK_WIDTH,), f32), jnp.full((ML_V_WIDTH,), BETA, f32),
        jnp.ones((ML_V_WIDTH + 2 * ML_HEADS,), f32)])
    mix_w_in = nrm(ks[8], (DEPTH, D_MODEL, N_IN), f32) * (D_MODEL ** -0.5) * col_scale
    mlstm_conv_w = nrm(ks[9], (DEPTH, CONV_WIDTH, 2 * ML_QK_WIDTH), f32) * CONV_WIDTH ** -0.5
    mlstm_conv_b = 0.01 * nrm(ks[10], (DEPTH, 2 * ML_QK_WIDTH), f32)
    i_bias = 0.1 * nrm(ks[11], (DEPTH, ML_HEADS), f32)
    f_bias = jnp.linspace(3.0, 6.0, ML_HEADS, dtype=f32) + 0.1 * nrm(ks[12], (DEPTH, ML_HEADS), f32)
    mlstm_gate_b = jnp.concatenate([i_bias, f_bias], axis=-1)
    mix_norm_g = 1.0 + 0.01 * nrm(ks[13], (DEPTH, D_MIX), f32)
    mix_w_out = nrm(ks[14], (DEPTH, D_MIX, D_MODEL), f32) * (BETA * D_MIX ** -0.5)
    ffn2_w1 = nrm(ks[15], (DEPTH, D_MODEL, 2 * D_FF), f32) * D_MODEL ** -0.5
    ffn2_w2 = nrm(ks[16], (DEPTH, D_FF, D_MODEL), f32) * (BETA * D_FF ** -0.5)
    return {"x": x, "c": c, "ada_w": ada_w, "ada_b": ada_b, "ln_g": ln_g, "ln_b": ln_b,
            "ffn1_w1": ffn1_w1, "ffn1_w2": ffn1_w2, "mix_w_in": mix_w_in,
            "mlstm_conv_w": mlstm_conv_w, "mlstm_conv_b": mlstm_conv_b,
            "mlstm_gate_b": mlstm_gate_b, "mix_norm_g": mix_norm_g, "mix_w_out": mix_w_out,
            "ffn2_w1": ffn2_w1, "ffn2_w2": ffn2_w2}


def reference(x, c, ada_w, ada_b, ln_g, ln_b, ffn1_w1, ffn1_w2, mix_w_in,
              mlstm_conv_w, mlstm_conv_b, mlstm_gate_b, mix_norm_g, mix_w_out,
              ffn2_w1, ffn2_w2):
    B = x.shape[0]
    for l in range(DEPTH):
        mod = (jax.nn.silu(c) @ ada_w[l] + ada_b[l]).reshape(B, N_SUB, N_MOD, 1, D_MODEL)

        shift, scale, gate = mod[:, 0, 0], mod[:, 0, 1], mod[:, 0, 2]
        f = swiglu_ffn(x * (1 + scale) + shift, ffn1_w1[l], ffn1_w2[l])
        x = layer_norm(ALPHA * x + 0.5 * (1 + gate) * f, ln_g[l, 0], ln_b[l, 0])

        shift, scale, gate = mod[:, 1, 0], mod[:, 1, 1], mod[:, 1, 2]
        m = hybrid_mixer(x * (1 + scale) + shift, mix_w_in[l], mlstm_conv_w[l], mlstm_conv_b[l],
                         mlstm_gate_b[l], mix_norm_g[l], mix_w_out[l])
        x = layer_norm(ALPHA * x + (1 + gate) * m, ln_g[l, 1], ln_b[l, 1])

        shift, scale, gate = mod[:, 2, 0], mod[:, 2, 1], mod[:, 2, 2]
        f = swiglu_ffn(x * (1 + scale) + shift, ffn2_w1[l], ffn2_w2[l])
        x = layer_norm(ALPHA * x + 0.5 * (1 + gate) * f, ln_g[l, 2], ln_b[l, 2])
    return x
```

```python
import numpy as np
import ml_dtypes
import concourse.bass as bass
import concourse.mybir as mybir
from concourse.bass_utils import run_bass_kernel_spmd

F32 = mybir.dt.float32
BF16 = mybir.dt.bfloat16
AF = mybir.ActivationFunctionType
ALU = mybir.AluOpType
NPBF = ml_dtypes.bfloat16

ENGS = ("pe", "act", "dve", "pool", "sp")
NCORES = 8


class Buf:
    __slots__ = ("name", "w", "r", "dsem", "dcnt")

    def __init__(self, name):
        self.name = name
        self.w = {}
        self.r = {}
        self.dsem = None
        self.dcnt = 0


class Prog:
    def __init__(self, nc):
        self.nc = nc
        self.ops = {e: [] for e in ENGS}
        self.sems = {}
        self.cnt = {}
        self.seen = {e: {} for e in ENGS}
        for e in ENGS:
            self.sems[e] = nc.alloc_semaphore("sem_" + e)
            self.cnt[e] = 0
        self.nbuf = 0

    def buf(self, name=None):
        self.nbuf += 1
        return Buf((name or "b") + "_" + str(self.nbuf))

    def bufs(self, n, name="b"):
        return [self.buf(f"{name}{i}") for i in range(n)]

    def _deps(self, eng, reads, writes):
        need = {}

        def add(d, same_ok):
            for k, v in d.items():
                if k == eng and same_ok:
                    continue
                if need.get(k, 0) < v:
                    need[k] = v
        for b in reads:
            add(b.w, False)
        for b in writes:
            add(b.w, True)
            add(b.r, True)
        waits = []
        seen = self.seen[eng]
        for k, v in need.items():
            if seen.get(k, 0) < v:
                seen[k] = v
                waits.append((k, v))
        return waits

    def _record(self, tok, reads, writes):
        k, v = tok
        for b in reads:
            if b.r.get(k, 0) < v:
                b.r[k] = v
        for b in writes:
            if b.w.get(k, 0) < v:
                b.w[k] = v

    def op(self, eng, fns, reads=(), writes=()):
        if callable(fns):
            fns = [fns]
        waits = self._deps(eng, reads, writes)
        self.cnt[eng] += 1
        tok = (eng, self.cnt[eng])
        self._record(tok, reads, writes)
        self.ops[eng].append((waits, fns, (eng, 1)))
        return tok

    def dma(self, q, fn, prim, reads=(), writes=()):
        waits = self._deps(q, reads, writes)
        if prim.dsem is None:
            prim.dsem = "d_" + prim.name
            self.sems[prim.dsem] = self.nc.alloc_semaphore(prim.dsem)
        prim.dcnt += 16
        tok = (prim.dsem, prim.dcnt)
        self._record(tok, reads, writes)
        self.ops[q].append((waits, [fn], (prim.dsem, 16)))
        return tok

    def wait_all(self, eng, bufs):
        waits = self._deps(eng, (), bufs)
        self.ops[eng].append((waits, [], None))

    def emit(self):
        nc, sems, ops = self.nc, self.sems, self.ops

        def run(e, lst):
            for waits, fns, inc in lst:
                for k, v in waits:
                    e.wait_ge(sems[k], v)
                n = len(fns)
                for i, fn in enumerate(fns):
                    ins = fn(e)
                    if i == n - 1 and inc is not None:
                        ins.then_inc(sems[inc[0]], inc[1])

        with nc.Block() as block:
            @block.tensor
            def _(e):
                run(e, ops["pe"])

            @block.scalar
            def _(e):
                run(e, ops["act"])

            @block.vector
            def _(e):
                run(e, ops["dve"])

            @block.gpsimd
            def _(e):
                run(e, ops["pool"])

            @block.sync
            def _(e):
                run(e, ops["sp"])


def cdma(e, out, in_):
    n = out.shape[-1]
    if n > 2048:
        for b in (2048, 1024, 512, 256, 128):
            if n % b == 0:
                break
        out = out.rearrange("p (a b) -> p a b", b=b)
        in_ = in_.rearrange("p (a b) -> p a b", b=b)
    return e.dma_start(out=out, in_=in_)


class Ctx:
    def __init__(self, nc, p, bf16_bank=None):
        self.nc, self.p = nc, p
        self.banks = [(nc.alloc_psum_tensor(f"bank{i}", [128, 1024], BF16).ap() if i == bf16_bank else
                       nc.alloc_psum_tensor(f"bank{i}", [128, 512], F32).ap()) for i in range(8)]
        self.bbufs = p.bufs(8, "bank")
        self.ones = nc.alloc_sbuf_tensor("ones_f32", [128, 128], F32).ap()
        self.onesB = nc.alloc_sbuf_tensor("ones_bf16", [128, 128], BF16).ap()
        self.b_ones = p.buf("ones")
        p.op("pool", lambda e: e.memset(self.ones, 1.0), writes=[self.b_ones])
        p.op("pool", lambda e: e.memset(self.onesB, 1.0), writes=[self.b_ones])
        self.n_sb = 0

    def sb(self, shape, dt, name=None):
        self.n_sb += 1
        return self.nc.alloc_sbuf_tensor((name or "sb") + f"_{self.n_sb}", shape, dt).ap()


def emit_ln_phase(cx, rT, outT, vec_g, vec_b, b_vec, b_r_dram, b_out_dram, GC, NT, LS, eps, ngroups=1,
                  banks=(6, 7), tag="ln"):
    nc, p = cx.nc, cx.p
    nsub = NT // LS
    Dg = GC * 128
    rl = [cx.sb([128, GC, LS], F32, f"{tag}_rl{i}") for i in range(2)]
    b_rl = p.bufs(2, tag + "rl")
    sq = [cx.sb([128, LS], F32, f"{tag}_sq{i}") for i in range(2)]
    b_sq = p.bufs(2, tag + "sq")
    mean = cx.sb([128, LS], F32, tag + "_mean"); b_mean = p.buf(tag + "mean")
    m2 = cx.sb([128, LS], F32, tag + "_m2"); b_m2 = p.buf(tag + "m2")
    var = cx.sb([128, LS], F32, tag + "_var"); b_var = p.buf(tag + "var")
    rstd = cx.sb([128, LS], F32, tag + "_rstd"); b_rstd = p.buf(tag + "rstd")
    nmr = cx.sb([128, LS], F32, tag + "_nmr"); b_nmr = p.buf(tag + "nmr")
    t1 = [cx.sb([128, LS], F32, f"{tag}_t1{i}") for i in range(2)]; b_t1 = p.bufs(2, tag + "t1")
    t2 = [cx.sb([128, LS], F32, f"{tag}_t2{i}") for i in range(2)]; b_t2 = p.bufs(2, tag + "t2")
    ot = [cx.sb([128, GC, LS], F32, f"{tag}_ot{i}") for i in range(2)]; b_ot = p.bufs(2, tag + "ot")
    bs, bq = banks
    k = 0
    it = 0
    for g in range(ngroups):
        rTg = rT[g * Dg:(g + 1) * Dg, :].rearrange("(c p) t -> p c t", p=128)
        oTg = outT[g * Dg:(g + 1) * Dg, :].rearrange("(c p) t -> p c t", p=128)
        for s in range(nsub):
            i2 = it % 2
            it += 1
            tsl = slice(s * LS, (s + 1) * LS)
            p.dma("sp", lambda e, i2=i2, tsl=tsl, rTg=rTg: e.dma_start(out=rl[i2], in_=rTg[:, :, tsl]),
                  b_rl[i2], reads=[b_r_dram], writes=[b_rl[i2]])
            for c in range(GC):
                q2 = k % 2
                k += 1
                p.op("act", lambda e, i2=i2, c=c, q2=q2: e.activation(out=sq[q2], in_=rl[i2][:, c, :], func=AF.Square),
                     reads=[b_rl[i2]], writes=[b_sq[q2]])
                p.op("pe", [lambda e, i2=i2, c=c: e.matmul(cx.banks[bs][:, 0:LS], lhsT=cx.ones, rhs=rl[i2][:, c, :],
                                                           start=(c == 0), stop=(c == GC - 1)),
                            lambda e, q2=q2, c=c: e.matmul(cx.banks[bq][:, 0:LS], lhsT=cx.ones, rhs=sq[q2],
                                                           start=(c == 0), stop=(c == GC - 1))],
                     reads=[b_rl[i2], b_sq[q2], cx.b_ones], writes=[cx.bbufs[bs], cx.bbufs[bq]])
            p.op("dve", lambda e: e.tensor_scalar(mean, cx.banks[bs][:, 0:LS], 1.0 / Dg, None, ALU.mult),
                 writes=[b_mean, cx.bbufs[bs]])
            p.op("dve", lambda e: e.tensor_tensor(m2, mean, mean, ALU.mult), reads=[b_mean], writes=[b_m2])
            p.op("dve", lambda e: e.tensor_scalar(var, cx.banks[bq][:, 0:LS], 1.0 / Dg, eps, ALU.mult, ALU.add),
                 writes=[b_var, cx.bbufs[bq]])
            p.op("dve", lambda e: e.tensor_tensor(m2, var, m2, ALU.subtract), reads=[b_var, b_m2], writes=[b_m2])
            p.op("act", lambda e: e.activation(out=var, in_=m2, func=AF.Sqrt), reads=[b_m2], writes=[b_var])
            p.op("dve", lambda e: e.reciprocal(rstd, var), reads=[b_var], writes=[b_rstd])
            p.op("dve", lambda e: e.scalar_tensor_tensor(nmr, mean, -1.0, rstd, ALU.mult, ALU.mult),
                 reads=[b_mean, b_rstd], writes=[b_nmr])
            for c in range(GC):
                q2 = k % 2
                k += 1
                p.op("dve", lambda e, i2=i2, c=c, q2=q2: e.tensor_tensor(t1[q2], rl[i2][:, c, :], rstd, ALU.mult),
                     reads=[b_rl[i2], b_rstd], writes=[b_t1[q2]])
                p.op("pool", lambda e, q2=q2: e.tensor_tensor(t2[q2], t1[q2], nmr, ALU.add),
                     reads=[b_t1[q2], b_nmr], writes=[b_t2[q2]])
                vi = g * GC + c
                p.op("act", lambda e, i2=i2, c=c, q2=q2, vi=vi: e.activation(
                    out=ot[i2][:, c, :], in_=t2[q2], func=AF.Identity,
                    bias=vec_b[:, vi:vi + 1], scale=vec_g[:, vi:vi + 1]),
                    reads=[b_t2[q2], b_vec], writes=[b_ot[i2]])
            p.dma("sp", lambda e, i2=i2, tsl=tsl, oTg=oTg: e.dma_start(out=oTg[:, :, tsl], in_=ot[i2]),
                  b_ot[i2], reads=[b_ot[i2]], writes=[b_out_dram])


class StageC:
    def __init__(self, cx, KC, DC, T, NS, tag):
        p = cx.p
        self.cx, self.KC, self.DC, self.T, self.NS, self.tag = cx, KC, DC, T, NS, tag
        nts = T // NS
        self.nts = nts
        self.w2t = [cx.sb([128, KC * 128], BF16, f"{tag}_w2t{i}") for i in range(2)]; self.b_w2t = p.bufs(2, tag + "w2t")
        self.xres = [cx.sb([128, NS], F32, f"{tag}_xres{i}") for i in range(2)]; self.b_xres = p.bufs(2, tag + "xres")
        self.rt = [cx.sb([128, NS], F32, f"{tag}_rt{i}") for i in range(2)]; self.b_rt = p.bufs(2, tag + "rt")
        self.rbf = [cx.sb([128, NS], BF16, f"{tag}_rbf{i}") for i in range(2)]; self.b_rbf = p.bufs(2, tag + "rbf")
        self.sqbf = [cx.sb([128, NS], BF16, f"{tag}_sqbf{i}") for i in range(2)]; self.b_sqbf = p.bufs(2, tag + "sqbf")
        self.asum = cx.sb([128, T], F32, tag + "_asum"); self.b_asum = p.bufs(nts, tag + "asum")
        self.asq = cx.sb([128, T], F32, tag + "_asq"); self.b_asq = p.bufs(nts, tag + "asq")
        self.nmr = cx.sb([128, T], F32, tag + "_nmr"); self.b_nmr = p.bufs(nts, tag + "nmr")
        self.nl = [cx.sb([128, NS], F32, f"{tag}_nl{i}") for i in range(3)]; self.b_nl = p.bufs(3, tag + "nl")
        self.kw2 = self.ky = self.kr = self.kn = 0

    def run(self, gT, b_gT, w2r, xTv, rTv, oTv, gw, b_gw, vg, vb, b_vgb, t0, eps, b_x, b_r, b_xo,
            bank_y=(4, 5), bank_s=(6, 7)):
        cx, p = self.cx, self.cx.p
        KC, DC, NS, nts = self.KC, self.DC, self.NS, self.nts
        banks, bb = cx.banks, cx.bbufs
        Dtot = DC * 128
        pend = None

        def stats(c, ts, r2):
            sl = slice(ts * NS, (ts + 1) * NS)
            bs, bq = bank_s
            p.op("pe", [lambda e: e.matmul(banks[bs][:, 0:NS], lhsT=cx.onesB, rhs=self.rbf[r2], start=True, stop=True),
                        lambda e: e.matmul(banks[bq][:, 0:NS], lhsT=cx.onesB, rhs=self.sqbf[r2], start=True, stop=True)],
                 reads=[self.b_rbf[r2], self.b_sqbf[r2], cx.b_ones], writes=[bb[bs], bb[bq]])
            if c == 0:
                p.op("dve", lambda e: e.tensor_copy(self.asum[:, sl], banks[bs][:, 0:NS]), writes=[self.b_asum[ts], bb[bs]])
                p.op("dve", lambda e: e.tensor_copy(self.asq[:, sl], banks[bq][:, 0:NS]), writes=[self.b_asq[ts], bb[bq]])
            else:
                p.op("dve", lambda e: e.tensor_tensor(self.asum[:, sl], self.asum[:, sl], banks[bs][:, 0:NS], ALU.add),
                     reads=[self.b_asum[ts]], writes=[self.b_asum[ts], bb[bs]])
                p.op("dve", lambda e: e.tensor_tensor(self.asq[:, sl], self.asq[:, sl], banks[bq][:, 0:NS], ALU.add),
                     reads=[self.b_asq[ts]], writes=[self.b_asq[ts], bb[bq]])

        for c in range(DC):
            i2 = self.kw2 % 2; self.kw2 += 1
            p.dma("pool", lambda e, i2=i2, c=c: cdma(e, self.w2t[i2], w2r[c]), self.b_w2t[i2], writes=[self.b_w2t[i2]])
            for ts in range(nts):
                by = bank_y[self.ky % 2]; self.ky += 1
                sl = slice(ts * NS, (ts + 1) * NS)
                gsl = slice(t0 + ts * NS, t0 + (ts + 1) * NS)
                fns = []
                for h in range(KC):
                    fns.append(lambda e, i2=i2, h=h, sl=sl, by=by: e.matmul(
                        banks[by][:, 0:NS], lhsT=self.w2t[i2][:, h * 128:(h + 1) * 128], rhs=gT[:, h, sl],
                        start=(h == 0), stop=(h == KC - 1)))
                p.op("pe", fns, reads=[self.b_w2t[i2], b_gT[ts]], writes=[bb[by]])
                if pend is not None:
                    stats(*pend)
                r2 = self.kr % 2; self.kr += 1
                p.dma("sp", lambda e, r2=r2, c=c, gsl=gsl: e.dma_start(out=self.xres[r2], in_=xTv[:, c, gsl]),
                      self.b_xres[r2], reads=[b_x], writes=[self.b_xres[r2]])
                p.op("dve", lambda e, r2=r2, by=by, c=c: e.scalar_tensor_tensor(
                    self.rt[r2], banks[by][:, 0:NS], gw[:, c:c + 1], self.xres[r2], ALU.mult, ALU.add),
                    reads=[self.b_xres[r2], b_gw], writes=[self.b_rt[r2], bb[by]])
                p.op("pool", lambda e, r2=r2: e.tensor_copy(self.rbf[r2], self.rt[r2]), reads=[self.b_rt[r2]], writes=[self.b_rbf[r2]])
                p.op("act", lambda e, r2=r2: e.activation(out=self.sqbf[r2], in_=self.rt[r2], func=AF.Square),
                     reads=[self.b_rt[r2]], writes=[self.b_sqbf[r2]])
                p.dma("sp", lambda e, r2=r2, c=c, gsl=gsl: e.dma_start(out=rTv[:, c, gsl], in_=self.rt[r2]),
                      self.b_rt[r2], reads=[self.b_rt[r2]], writes=[b_r])
                pend = (c, ts, r2)
        stats(*pend)
        for ts in range(nts):
            sl = slice(ts * NS, (ts + 1) * NS)
            A, Q, M = self.asum[:, sl], self.asq[:, sl], self.nmr[:, sl]
            rd = [self.b_asum[ts], self.b_asq[ts], self.b_nmr[ts]]
            p.op("dve", lambda e, A=A: e.tensor_scalar(A, A, 1.0 / Dtot, None, ALU.mult), reads=rd, writes=rd)
            p.op("dve", lambda e, A=A, M=M: e.tensor_tensor(M, A, A, ALU.mult), reads=rd, writes=rd)
            p.op("dve", lambda e, Q=Q: e.tensor_scalar(Q, Q, 1.0 / Dtot, eps, ALU.mult, ALU.add), reads=rd, writes=rd)
            p.op("dve", lambda e, Q=Q, M=M: e.tensor_tensor(Q, Q, M, ALU.subtract), reads=rd, writes=rd)
            p.op("act", lambda e, Q=Q: e.activation(out=Q, in_=Q, func=AF.Sqrt), reads=rd, writes=rd)
            p.op("dve", lambda e, Q=Q: e.reciprocal(Q, Q), reads=rd, writes=rd)
            p.op("dve", lambda e, A=A, Q=Q, M=M: e.scalar_tensor_tensor(M, A, -1.0, Q, ALU.mult, ALU.mult), reads=rd, writes=rd)
        for ts in range(nts):
            sl = slice(ts * NS, (ts + 1) * NS)
            gsl = slice(t0 + ts * NS, t0 + (ts + 1) * NS)
            rd = [self.b_asum[ts], self.b_asq[ts], self.b_nmr[ts]]
            for c in range(DC):
                n3 = self.kn % 3; self.kn += 1
                nl = self.nl[n3]; bnl = self.b_nl[n3]
                p.dma("sp", lambda e, nl=nl, c=c, gsl=gsl: e.dma_start(out=nl, in_=rTv[:, c, gsl]), bnl, reads=[b_r], writes=[bnl])
                p.op("dve", lambda e, nl=nl, sl=sl: e.tensor_tensor(nl, nl, self.asq[:, sl], ALU.mult), reads=[bnl] + rd, writes=[bnl])
                p.op("pool", lambda e, nl=nl, sl=sl: e.tensor_tensor(nl, nl, self.nmr[:, sl], ALU.add), reads=[bnl] + rd, writes=[bnl])
                p.op("act", lambda e, nl=nl, c=c: e.activation(out=nl, in_=nl, func=AF.Identity, bias=vb[:, c:c + 1], scale=vg[:, c:c + 1]),
                     reads=[bnl, b_vgb], writes=[bnl])
                p.dma("sp", lambda e, nl=nl, c=c, gsl=gsl: e.dma_start(out=oTv[:, c, gsl], in_=nl), bnl, reads=[bnl], writes=[b_xo])


def emit_modulate(cx, xTv, xbf, b_xbf, s1, sh, b_vec, xin, b_xin, kx, t0, nts, NS, DC, b_x):
    p = cx.p
    for ts in range(nts):
        for c in range(DC):
            i3 = kx[0] % 3; kx[0] += 1
            sl = slice(t0 + ts * NS, t0 + (ts + 1) * NS)
            p.dma("sp", lambda e, i3=i3, c=c, sl=sl: e.dma_start(out=xin[i3], in_=xTv[:, c, sl]),
                  b_xin[i3], reads=[b_x], writes=[b_xin[i3]])
            p.op("act", lambda e, i3=i3, c=c, ts=ts: e.activation(
                out=xbf[:, c, ts * NS:(ts + 1) * NS], in_=xin[i3], func=AF.Identity,
                bias=sh[:, c:c + 1], scale=s1[:, c:c + 1]),
                reads=[b_xin[i3], b_vec], writes=[b_xbf[ts]])


def emit_ffn(cx, xT, w1r, w2r, vecs, rT, xoT, b_x, b_r, b_xo, D, DFF, NT, T, resw, alpha, ln_eps, tag="f"):
    nc, p = cx.nc, cx.p
    DC, HC = D // 128, DFF // 128
    NS = min(512, T)
    assert T % NS == 0 and NT % T == 0
    nts = T // NS
    vec = cx.sb([128, 5 * DC], F32, tag + "_vec"); b_vec = p.buf(tag + "vec")
    der = cx.sb([128, 2 * DC], F32, tag + "_der")
    p.dma("sp", lambda e: e.dma_start(out=vec, in_=vecs), b_vec, writes=[b_vec])
    p.op("dve", lambda e: e.tensor_scalar(der[:, 0:DC], vec[:, 0:DC], 1.0, None, ALU.add), reads=[b_vec], writes=[b_vec])
    p.op("dve", lambda e: e.tensor_scalar(der[:, DC:2 * DC], vec[:, 2 * DC:3 * DC], 1.0, resw / alpha, ALU.add, ALU.mult),
         reads=[b_vec], writes=[b_vec])
    xin = [cx.sb([128, NS], F32, f"{tag}_xin{i}") for i in range(3)]; b_xin = p.bufs(3, tag + "xin")
    xbf = cx.sb([128, DC, T], BF16, tag + "_xbf"); b_xbf = p.bufs(nts, tag + "xbf")
    w1t = [cx.sb([128, DC * 256], BF16, f"{tag}_w1t{i}") for i in range(2)]; b_w1t = p.bufs(2, tag + "w1t")
    gT = cx.sb([128, HC, T], BF16, tag + "_gT"); b_gT = p.bufs(nts, tag + "gT")
    sa = [cx.sb([128, NS], F32, f"{tag}_sa{i}") for i in range(2)]; b_sa = p.bufs(2, tag + "sa")
    stc = StageC(cx, HC, DC, T, NS, tag + "C")
    xTv = xT.rearrange("(c p) t -> p c t", p=128)
    rTv = rT.rearrange("(c p) t -> p c t", p=128)
    oTv = xoT.rearrange("(c p) t -> p c t", p=128)
    kx = [0]
    ks = kw1 = kb = 0
    bank_a, bank_u = (0, 1), (2, 3)
    for tt in range(NT // T):
        t0 = tt * T
        emit_modulate(cx, xTv, xbf, b_xbf, der[:, 0:DC], vec[:, DC:2 * DC], b_vec, xin, b_xin, kx, t0, nts, NS, DC, b_x)
        for j in range(HC):
            i2 = kw1 % 2; kw1 += 1
            p.dma("pool", lambda e, i2=i2, j=j: cdma(e, w1t[i2], w1r[j]), b_w1t[i2], writes=[b_w1t[i2]])
            for ts in range(nts):
                ba, bu = bank_a[kb % 2], bank_u[kb % 2]; kb += 1
                sl = slice(ts * NS, (ts + 1) * NS)
                fns = []
                for c in range(DC):
                    fns.append(lambda e, i2=i2, c=c, sl=sl, ba=ba: e.matmul(
                        cx.banks[ba][:, 0:NS], lhsT=w1t[i2][:, c * 256:c * 256 + 128], rhs=xbf[:, c, sl],
                        start=(c == 0), stop=(c == DC - 1)))
                p.op("pe", fns, reads=[b_w1t[i2], b_xbf[ts]], writes=[cx.bbufs[ba]])
                fns = []
                for c in range(DC):
                    fns.append(lambda e, i2=i2, c=c, sl=sl, bu=bu: e.matmul(
                        cx.banks[bu][:, 0:NS], lhsT=w1t[i2][:, c * 256 + 128:c * 256 + 256], rhs=xbf[:, c, sl],
                        start=(c == 0), stop=(c == DC - 1)))
                p.op("pe", fns, reads=[b_w1t[i2], b_xbf[ts]], writes=[cx.bbufs[bu]])
                s2 = ks % 2; ks += 1
                p.op("act", lambda e, s2=s2, ba=ba: e.activation(out=sa[s2], in_=cx.banks[ba][:, 0:NS], func=AF.Silu),
                     writes=[b_sa[s2], cx.bbufs[ba]])
                p.op("dve", lambda e, s2=s2, bu=bu, j=j, sl=sl: e.tensor_tensor(gT[:, j, sl], sa[s2], cx.banks[bu][:, 0:NS], ALU.mult),
                     reads=[b_sa[s2]], writes=[b_gT[ts], cx.bbufs[bu]])
        stc.run(gT, b_gT, w2r, xTv, rTv, oTv, der[:, DC:2 * DC], b_vec, vec[:, 3 * DC:4 * DC], vec[:, 4 * DC:5 * DC], b_vec,
                t0, ln_eps / (alpha * alpha), b_x, b_r, b_xo)


def emit_inproj(cx, xT, vecs, wfm, wtm, qkT, vsb, mqkT, vml, oT, gTd, b_x, b_out, D, NT, qscale, tag="ip"):
    nc, p = cx.nc, cx.p
    DC = D // 128
    NS = min(512, NT)
    nts = NT // NS
    vec = cx.sb([128, 2 * DC], F32, tag + "_vec"); b_vec = p.buf(tag + "vec")
    s1 = cx.sb([128, DC], F32, tag + "_s1")
    p.dma("sp", lambda e: e.dma_start(out=vec, in_=vecs), b_vec, writes=[b_vec])
    p.op("dve", lambda e: e.tensor_scalar(s1, vec[:, 0:DC], 1.0, None, ALU.add), reads=[b_vec], writes=[b_vec])
    xin = [cx.sb([128, NS], F32, f"{tag}_xin{i}") for i in range(3)]; b_xin = p.bufs(3, tag + "xin")
    xbf = cx.sb([128, DC, NT], BF16, tag + "_xbf"); b_xbf = p.bufs(nts, tag + "xbf")
    xTv = xT.rearrange("(c p) t -> p c t", p=128)
    emit_modulate(cx, xTv, xbf, b_xbf, s1, vec[:, DC:2 * DC], b_vec, xin, b_xin, [0], 0, nts, NS, DC, b_x)
    wf = [cx.sb([128, DC * 128], BF16, f"{tag}_wf{i}") for i in range(3)]; b_wf = p.bufs(3, tag + "wf")
    obf = [cx.sb([128, NS], BF16, f"{tag}_obf{i}") for i in range(3)]; b_obf = p.bufs(3, tag + "obf")
    of32 = [cx.sb([128, NS], F32, f"{tag}_of{i}") for i in range(3)]; b_of = p.bufs(3, tag + "of")
    kbk = ko = 0
    for j in range(33):
        i3 = j % 3
        p.dma("pool", lambda e, i3=i3, j=j: cdma(e, wf[i3], wfm[j]), b_wf[i3], writes=[b_wf[i3]])
        for ts in range(nts):
            bk = kbk % 4; kbk += 1
            sl = slice(ts * NS, (ts + 1) * NS)
            fns = []
            for c in range(DC):
                fns.append(lambda e, i3=i3, c=c, sl=sl, bk=bk: e.matmul(
                    cx.banks[bk][:, 0:NS], lhsT=wf[i3][:, c * 128:(c + 1) * 128], rhs=xbf[:, c, sl],
                    start=(c == 0), stop=(c == DC - 1)))
            p.op("pe", fns, reads=[b_wf[i3], b_xbf[ts]], writes=[cx.bbufs[bk]])
            o3 = ko % 3; ko += 1
            eng = "act" if (ko % 2 == 0) else "dve"
            src = cx.banks[bk][:, 0:NS]
            if j < 16:
                dst, bd, ddst = obf[o3], b_obf[o3], qkT[j][:, sl]
                sc = qscale if j < 8 else 1.0
            elif j < 24:
                dst, bd, ddst, sc = of32[o3], b_of[o3], mqkT[j - 16][:, sl], 1.0
            elif j < 32:
                dst, bd, ddst, sc = of32[o3], b_of[o3], oT[j - 24][:, sl], 1.0
            else:
                dst, bd, ddst, sc = of32[o3][0:8, :], b_of[o3], gTd[:, sl], 1.0
                src = cx.banks[bk][0:8, 0:NS]
            if eng == "act":
                p.op("act", lambda e, dst=dst, src=src, sc=sc: e.activation(out=dst, in_=src, func=AF.Copy, scale=sc),
                     writes=[bd, cx.bbufs[bk]])
            else:
                p.op("dve", lambda e, dst=dst, src=src, sc=sc: e.tensor_scalar(dst, src, sc, None, ALU.mult),
                     writes=[bd, cx.bbufs[bk]])
            p.dma("sp", lambda e, dst=dst, ddst=ddst: e.dma_start(out=ddst, in_=dst), bd, reads=[bd], writes=[b_out])
    wt = [cx.sb([128, DC * 512], BF16, f"{tag}_wt{i}") for i in range(2)]; b_wt = p.bufs(2, tag + "wt")
    otm = [cx.sb([128, 512], BF16, f"{tag}_otm{i}") for i in range(3)]; b_otm = p.bufs(3, tag + "otm")
    for s in range(4):
        i2 = s % 2
        p.dma("pool", lambda e, i2=i2, s=s: cdma(e, wt[i2], wtm[s]), b_wt[i2], writes=[b_wt[i2]])
        dd = vsb if s < 2 else vml
        for tc in range(NT // 128):
            bk = 4 + kbk % 4; kbk += 1
            fns = []
            for c in range(DC):
                fns.append(lambda e, i2=i2, c=c, tc=tc, bk=bk: e.matmul(
                    cx.banks[bk], lhsT=xbf[:, c, tc * 128:(tc + 1) * 128], rhs=wt[i2][:, c * 512:(c + 1) * 512],
                    start=(c == 0), stop=(c == DC - 1)))
            p.op("pe", fns, reads=[b_wt[i2]] + b_xbf, writes=[cx.bbufs[bk]])
            o3 = ko % 3; ko += 1
            eng = "act" if (ko % 2 == 0) else "dve"
            if eng == "act":
                p.op("act", lambda e, o3=o3, bk=bk: e.activation(out=otm[o3], in_=cx.banks[bk], func=AF.Copy),
                     writes=[b_otm[o3], cx.bbufs[bk]])
            else:
                p.op("dve", lambda e, o3=o3, bk=bk: e.tensor_copy(otm[o3], cx.banks[bk]), writes=[b_otm[o3], cx.bbufs[bk]])
            p.dma("sp", lambda e, o3=o3, tc=tc, s=s, dd=dd: e.dma_start(
                out=dd[tc * 128:(tc + 1) * 128, (s % 2) * 512:(s % 2 + 1) * 512], in_=otm[o3]),
                b_otm[o3], reads=[b_otm[o3]], writes=[b_out])


def emit_sb_attn(cx, qT_d, kT_d, v_d, yT_d, b_in, b_out, S, bigQ, bigK, bigV, b_bigQ, b_bigK, b_bigV, tag="sb"):
    nc, p = cx.nc, cx.p
    NB, NQ = S // 128, S // 512
    U = cx.sb([128, 128], BF16, tag + "_U"); Lr = cx.sb([128, 128], BF16, tag + "_Lr")
    b_c = p.buf(tag + "const")
    p.op("pool", lambda e: e.memset(U, 1.0), writes=[b_c])
    p.op("pool", lambda e: e.memset(Lr, 1.0), writes=[b_c])
    p.op("pool", lambda e: e.affine_select(U, U, [[-1, 128]], ALU.is_gt, 0.0, base=0, channel_multiplier=1),
         reads=[b_c], writes=[b_c])
    p.op("pool", lambda e: e.affine_select(Lr, Lr, [[1, 128]], ALU.is_ge, 0.0, base=0, channel_multiplier=-1),
         reads=[b_c], writes=[b_c])
    npc = max(1, S // 4096)
    pc = S // npc
    for i in range(npc):
        sl = slice(i * pc, (i + 1) * pc)
        p.dma("sp", lambda e, sl=sl: e.dma_start(out=bigQ[:, sl], in_=qT_d[:, sl]), b_bigQ, reads=[b_in], writes=[b_bigQ])
        p.dma("sp", lambda e, sl=sl: e.dma_start(out=bigK[:, sl], in_=kT_d[:, sl]), b_bigK, reads=[b_in], writes=[b_bigK])
    v_v = v_d.rearrange("(b p) d -> p b d", p=128)
    nvb = max(1, NB // 32)
    for i in range(nvb):
        sl = slice(i * (NB // nvb), (i + 1) * (NB // nvb))
        p.dma("sp", lambda e, sl=sl: e.dma_start(out=bigV[:, sl, :], in_=v_v[:, sl, :]), b_bigV, reads=[b_in], writes=[b_bigV])
    NR = 3
    e32 = [cx.sb([128, 512], F32, f"{tag}_e{i}") for i in range(NR)]; b_e = p.bufs(NR, tag + "e")
    sp32 = [cx.sb([128, 512], F32, f"{tag}_sp{i}") for i in range(NR)]; b_sp = p.bufs(NR, tag + "sp")
    spb = [cx.sb([128, 512], BF16, f"{tag}_spb{i}") for i in range(NR)]; b_spb = p.bufs(NR, tag + "spb")
    w32 = [cx.sb([128, 512], F32, f"{tag}_w{i}") for i in range(NR)]; b_w = p.bufs(NR, tag + "w")
    w2 = [cx.sb([128, 512], F32, f"{tag}_w2{i}") for i in range(NR)]; b_w2 = p.bufs(NR, tag + "w2")
    ab = [cx.sb([128, 512], BF16, f"{tag}_ab{i}") for i in range(NR)]; b_ab = p.bufs(NR, tag + "ab")
    ot = [cx.sb([128, 512], F32, f"{tag}_ot{i}") for i in range(2)]; b_ot = p.bufs(2, tag + "ot")
    ZB = (0, 1, 2)
    RB = 3
    OB = (4, 5)
    banks, bb = cx.banks, cx.bbufs
    steps = []
    for tq in range(NQ):
        kbs = list(range(4 * tq + 3, -1, -1))
        for n, kb in enumerate(kbs):
            o = kb - 4 * tq
            steps.append(dict(tq=tq, kb=kb, first=(n == 0), last=(n == len(kbs) - 1), diag=(o >= 0), cs=max(o, 0) * 128))
    N = len(steps)

    def QK(i):
        s = steps[i]; z = ZB[i % 3]; cs = s["cs"]; tq, kb = s["tq"], s["kb"]
        p.op("pe", lambda e: e.matmul(banks[z][:, cs:512], lhsT=bigK[:, kb * 128:(kb + 1) * 128],
                                      rhs=bigQ[:, tq * 512 + cs:(tq + 1) * 512], start=True, stop=True),
             reads=[b_bigK, b_bigQ], writes=[bb[z]])

    def A(i):
        s = steps[i]; z = ZB[i % 3]; r = i % NR; cs = s["cs"]
        p.op("act", lambda e: e.activation(out=e32[r][:, cs:], in_=banks[z][:, cs:512], func=AF.Exp),
             writes=[b_e[r], bb[z]])
        p.op("act", lambda e: e.activation(out=sp32[r][:, cs:], in_=e32[r][:, cs:], func=AF.Ln, bias=1.0),
             reads=[b_e[r]], writes=[b_sp[r]])
        if s["diag"]:
            p.op("pool", lambda e: e.affine_select(spb[r][:, cs:], sp32[r][:, cs:], [[1, 512 - cs]], ALU.is_gt, 0.0,
                                                   base=0, channel_multiplier=-1),
                 reads=[b_sp[r]], writes=[b_spb[r]])
        else:
            p.op("pool", lambda e: e.tensor_copy(spb[r], sp32[r]), reads=[b_sp[r]], writes=[b_spb[r]])
        p.op("dve", lambda e: e.tensor_tensor(w32[r][:, cs:], banks[z][:, cs:512], sp32[r][:, cs:], ALU.subtract),
             reads=[b_sp[r]], writes=[b_w[r], bb[z]])

    def Pa(i):
        s = steps[i]; r = i % NR; cs = s["cs"]
        p.op("pe", lambda e: e.matmul(banks[RB][:, cs:512], lhsT=U, rhs=spb[r][:, cs:], start=s["first"], stop=False,
                                      skip_group_check=True),
             reads=[b_spb[r], b_c], writes=[bb[RB]])

    def B(i):
        s = steps[i]; r = i % NR; cs = s["cs"]
        p.op("dve", lambda e: e.tensor_tensor(w2[r][:, cs:], w32[r][:, cs:], banks[RB][:, cs:512], ALU.subtract),
             reads=[b_w[r]], writes=[b_w2[r], bb[RB]])
        p.op("act", lambda e: e.activation(out=ab[r][:, cs:], in_=w2[r][:, cs:], func=AF.Exp),
             reads=[b_w2[r]], writes=[b_ab[r]])
        if s["diag"]:
            p.op("pool", lambda e: e.affine_select(ab[r][:, cs:], ab[r][:, cs:], [[1, 512 - cs]], ALU.is_gt, 0.0,
                                                   base=0, channel_multiplier=-1),
                 reads=[b_ab[r]], writes=[b_ab[r]])

    def Pb(i):
        s = steps[i]; r = i % NR; cs = s["cs"]
        p.op("pe", lambda e: e.matmul(banks[RB][:, cs:512], lhsT=Lr, rhs=spb[r][:, cs:], start=False, stop=s["last"],
                                      skip_group_check=True),
             reads=[b_spb[r], b_c], writes=[bb[RB]])

    def AV(i):
        s = steps[i]; r = i % NR; cs = s["cs"]; tq, kb = s["tq"], s["kb"]
        ob = OB[tq % 2]
        p.op("pe", lambda e: e.matmul(banks[ob][:, cs:512], lhsT=bigV[:, kb, :], rhs=ab[r][:, cs:], start=s["first"],
                                      stop=s["last"], skip_group_check=True),
             reads=[b_bigV, b_ab[r]], writes=[bb[ob]])
        if s["last"]:
            o2 = tq % 2
            p.op("act", lambda e: e.activation(out=ot[o2], in_=banks[ob], func=AF.Copy), writes=[b_ot[o2], bb[ob]])
            p.dma("sp", lambda e: e.dma_start(out=yT_d[:, tq * 512:(tq + 1) * 512], in_=ot[o2]),
                  b_ot[o2], reads=[b_ot[o2]], writes=[b_out])

    QK(0)
    if N > 1:
        QK(1)
    A(0)
    for i in range(N):
        if i + 1 < N:
            A(i + 1)
        Pa(i)
        if i + 2 < N:
            QK(i + 2)
        B(i)
        Pb(i)
        if i >= 1:
            AV(i - 1)
    AV(N - 1)


def emit_mlstm(cx, uqT_d, ukT_d, cw_d, v_d, ig_d, fg_d, gb_d, hT_d, b_in, b_out, S,
               bigQ, bigK, bigV, b_bigQ, b_bigK, b_bigV, bank_bf, bbuf_bf, qscale, tag="ml"):
    nc, p = cx.nc, cx.p
    NCH = S // 128
    assert NCH <= 128
    banks, bb = cx.banks, cx.bbufs
    onesB = cx.onesB
    TriF = cx.sb([128, 128], F32, tag + "_TriF")
    identF = cx.sb([128, 128], F32, tag + "_idF")
    identB = cx.sb([128, 128], BF16, tag + "_idB")
    b_c = p.buf(tag + "const")
    for t_ in (TriF, identF, identB):
        p.op("pool", lambda e, t_=t_: e.memset(t_, 1.0), writes=[b_c])
    p.op("pool", lambda e: e.affine_select(TriF, TriF, [[1, 128]], ALU.is_ge, 0.0, base=0, channel_multiplier=-1),
         reads=[b_c], writes=[b_c])
    p.op("pool", lambda e: e.affine_select(identF, identF, [[1, 128]], ALU.is_equal, 0.0, base=0, channel_multiplier=-1),
         reads=[b_c], writes=[b_c])
    p.op("pool", lambda e: e.affine_select(identB, identB, [[1, 128]], ALU.is_equal, 0.0, base=0, channel_multiplier=-1),
         reads=[b_c], writes=[b_c])
    cw = cx.sb([128, 10], F32, tag + "_cw"); gb = cx.sb([128, 2], F32, tag + "_gb"); ngb = cx.sb([128, 1], F32, tag + "_ngb")
    b_small = p.buf(tag + "small")
    p.dma("sp", lambda e: e.dma_start(out=cw, in_=cw_d), b_small, reads=[b_in], writes=[b_small])
    p.dma("sp", lambda e: e.dma_start(out=gb, in_=gb_d), b_small, reads=[b_in], writes=[b_small])
    p.op("dve", lambda e: e.tensor_scalar(ngb, gb[:, 1:2], -1.0, None, ALU.mult), reads=[b_small], writes=[b_small])
    Gi = cx.sb([128, 128], F32, tag + "_Gi"); Gf = cx.sb([128, 128], F32, tag + "_Gf")
    b_G = p.buf(tag + "G")
    if NCH < 128:
        p.op("pool", lambda e: e.memset(Gi, 0.0), writes=[b_G])
        p.op("pool", lambda e: e.memset(Gf, 0.0), writes=[b_G])
    p.dma("sp", lambda e: e.dma_start(out=Gi[0:NCH, :], in_=ig_d), b_G, reads=[b_in], writes=[b_G])
    p.dma("sp", lambda e: e.dma_start(out=Gf[0:NCH, :], in_=fg_d), b_G, reads=[b_in], writes=[b_G])
    p.op("act", lambda e: e.activation(out=Gf, in_=Gf, func=AF.Exp, bias=ngb[:, 0:1], scale=-1.0), reads=[b_G, b_small], writes=[b_G])
    p.op("act", lambda e: e.activation(out=Gf, in_=Gf, func=AF.Ln, bias=1.0), reads=[b_G], writes=[b_G])
    p.op("dve", lambda e: e.tensor_scalar(Gf, Gf, -1.0, None, ALU.mult), reads=[b_G], writes=[b_G])
    p.op("dve", lambda e: e.tensor_scalar(Gi, Gi, gb[:, 0:1], None, ALU.add), reads=[b_G, b_small], writes=[b_G])
    FT = cx.sb([128, 128], F32, tag + "_FT"); IT = cx.sb([128, 128], F32, tag + "_IT")
    b_FT = p.buf(tag + "FT")
    GB = 6
    p.op("pe", lambda e: e.transpose(banks[GB][:, 0:128], Gf, identF), reads=[b_G, b_c], writes=[bb[GB]])
    p.op("pe", lambda e: e.transpose(banks[GB][:, 128:256], Gi, identF), reads=[b_G, b_c], writes=[bb[GB]])
    p.op("dve", lambda e: e.tensor_copy(FT, banks[GB][:, 0:128]), writes=[b_FT, bb[GB]])
    p.op("dve", lambda e: e.tensor_copy(IT, banks[GB][:, 128:256]), writes=[b_FT, bb[GB]])
    p.op("pe", lambda e: e.matmul(banks[GB][:, 256:384], lhsT=TriF, rhs=FT, start=True, stop=True), reads=[b_FT, b_c], writes=[bb[GB]])
    p.op("pe", lambda e: e.matmul(banks[GB][:, 384:512], lhsT=cx.ones, rhs=FT, start=True, stop=True),
         reads=[b_FT, b_c, cx.b_ones], writes=[bb[GB]])
    biasS = cx.sb([128, 128], F32, tag + "_biasS"); Wcol = cx.sb([128, 128], F32, tag + "_Wcol")
    decay = cx.sb([128, 128], F32, tag + "_decay")
    b_gs = p.buf(tag + "gs")
    p.op("dve", lambda e: e.tensor_tensor(biasS, IT, banks[GB][:, 256:384], ALU.subtract), reads=[b_FT], writes=[b_gs, bb[GB]])
    p.op("dve", lambda e: e.tensor_tensor(Wcol, biasS, banks[GB][:, 384:512], ALU.add), reads=[b_gs], writes=[b_gs, bb[GB]])
    p.op("act", lambda e: e.activation(out=Wcol, in_=Wcol, func=AF.Exp), reads=[b_gs], writes=[b_gs])
    p.op("act", lambda e: e.activation(out=decay, in_=banks[GB][:, 384:512], func=AF.Exp), writes=[b_gs, bb[GB]])
    PC = min(2048, S)
    stg = [cx.sb([128, PC + 3], F32, f"{tag}_stg{i}") for i in range(2)]; b_stg = p.bufs(2, tag + "stg")
    acc = [cx.sb([128, PC], F32, f"{tag}_acc{i}") for i in range(2)]; b_acc = p.bufs(2, tag + "acc")
    k2 = 0
    for which, (u_d, big, b_big) in enumerate(((uqT_d, bigQ, b_bigQ), (ukT_d, bigK, b_bigK))):
        wo = which * 4
        for pi in range(S // PC):
            i2 = k2 % 2; k2 += 1
            t0 = pi * PC
            if pi == 0:
                p.op("pool", lambda e, i2=i2: e.memset(stg[i2][:, 0:3], 0.0), writes=[b_stg[i2]])
                p.dma("sp", lambda e, i2=i2, u_d=u_d: e.dma_start(out=stg[i2][:, 3:], in_=u_d[:, 0:PC]),
                      b_stg[i2], reads=[b_in], writes=[b_stg[i2]])
            else:
                p.dma("sp", lambda e, i2=i2, u_d=u_d, t0=t0: e.dma_start(out=stg[i2], in_=u_d[:, t0 - 3:t0 + PC]),
                      b_stg[i2], reads=[b_in], writes=[b_stg[i2]])
            p.op("dve", lambda e, i2=i2, wo=wo, which=which: e.tensor_scalar(
                acc[i2], stg[i2][:, 3:3 + PC], cw[:, wo + 3:wo + 4], cw[:, 8 + which:9 + which], ALU.mult, ALU.add),
                reads=[b_stg[i2], b_small], writes=[b_acc[i2]])
            for kk in (2, 1, 0):
                p.op("dve", lambda e, i2=i2, wo=wo, kk=kk: e.scalar_tensor_tensor(
                    acc[i2], stg[i2][:, kk:kk + PC], cw[:, wo + kk:wo + kk + 1], acc[i2], ALU.mult, ALU.add),
                    reads=[b_stg[i2], b_small, b_acc[i2]], writes=[b_acc[i2]])
            if which == 0:
                p.op("act", lambda e, i2=i2: e.activation(out=acc[i2], in_=acc[i2], func=AF.Silu),
                     reads=[b_acc[i2]], writes=[b_acc[i2]])
                p.op("pool", lambda e, i2=i2, t0=t0, big=big: e.tensor_scalar(big[:, t0:t0 + PC], acc[i2], qscale, None, ALU.mult),
                     reads=[b_acc[i2]], writes=[b_big])
            else:
                p.op("act", lambda e, i2=i2, t0=t0, big=big: e.activation(out=big[:, t0:t0 + PC], in_=acc[i2], func=AF.Silu),
                     reads=[b_acc[i2]], writes=[b_big])
    v_v = v_d.rearrange("(b p) d -> p b d", p=128)
    nvb = max(1, NCH // 32)
    for i in range(nvb):
        sl = slice(i * (NCH // nvb), (i + 1) * (NCH // nvb))
        p.dma("sp", lambda e, sl=sl: e.dma_start(out=bigV[:, sl, :], in_=v_v[:, sl, :]), b_bigV, reads=[b_in], writes=[b_bigV])
    kw = [cx.sb([128, 128], BF16, f"{tag}_kw{i}") for i in range(2)]; b_kw = p.bufs(2, tag + "kw")
    FTri = [cx.sb([128, 128], F32, f"{tag}_FTri{i}") for i in range(2)]; b_FTri = p.bufs(2, tag + "FTri")
    DT = [cx.sb([128, 128], F32, f"{tag}_DT{i}") for i in range(2)]; b_DT = p.bufs(2, tag + "DT")
    PT = [cx.sb([128, 128], BF16, f"{tag}_PT{i}") for i in range(2)]; b_PT = p.bufs(2, tag + "PT")
    Gx = [cx.sb([128, 128], F32, f"{tag}_Gx{i}") for i in range(2)]; b_Gx = p.bufs(2, tag + "Gx")
    qg = [cx.sb([128, 128], BF16, f"{tag}_qg{i}") for i in range(2)]; b_qg = p.bufs(2, tag + "qg")
    CN = cx.sb([128, 256], F32, tag + "_CN"); b_CN = p.buf(tag + "CN")
    CNb = [cx.sb([128, 256], BF16, f"{tag}_CNb{i}") for i in range(2)]; b_CNb = p.bufs(2, tag + "CNb")
    dn = [cx.sb([128, 128], F32, f"{tag}_dn{i}") for i in range(2)]; b_dn = p.bufs(2, tag + "dn")
    hst = [cx.sb([128, 512], F32, f"{tag}_hst{i}") for i in range(2)]; b_hst = p.bufs(2, tag + "hst")
    SB_, NB_, CB_ = (0, 1), (2, 3), (4, 5)
    p.op("pool", lambda e: e.memset(CN, 0.0), writes=[b_CN])

    def pre(c):
        i2 = c % 2
        csl = slice(c * 128, (c + 1) * 128)
        p.op("pe", lambda e: e.transpose(bank_bf[:, 0:128], bigK[:, csl], identB), reads=[b_bigK, b_c], writes=[bbuf_bf])
        p.op("act", lambda e: e.activation(out=kw[i2], in_=bank_bf[:, 0:128], func=AF.Identity, scale=Wcol[:, c:c + 1]),
             reads=[b_gs], writes=[b_kw[i2], bbuf_bf])
        cb = CB_[i2]
        p.op("pe", [lambda e: e.matmul(banks[cb][:, 0:128], lhsT=kw[i2], rhs=bigV[:, c, :], start=True, stop=True),
                    lambda e: e.matmul(banks[cb][:, 128:256], lhsT=kw[i2], rhs=onesB, start=True, stop=True)],
             reads=[b_kw[i2], b_bigV, cx.b_ones], writes=[bb[cb]])
        p.op("dve", lambda e: e.tensor_scalar(FTri[i2], TriF, FT[:, c:c + 1], None, ALU.mult), reads=[b_c, b_FT], writes=[b_FTri[i2]])
        sb_ = SB_[i2]
        p.op("pe", [lambda e: e.matmul(banks[sb_][:, 0:128], lhsT=bigK[:, csl], rhs=bigQ[:, csl], start=True, stop=True),
                    lambda e: e.matmul(banks[sb_][:, 128:256], lhsT=cx.ones, rhs=FTri[i2], start=True, stop=True)],
             reads=[b_bigK, b_bigQ, b_FTri[i2], cx.b_ones], writes=[bb[sb_]])
        p.op("act", lambda e: e.activation(out=DT[i2], in_=banks[sb_][:, 128:256], func=AF.Exp, bias=biasS[:, c:c + 1]),
             reads=[b_gs], writes=[b_DT[i2], bb[sb_]])
        p.op("act", lambda e: e.activation(out=Gx[i2], in_=banks[sb_][:, 128:256], func=AF.Exp),
             writes=[b_Gx[i2], bb[sb_]])
        p.op("pool", lambda e: e.affine_select(DT[i2], DT[i2], [[1, 128]], ALU.is_ge, 0.0, base=0, channel_multiplier=-1),
             reads=[b_DT[i2]], writes=[b_DT[i2]])
        p.op("dve", lambda e: e.tensor_tensor(PT[i2], banks[sb_][:, 0:128], DT[i2], ALU.mult),
             reads=[b_DT[i2]], writes=[b_PT[i2], bb[sb_]])
        p.op("pool", lambda e: e.tensor_tensor(qg[i2], bigQ[:, csl], Gx[i2], ALU.mult),
             reads=[b_bigQ, b_Gx[i2]], writes=[b_qg[i2]])

    def post(c):
        i2 = c % 2
        nb_ = NB_[i2]
        sprev = (c - 1) % 2
        fns = [lambda e: e.matmul(banks[nb_][:, 0:128], lhsT=bigV[:, c, :], rhs=PT[i2], start=True, stop=(c == 0))]
        if c > 0:
            fns.append(lambda e: e.matmul(banks[nb_][:, 0:128], lhsT=CNb[sprev][:, 0:128], rhs=qg[i2], start=False, stop=True))
        fns.append(lambda e: e.matmul(banks[nb_][:, 128:256], lhsT=onesB, rhs=PT[i2], start=True, stop=(c == 0)))
        if c > 0:
            fns.append(lambda e: e.matmul(banks[nb_][:, 128:256], lhsT=CNb[sprev][:, 128:256], rhs=qg[i2], start=False, stop=True))
        p.op("pe", fns, reads=[b_bigV, b_PT[i2], b_qg[i2], b_CNb[sprev], cx.b_ones], writes=[bb[nb_]])
        p.op("act", lambda e: e.activation(out=dn[i2], in_=banks[nb_][:, 128:256], func=AF.Abs),
             writes=[b_dn[i2], bb[nb_]])
        p.op("dve", lambda e: e.tensor_scalar(dn[i2], dn[i2], 1.0, None, ALU.max), reads=[b_dn[i2]], writes=[b_dn[i2]])
        p.op("dve", lambda e: e.reciprocal(dn[i2], dn[i2]), reads=[b_dn[i2]], writes=[b_dn[i2]])
        h2 = (c // 4) % 2
        hs = (c % 4) * 128
        p.op("dve", lambda e: e.tensor_tensor(hst[h2][:, hs:hs + 128], banks[nb_][:, 0:128], dn[i2], ALU.mult),
             reads=[b_dn[i2]], writes=[b_hst[h2], bb[nb_]])
        if c % 4 == 3 or c == NCH - 1:
            g0 = (c // 4) * 512
            n = (c % 4 + 1) * 128
            p.dma("sp", lambda e: e.dma_start(out=hT_d[:, g0:g0 + n], in_=hst[h2][:, 0:n]), b_hst[h2],
                  reads=[b_hst[h2]], writes=[b_out])
        cb = CB_[i2]
        p.op("dve", lambda e: e.scalar_tensor_tensor(CN, CN, decay[:, c:c + 1], banks[cb][:, 0:256], ALU.mult, ALU.add),
             reads=[b_CN, b_gs], writes=[b_CN, bb[cb]])
        p.op("act", lambda e: e.activation(out=CNb[i2], in_=CN, func=AF.Copy), reads=[b_CN], writes=[b_CNb[i2]])

    pre(0)
    for c in range(NCH):
        if c + 1 < NCH:
            pre(c + 1)
        post(c)


def emit_outproj(cx, yT_all, oT, xT, woutr, vecs, ngv, ynT, rT, xoT, b_in, b_yn, b_r, b_xo, D, NT, T, ln_eps, alpha, tag="op"):
    nc, p = cx.nc, cx.p
    DC = D // 128
    NS = min(512, T)
    nts = T // NS
    vec = cx.sb([128, 3 * DC], F32, tag + "_vec"); b_vec = p.buf(tag + "vec")
    gw = cx.sb([128, DC], F32, tag + "_gw")
    ng = cx.sb([128, 32], F32, tag + "_ng")
    p.dma("sp", lambda e: e.dma_start(out=vec, in_=vecs), b_vec, writes=[b_vec])
    p.dma("sp", lambda e: e.dma_start(out=ng, in_=ngv), b_vec, writes=[b_vec])
    p.op("dve", lambda e: e.tensor_scalar(gw, vec[:, 0:DC], 1.0, 1.0 / alpha, ALU.add, ALU.mult), reads=[b_vec], writes=[b_vec])
    LS = min(512, NT)
    emit_ln_phase(cx, yT_all[0:1024, :], ynT[0:1024, :], ng[:, 0:8], ng[:, 16:24], b_vec, b_in, b_yn, GC=1, NT=NT, LS=LS,
                  eps=ln_eps, ngroups=8, tag=tag + "hn1")
    emit_ln_phase(cx, yT_all[1024:2048, :], ynT[1024:2048, :], ng[:, 8:16], ng[:, 16:24], b_vec, b_in, b_yn, GC=2, NT=NT, LS=LS,
                  eps=ln_eps, ngroups=4, tag=tag + "hn2")
    ybf = cx.sb([128, 16, T], BF16, tag + "_ybf"); b_ybf = p.bufs(nts, tag + "ybf")
    yin = [cx.sb([128, NS], F32, f"{tag}_yin{i}") for i in range(3)]; b_yin = p.bufs(3, tag + "yin")
    oin = [cx.sb([128, NS], F32, f"{tag}_oin{i}") for i in range(3)]; b_oin = p.bufs(3, tag + "oin")
    stc = StageC(cx, 16, DC, T, NS, tag + "C")
    ynv = ynT.rearrange("(c p) t -> p c t", p=128)
    ov = oT.rearrange("(c p) t -> p c t", p=128)
    xTv = xT.rearrange("(c p) t -> p c t", p=128)
    rTv = rT.rearrange("(c p) t -> p c t", p=128)
    oTv = xoT.rearrange("(c p) t -> p c t", p=128)
    ky = 0
    for tt in range(NT // T):
        t0 = tt * T
        for ts in range(nts):
            gsl = slice(t0 + ts * NS, t0 + (ts + 1) * NS)
            sl = slice(ts * NS, (ts + 1) * NS)
            for c in range(16):
                i3 = ky % 3; ky += 1
                p.dma("sp", lambda e, i3=i3, c=c, gsl=gsl: e.dma_start(out=yin[i3], in_=ynv[:, c, gsl]), b_yin[i3],
                      reads=[b_yn], writes=[b_yin[i3]])
                if c < 8:
                    p.op("pool", lambda e, i3=i3, c=c, sl=sl: e.tensor_copy(ybf[:, c, sl], yin[i3]), reads=[b_yin[i3]], writes=[b_ybf[ts]])
                else:
                    p.dma("sp", lambda e, i3=i3, c=c, gsl=gsl: e.dma_start(out=oin[i3], in_=ov[:, c - 8, gsl]), b_oin[i3],
                          reads=[b_in], writes=[b_oin[i3]])
                    p.op("act", lambda e, i3=i3: e.activation(out=oin[i3], in_=oin[i3], func=AF.Sigmoid), reads=[b_oin[i3]], writes=[b_oin[i3]])
                    p.op("dve", lambda e, i3=i3, c=c, sl=sl: e.tensor_tensor(ybf[:, c, sl], yin[i3], oin[i3], ALU.mult),
                         reads=[b_yin[i3], b_oin[i3]], writes=[b_ybf[ts]])
        stc.run(ybf, b_ybf, woutr, xTv, rTv, oTv, gw, b_vec, vec[:, DC:2 * DC], vec[:, 2 * DC:3 * DC], b_vec,
                t0, ln_eps / (alpha * alpha), b_in, b_r, b_xo)


def emit_mod(cx, cl, wr, br, outd, b_out, D, NCOL, nlayers, tag="md"):
    nc, p = cx.nc, cx.p
    DC = D // 128
    cs = cx.sb([128, DC], F32, tag + "_c"); cb = cx.sb([128, DC], BF16, tag + "_cb"); b_c = p.buf(tag + "c")
    p.dma("sp", lambda e: e.dma_start(out=cs, in_=cl), b_c, writes=[b_c])
    p.op("act", lambda e: e.activation(out=cb, in_=cs, func=AF.Silu), reads=[b_c], writes=[b_c])
    wt = [cx.sb([128, DC * NCOL], BF16, f"{tag}_w{i}") for i in range(2)]; b_wt = p.bufs(2, tag + "w")
    bt = [cx.sb([1, NCOL], F32, f"{tag}_b{i}") for i in range(2)]; b_bt = p.bufs(2, tag + "b")
    ob = [cx.sb([1, NCOL], F32, f"{tag}_o{i}") for i in range(2)]; b_ob = p.bufs(2, tag + "o")
    kb = 0
    for l in range(nlayers):
        i2 = l % 2
        p.dma("pool", lambda e, i2=i2, l=l: cdma(e, wt[i2], wr[l]), b_wt[i2], writes=[b_wt[i2]])
        p.dma("sp", lambda e, i2=i2, l=l: e.dma_start(out=bt[i2], in_=br[l]), b_bt[i2], writes=[b_bt[i2]])
        n0 = 0
        while n0 < NCOL:
            n = min(512, NCOL - n0)
            bk = kb % 2; kb += 1
            fns = []
            for c in range(DC):
                fns.append(lambda e, i2=i2, c=c, n0=n0, n=n, bk=bk: e.matmul(
                    cx.banks[bk][0:1, 0:n], lhsT=cb[:, c:c + 1], rhs=wt[i2][:, c * NCOL + n0:c * NCOL + n0 + n],
                    start=(c == 0), stop=(c == DC - 1)))
            p.op("pe", fns, reads=[b_c, b_wt[i2]], writes=[cx.bbufs[bk]])
            p.op("dve", lambda e, i2=i2, n0=n0, n=n, bk=bk: e.tensor_tensor(ob[i2][:, n0:n0 + n], cx.banks[bk][0:1, 0:n], bt[i2][:, n0:n0 + n], ALU.add),
                 reads=[b_bt[i2]], writes=[b_ob[i2], cx.bbufs[bk]])
            n0 += n
        p.dma("sp", lambda e, i2=i2, l=l: e.dma_start(out=outd[l], in_=ob[i2]), b_ob[i2], reads=[b_ob[i2]], writes=[b_out])


class Cfg:
    def __init__(self, D=2048, DFF=5632, S=16384, depth=2):
        self.D, self.DFF, self.S, self.depth = D, DFF, S, depth
        self.NT = S // NCORES
        self.T = min(1024, self.NT)
        self.alpha = (2 * depth) ** 0.25
        self.eps = 1e-5
        self.NMOD = 9 * D
        assert self.NMOD % NCORES == 0
        self.NCOL = self.NMOD // NCORES


_CACHE = {}


def _new_nc():
    return bass.Bass("TRN2", target_bir_lowering=False)


def _din(nc, name, shape, dt=F32):
    return nc.dram_tensor(name, list(shape), dt, kind="ExternalInput").ap()


def _dout(nc, name, shape, dt=F32):
    return nc.dram_tensor(name, list(shape), dt, kind="ExternalOutput").ap()


def _dscr(nc, name, shape, dt=F32):
    return nc.dram_tensor(name, list(shape), dt).ap()


def build_mod(cfg):
    nc = _new_nc(); p = Prog(nc); cx = Ctx(nc, p)
    DC = cfg.D // 128
    cl = _din(nc, "cl", [128, DC]); wr = _din(nc, "wr", [cfg.depth, 128, DC * cfg.NCOL]); br = _din(nc, "br", [cfg.depth, 1, cfg.NCOL])
    outd = _dout(nc, "mod", [cfg.depth, 1, cfg.NCOL])
    b_out = p.buf("out")
    emit_mod(cx, cl, wr, br, outd, b_out, cfg.D, cfg.NCOL, cfg.depth)
    p.wait_all("sp", [b_out]); p.emit()
    return nc


def build_ffn(cfg):
    nc = _new_nc(); p = Prog(nc); cx = Ctx(nc, p)
    D, DFF, NT = cfg.D, cfg.DFF, cfg.NT
    DC, HC = D // 128, DFF // 128
    xT = _din(nc, "xT", [D, NT]); w1r = _din(nc, "w1r", [HC, 128, DC * 256]); w2r = _din(nc, "w2r", [DC, 128, HC * 128])
    vecs = _din(nc, "vecs", [128, 5 * DC]); rT = _dscr(nc, "rT", [D, NT]); xoT = _dout(nc, "xoT", [D, NT])
    b_x, b_r, b_xo = p.bufs(3, "dram")
    emit_ffn(cx, xT, w1r, w2r, vecs, rT, xoT, b_x, b_r, b_xo, D, DFF, NT, cfg.T, 0.5, cfg.alpha, cfg.eps)
    p.wait_all("sp", [b_xo]); p.emit()
    return nc


def build_inproj(cfg):
    nc = _new_nc(); p = Prog(nc); cx = Ctx(nc, p)
    D, NT = cfg.D, cfg.NT
    DC = D // 128
    xT = _din(nc, "xT", [D, NT]); vecs = _din(nc, "vecs", [128, 2 * DC])
    wfm = _din(nc, "wfm", [33, 128, DC * 128]); wtm = _din(nc, "wtm", [4, 128, DC * 512])
    qkT = _dout(nc, "qkT", [16, 128, NT], BF16); vsb = _dout(nc, "vsb", [NT, 1024], BF16)
    mqkT = _dout(nc, "mqkT", [8, 128, NT]); vml = _dout(nc, "vml", [NT, 1024], BF16)
    oT = _dout(nc, "oT", [8, 128, NT]); gTd = _dout(nc, "gT", [8, NT])
    b_x, b_out = p.bufs(2, "dram")
    emit_inproj(cx, xT, vecs, wfm, wtm, qkT, vsb, mqkT, vml, oT, gTd, b_x, b_out, D, NT, 128 ** -0.5)
    p.wait_all("sp", [b_out]); p.emit()
    return nc


def build_attn(cfg):
    nc = _new_nc(); p = Prog(nc); cx = Ctx(nc, p, bf16_bank=7)
    S = cfg.S
    NCH = S // 128
    qT = _din(nc, "qT", [128, S], BF16); kT = _din(nc, "kT", [128, S], BF16); v = _din(nc, "v", [S, 128], BF16)
    uq = _din(nc, "uq", [128, S]); uk = _din(nc, "uk", [128, S]); cw = _din(nc, "cw", [128, 10])
    vm = _din(nc, "vm", [S, 128], BF16); ig = _din(nc, "ig", [NCH, 128]); fg = _din(nc, "fg", [NCH, 128]); gb = _din(nc, "gb", [128, 2])
    yT = _dout(nc, "yT", [128, S]); hT = _dout(nc, "hT", [128, S])
    bigQ = cx.sb([128, S], BF16, "bigQ"); bigK = cx.sb([128, S], BF16, "bigK"); bigV = cx.sb([128, S // 128, 128], BF16, "bigV")
    bQ, bK, bV, b_in, b_out = p.bufs(5, "x")
    emit_mlstm(cx, uq, uk, cw, vm, ig, fg, gb, hT, b_in, b_out, S, bigQ, bigK, bigV, bQ, bK, bV, cx.banks[7], cx.bbufs[7], 128 ** -0.5)
    emit_sb_attn(cx, qT, kT, v, yT, b_in, b_out, S, bigQ, bigK, bigV, bQ, bK, bV)
    p.wait_all("sp", [b_out]); p.emit()
    return nc


def build_outproj(cfg):
    nc = _new_nc(); p = Prog(nc); cx = Ctx(nc, p)
    D, NT = cfg.D, cfg.NT
    DC = D // 128
    yT_all = _din(nc, "yT_all", [2048, NT]); oT = _din(nc, "oT", [1024, NT]); xT = _din(nc, "xT", [D, NT])
    woutr = _din(nc, "woutr", [DC, 128, 16 * 128]); vecs = _din(nc, "vecs", [128, 3 * DC]); ngv = _din(nc, "ngv", [128, 32])
    ynT = _dscr(nc, "ynT", [2048, NT]); rT = _dscr(nc, "rT", [D, NT]); xoT = _dout(nc, "xoT", [D, NT])
    b_in, b_yn, b_r, b_xo = p.bufs(4, "dram")
    emit_outproj(cx, yT_all, oT, xT, woutr, vecs, ngv, ynT, rT, xoT, b_in, b_yn, b_r, b_xo, D, NT, cfg.T, cfg.eps, cfg.alpha)
    p.wait_all("sp", [b_xo]); p.emit()
    return nc


def _get(cfg, name, builder):
    key = (name, cfg.D, cfg.DFF, cfg.S, cfg.depth)
    if key not in _CACHE:
        _CACHE[key] = builder(cfg)
    return _CACHE[key]


def _run(nc, in_maps):
    res = run_bass_kernel_spmd(nc, in_maps, core_ids=list(range(NCORES)))
    return res.results


def ffn_host_layout(w1, w2, D, DFF):
    DC, HC = D // 128, DFF // 128
    a = w1[:, :DFF].reshape(DC, 128, HC, 128)
    u = w1[:, DFF:].reshape(DC, 128, HC, 128)
    au = np.stack([a, u], axis=3)
    w1r = np.ascontiguousarray(au.transpose(2, 1, 0, 3, 4)).reshape(HC, 128, DC * 256)
    w2r = np.ascontiguousarray(w2.reshape(HC, 128, DC, 128).transpose(2, 1, 0, 3)).reshape(DC, 128, HC * 128)
    return w1r, w2r


def vec_layout(vs, D):
    DC = D // 128
    return np.ascontiguousarray(np.stack([np.asarray(v, np.float32).reshape(DC, 128).T for v in vs], axis=1)).reshape(128, len(vs) * DC)


def kernel_cfg(cfg, x, c, ada_w, ada_b, ln_g, ln_b, ffn1_w1, ffn1_w2, mix_w_in, mlstm_conv_w, mlstm_conv_b,
               mlstm_gate_b, mix_norm_g, mix_w_out, ffn2_w1, ffn2_w2):
    D, DFF, S, NT, depth = cfg.D, cfg.DFF, cfg.S, cfg.NT, cfg.depth
    DC = D // 128
    f32 = np.float32
    x = np.asarray(x, f32); c = np.asarray(c, f32)
    NCOL = cfg.NCOL
    cl = vec_layout([c[0]], D)
    aw = np.asarray(ada_w, f32).reshape(depth, DC, 128, NCORES, NCOL)
    in_maps = []
    for i in range(NCORES):
        wr = np.ascontiguousarray(aw[:, :, :, i, :].transpose(0, 2, 1, 3)).reshape(depth, 128, DC * NCOL)
        br = np.ascontiguousarray(np.asarray(ada_b, f32).reshape(depth, NCORES, 1, NCOL)[:, i])
        in_maps.append({"cl": cl, "wr": wr, "br": br})
    res = _run(_get(cfg, "mod", build_mod), in_maps)
    mod = np.concatenate([r["mod"].reshape(depth, NCOL) for r in res], axis=1).reshape(depth, 3, 3, D)
    xTs = [np.ascontiguousarray(x[0, i * NT:(i + 1) * NT, :].T) for i in range(NCORES)]
    nc_ffn = _get(cfg, "ffn", build_ffn)
    nc_ip = _get(cfg, "inproj", build_inproj)
    nc_at = _get(cfg, "attn", build_attn)
    nc_op = _get(cfg, "outproj", build_outproj)
    zeros = np.zeros(D, f32)

    def run_ffn(xTs, w1, w2, l, sub):
        w1r, w2r = ffn_host_layout(np.asarray(w1, f32), np.asarray(w2, f32), D, DFF)
        vecs = vec_layout([mod[l, sub, 1], mod[l, sub, 0], mod[l, sub, 2], ln_g[l, sub], ln_b[l, sub]], D)
        res = _run(nc_ffn, [{"xT": xTs[i], "w1r": w1r, "w2r": w2r, "vecs": vecs} for i in range(NCORES)])
        return [r["xoT"] for r in res]

    for l in range(depth):
        xTs = run_ffn(xTs, ffn1_w1[l], ffn1_w2[l], l, 0)
        W = np.asarray(mix_w_in[l], f32)
        fm_cols = [W[:, 0:2048], W[:, 3072:4096], W[:, 5120:6144]]
        gpad = np.zeros((D, 128), f32); gpad[:, 0:8] = W[:, 6144:6152]
        Wfm = np.concatenate(fm_cols + [gpad], axis=1)
        wfm = np.ascontiguousarray(Wfm.reshape(DC, 128, 33, 128).transpose(2, 1, 0, 3)).reshape(33, 128, DC * 128)
        Wtm = np.concatenate([W[:, 2048:3072], W[:, 4096:5120]], axis=1)
        wtm = np.ascontiguousarray(Wtm.reshape(DC, 128, 4, 512).transpose(2, 1, 0, 3)).reshape(4, 128, DC * 512)
        vecs = vec_layout([mod[l, 1, 1], mod[l, 1, 0]], D)
        rip = _run(nc_ip, [{"xT": xTs[i], "vecs": vecs, "wfm": wfm, "wtm": wtm} for i in range(NCORES)])
        qkT = np.concatenate([r["qkT"] for r in rip], axis=2)
        vsb = np.concatenate([r["vsb"] for r in rip], axis=0)
        mqkT = np.concatenate([r["mqkT"] for r in rip], axis=2)
        vml = np.concatenate([r["vml"] for r in rip], axis=0)
        gT = np.concatenate([r["gT"] for r in rip], axis=1)
        cwl = np.asarray(mlstm_conv_w[l], f32); cbl = np.asarray(mlstm_conv_b[l], f32); gbl = np.asarray(mlstm_gate_b[l], f32)
        in_maps = []
        for h in range(NCORES):
            hm, vh = h // 2, h % 2
            cw = np.concatenate([cwl[:, hm * 128:(hm + 1) * 128].T, cwl[:, 512 + hm * 128:512 + (hm + 1) * 128].T,
                                 cbl[hm * 128:(hm + 1) * 128, None], cbl[512 + hm * 128:512 + (hm + 1) * 128, None]], axis=1)
            gb = np.ascontiguousarray(np.broadcast_to(np.array([gbl[hm], gbl[4 + hm]], f32), (128, 2)))
            in_maps.append({
                "qT": np.ascontiguousarray(qkT[h]), "kT": np.ascontiguousarray(qkT[8 + h]),
                "v": np.ascontiguousarray(vsb[:, h * 128:(h + 1) * 128]),
                "uq": np.ascontiguousarray(mqkT[hm]), "uk": np.ascontiguousarray(mqkT[4 + hm]),
                "cw": np.ascontiguousarray(cw, dtype=f32),
                "vm": np.ascontiguousarray(vml[:, hm * 256 + vh * 128:hm * 256 + (vh + 1) * 128]),
                "ig": np.ascontiguousarray(gT[hm].reshape(S // 128, 128)), "fg": np.ascontiguousarray(gT[4 + hm].reshape(S // 128, 128)),
                "gb": gb})
        rat = _run(nc_at, in_maps)
        yall = np.concatenate([r["yT"] for r in rat] + [r["hT"] for r in rat], axis=0)
        ng = np.asarray(mix_norm_g[l], f32)
        ngv = np.concatenate([vec_layout([ng], 2048), np.zeros((128, 16), f32)], axis=1)
        woutr = np.ascontiguousarray(np.asarray(mix_w_out[l], f32).reshape(16, 128, DC, 128).transpose(2, 1, 0, 3)).reshape(DC, 128, 16 * 128)
        vecs = vec_layout([mod[l, 1, 2], ln_g[l, 1], ln_b[l, 1]], D)
        rop = _run(nc_op, [{"yT_all": np.ascontiguousarray(yall[:, i * NT:(i + 1) * NT]),
                            "oT": rip[i]["oT"].reshape(1024, NT), "xT": xTs[i], "woutr": woutr, "vecs": vecs, "ngv": ngv}
                           for i in range(NCORES)])
        xTs = [r["xoT"] for r in rop]
        xTs = run_ffn(xTs, ffn2_w1[l], ffn2_w2[l], l, 2)
    out = np.concatenate([xt.T for xt in xTs], axis=0)[None]
    return np.ascontiguousarray(out, dtype=f32)


def kernel(**inputs):
    cfg = Cfg()
    return kernel_cfg(cfg, **inputs)
```

```python
import numpy as np
import ml_dtypes
import concourse.bass as bass
import concourse.mybir as mybir
from concourse.bass_utils import run_bass_kernel_spmd

F32 = mybir.dt.float32
BF16 = mybir.dt.bfloat16
AF = mybir.ActivationFunctionType
ALU = mybir.AluOpType
NPBF = ml_dtypes.bfloat16

ENGS = ("pe", "act", "dve", "pool", "sp")
NCORES = 8


class Buf:
    __slots__ = ("name", "w", "r", "dsem", "dcnt")

    def __init__(self, name):
        self.name = name
        self.w = {}
        self.r = {}
        self.dsem = None
        self.dcnt = 0


class Prog:
    def __init__(self, nc):
        self.nc = nc
        self.ops = {e: [] for e in ENGS}
        self.sems = {}
        self.cnt = {}
        self.seen = {e: {} for e in ENGS}
        for e in ENGS:
            self.sems[e] = nc.alloc_semaphore("sem_" + e)
            self.cnt[e] = 0
        self.nbuf = 0

    def buf(self, name=None):
        self.nbuf += 1
        return Buf((name or "b") + "_" + str(self.nbuf))

    def bufs(self, n, name="b"):
        return [self.buf(f"{name}{i}") for i in range(n)]

    def _deps(self, eng, reads, writes):
        need = {}

        def add(d, same_ok):
            for k, v in d.items():
                if k == eng and same_ok:
                    continue
                if need.get(k, 0) < v:
                    need[k] = v
        for b in reads:
            add(b.w, False)
        for b in writes:
            add(b.w, True)
            add(b.r, True)
        waits = []
        seen = self.seen[eng]
        for k, v in need.items():
            if seen.get(k, 0) < v:
                seen[k] = v
                waits.append((k, v))
        return waits

    def _record(self, tok, reads, writes):
        k, v = tok
        for b in reads:
            if b.r.get(k, 0) < v:
                b.r[k] = v
        for b in writes:
            if b.w.get(k, 0) < v:
                b.w[k] = v

    def op(self, eng, fns, reads=(), writes=()):
        if callable(fns):
            fns = [fns]
        waits = self._deps(eng, reads, writes)
        self.cnt[eng] += 1
        tok = (eng, self.cnt[eng])
        self._record(tok, reads, writes)
        self.ops[eng].append((waits, fns, (eng, 1)))
        return tok

    def dma(self, q, fn, prim, reads=(), writes=()):
        waits = self._deps(q, reads, writes)
        if prim.dsem is None:
            prim.dsem = "d_" + prim.name
            self.sems[prim.dsem] = self.nc.alloc_semaphore(prim.dsem)
        prim.dcnt += 16
        tok = (prim.dsem, prim.dcnt)
        self._record(tok, reads, writes)
        self.ops[q].append((waits, [fn], (prim.dsem, 16)))
        return tok

    def wait_all(self, eng, bufs):
        waits = self._deps(eng, (), bufs)
        self.ops[eng].append((waits, [], None))

    def emit(self):
        nc, sems, ops = self.nc, self.sems, self.ops
        self.ops = {e: [] for e in ENGS}

        def run(e, lst):
            for waits, fns, inc in lst:
                for k, v in waits:
                    e.wait_ge(sems[k], v)
                n = len(fns)
                for i, fn in enumerate(fns):
                    ins = fn(e)
                    if i == n - 1 and inc is not None:
                        ins.then_inc(sems[inc[0]], inc[1])

        with nc.Block() as block:
            @block.tensor
            def _(e):
                run(e, ops["pe"])

            @block.scalar
            def _(e):
                run(e, ops["act"])

            @block.vector
            def _(e):
                run(e, ops["dve"])

            @block.gpsimd
            def _(e):
                run(e, ops["pool"])

            @block.sync
            def _(e):
                run(e, ops["sp"])


def cdma(e, out, in_):
    n = out.shape[-1]
    if n > 2048:
        for b in (2048, 1024, 512, 256, 128):
            if n % b == 0:
                break
        out = out.rearrange("p (a b) -> p a b", b=b)
        in_ = in_.rearrange("p (a b) -> p a b", b=b)
    return e.dma_start(out=out, in_=in_)


class Ctx:
    def __init__(self, nc, p, bf16_bank=None):
        self.nc, self.p = nc, p
        self.banks = [(nc.alloc_psum_tensor(f"bank{i}", [128, 1024], BF16).ap() if i == bf16_bank else
                       nc.alloc_psum_tensor(f"bank{i}", [128, 512], F32).ap()) for i in range(8)]
        self.bbufs = p.bufs(8, "bank")
        self.ones = nc.alloc_sbuf_tensor("ones_f32", [128, 128], F32).ap()
        self.onesB = nc.alloc_sbuf_tensor("ones_bf16", [128, 128], BF16).ap()
        self.b_ones = p.buf("ones")
        p.op("pool", lambda e: e.memset(self.ones, 1.0), writes=[self.b_ones])
        p.op("pool", lambda e: e.memset(self.onesB, 1.0), writes=[self.b_ones])
        self.n_sb = 0

    def sb(self, shape, dt, name=None):
        self.n_sb += 1
        return self.nc.alloc_sbuf_tensor((name or "sb") + f"_{self.n_sb}", shape, dt).ap()


def emit_ln_phase(cx, rT, outT, vec_g, vec_b, b_vec, b_r_dram, b_out_dram, GC, NT, LS, eps, ngroups=1,
                  banks=(6, 7), tag="ln"):
    nc, p = cx.nc, cx.p
    nsub = NT // LS
    Dg = GC * 128
    rl = [cx.sb([128, GC, LS], F32, f"{tag}_rl{i}") for i in range(2)]
    b_rl = p.bufs(2, tag + "rl")
    sq = [cx.sb([128, LS], F32, f"{tag}_sq{i}") for i in range(2)]
    b_sq = p.bufs(2, tag + "sq")
    mean = cx.sb([128, LS], F32, tag + "_mean"); b_mean = p.buf(tag + "mean")
    m2 = cx.sb([128, LS], F32, tag + "_m2"); b_m2 = p.buf(tag + "m2")
    var = cx.sb([128, LS], F32, tag + "_var"); b_var = p.buf(tag + "var")
    rstd = cx.sb([128, LS], F32, tag + "_rstd"); b_rstd = p.buf(tag + "rstd")
    nmr = cx.sb([128, LS], F32, tag + "_nmr"); b_nmr = p.buf(tag + "nmr")
    t1 = [cx.sb([128, LS], F32, f"{tag}_t1{i}") for i in range(2)]; b_t1 = p.bufs(2, tag + "t1")
    t2 = [cx.sb([128, LS], F32, f"{tag}_t2{i}") for i in range(2)]; b_t2 = p.bufs(2, tag + "t2")
    ot = [cx.sb([128, GC, LS], F32, f"{tag}_ot{i}") for i in range(2)]; b_ot = p.bufs(2, tag + "ot")
    bs, bq = banks
    k = 0
    it = 0
    for g in range(ngroups):
        rTg = rT[g * Dg:(g + 1) * Dg, :].rearrange("(c p) t -> p c t", p=128)
        oTg = outT[g * Dg:(g + 1) * Dg, :].rearrange("(c p) t -> p c t", p=128)
        for s in range(nsub):
            i2 = it % 2
            it += 1
            tsl = slice(s * LS, (s + 1) * LS)
            p.dma("sp", lambda e, i2=i2, tsl=tsl, rTg=rTg: e.dma_start(out=rl[i2], in_=rTg[:, :, tsl]),
                  b_rl[i2], reads=[b_r_dram], writes=[b_rl[i2]])
            for c in range(GC):
                q2 = k % 2
                k += 1
                p.op("act", lambda e, i2=i2, c=c, q2=q2: e.activation(out=sq[q2], in_=rl[i2][:, c, :], func=AF.Square),
                     reads=[b_rl[i2]], writes=[b_sq[q2]])
                p.op("pe", [lambda e, i2=i2, c=c: e.matmul(cx.banks[bs][:, 0:LS], lhsT=cx.ones, rhs=rl[i2][:, c, :],
                                                           start=(c == 0), stop=(c == GC - 1)),
                            lambda e, q2=q2, c=c: e.matmul(cx.banks[bq][:, 0:LS], lhsT=cx.ones, rhs=sq[q2],
                                                           start=(c == 0), stop=(c == GC - 1))],
                     reads=[b_rl[i2], b_sq[q2], cx.b_ones], writes=[cx.bbufs[bs], cx.bbufs[bq]])
            p.op("dve", lambda e: e.tensor_scalar(mean, cx.banks[bs][:, 0:LS], 1.0 / Dg, None, ALU.mult),
                 writes=[b_mean, cx.bbufs[bs]])
            p.op("dve", lambda e: e.tensor_tensor(m2, mean, mean, ALU.mult), reads=[b_mean], writes=[b_m2])
            p.op("dve", lambda e: e.tensor_scalar(var, cx.banks[bq][:, 0:LS], 1.0 / Dg, eps, ALU.mult, ALU.add),
                 writes=[b_var, cx.bbufs[bq]])
            p.op("dve", lambda e: e.tensor_tensor(m2, var, m2, ALU.subtract), reads=[b_var, b_m2], writes=[b_m2])
            p.op("act", lambda e: e.activation(out=var, in_=m2, func=AF.Sqrt), reads=[b_m2], writes=[b_var])
            p.op("dve", lambda e: e.reciprocal(rstd, var), reads=[b_var], writes=[b_rstd])
            p.op("dve", lambda e: e.scalar_tensor_tensor(nmr, mean, -1.0, rstd, ALU.mult, ALU.mult),
                 reads=[b_mean, b_rstd], writes=[b_nmr])
            for c in range(GC):
                q2 = k % 2
                k += 1
                p.op("dve", lambda e, i2=i2, c=c, q2=q2: e.tensor_tensor(t1[q2], rl[i2][:, c, :], rstd, ALU.mult),
                     reads=[b_rl[i2], b_rstd], writes=[b_t1[q2]])
                p.op("pool", lambda e, q2=q2: e.tensor_tensor(t2[q2], t1[q2], nmr, ALU.add),
                     reads=[b_t1[q2], b_nmr], writes=[b_t2[q2]])
                vi = g * GC + c
                p.op("act", lambda e, i2=i2, c=c, q2=q2, vi=vi: e.activation(
                    out=ot[i2][:, c, :], in_=t2[q2], func=AF.Identity,
                    bias=vec_b[:, vi:vi + 1], scale=vec_g[:, vi:vi + 1]),
                    reads=[b_t2[q2], b_vec], writes=[b_ot[i2]])
            p.dma("sp", lambda e, i2=i2, tsl=tsl, oTg=oTg: e.dma_start(out=oTg[:, :, tsl], in_=ot[i2]),
                  b_ot[i2], reads=[b_ot[i2]], writes=[b_out_dram])


class StageC:
    def __init__(self, cx, KC, DC, T, NS, tag, w2ext=None, NW2=3):
        p = cx.p
        self.cx, self.KC, self.DC, self.T, self.NS, self.tag = cx, KC, DC, T, NS, tag
        nts = T // NS
        self.nts = nts
        if w2ext is None:
            self.w2t = [cx.sb([128, KC * 128], BF16, f"{tag}_w2t{i}") for i in range(NW2)]; self.b_w2t = p.bufs(NW2, tag + "w2t")
            self.w2wr = [[b] for b in self.b_w2t]
        else:
            self.w2t, self.b_w2t, self.w2wr = w2ext
        self.NW2 = len(self.w2t)
        self.xres = [cx.sb([128, NS], F32, f"{tag}_xres{i}") for i in range(2)]; self.b_xres = p.bufs(2, tag + "xres")
        self.rt = [cx.sb([128, NS], F32, f"{tag}_rt{i}") for i in range(2)]; self.b_rt = p.bufs(2, tag + "rt")
        self.rbf = [cx.sb([128, NS], BF16, f"{tag}_rbf{i}") for i in range(2)]; self.b_rbf = p.bufs(2, tag + "rbf")
        self.sqbf = [cx.sb([128, NS], BF16, f"{tag}_sqbf{i}") for i in range(2)]; self.b_sqbf = p.bufs(2, tag + "sqbf")
        self.asum = cx.sb([128, T], F32, tag + "_asum"); self.b_asum = p.bufs(nts, tag + "asum")
        self.asq = cx.sb([128, T], F32, tag + "_asq"); self.b_asq = p.bufs(nts, tag + "asq")
        self.nmr = cx.sb([128, T], F32, tag + "_nmr"); self.b_nmr = p.bufs(nts, tag + "nmr")
        self.nl = [cx.sb([128, NS], F32, f"{tag}_nl{i}") for i in range(3)]; self.b_nl = p.bufs(3, tag + "nl")
        self.kw2 = self.ky = self.kr = self.kn = 0

    def run(self, gT, b_gT, w2r, xTv, rTv, oTv, gw, b_gw, vg, vb, b_vgb, t0, eps, b_x, b_r, b_xo,
            bank_y=(4, 5), bank_s=(6, 7)):
        cx, p = self.cx, self.cx.p
        KC, DC, NS, nts = self.KC, self.DC, self.NS, self.nts
        banks, bb = cx.banks, cx.bbufs
        Dtot = DC * 128
        pend = None

        def stats(c, ts, r2):
            sl = slice(ts * NS, (ts + 1) * NS)
            bs, bq = bank_s
            p.op("pe", [lambda e: e.matmul(banks[bs][:, 0:NS], lhsT=cx.onesB, rhs=self.rbf[r2], start=True, stop=True),
                        lambda e: e.matmul(banks[bq][:, 0:NS], lhsT=cx.onesB, rhs=self.sqbf[r2], start=True, stop=True)],
                 reads=[self.b_rbf[r2], self.b_sqbf[r2], cx.b_ones], writes=[bb[bs], bb[bq]])
            if c == 0:
                p.op("dve", lambda e: e.tensor_copy(self.asum[:, sl], banks[bs][:, 0:NS]), writes=[self.b_asum[ts], bb[bs]])
                p.op("dve", lambda e: e.tensor_copy(self.asq[:, sl], banks[bq][:, 0:NS]), writes=[self.b_asq[ts], bb[bq]])
            else:
                p.op("dve", lambda e: e.tensor_tensor(self.asum[:, sl], self.asum[:, sl], banks[bs][:, 0:NS], ALU.add),
                     reads=[self.b_asum[ts]], writes=[self.b_asum[ts], bb[bs]])
                p.op("dve", lambda e: e.tensor_tensor(self.asq[:, sl], self.asq[:, sl], banks[bq][:, 0:NS], ALU.add),
                     reads=[self.b_asq[ts]], writes=[self.b_asq[ts], bb[bq]])

        for c in range(DC):
            i2 = self.kw2 % self.NW2; self.kw2 += 1
            p.dma("pool", lambda e, i2=i2, c=c: cdma(e, self.w2t[i2], w2r[c]), self.b_w2t[i2], writes=self.w2wr[i2])
            for ts in range(nts):
                by = bank_y[self.ky % 2]; self.ky += 1
                sl = slice(ts * NS, (ts + 1) * NS)
                gsl = slice(t0 + ts * NS, t0 + (ts + 1) * NS)
                fns = []
                for h in range(KC):
                    fns.append(lambda e, i2=i2, h=h, sl=sl, by=by: e.matmul(
                        banks[by][:, 0:NS], lhsT=self.w2t[i2][:, h * 128:(h + 1) * 128], rhs=gT[:, h, sl],
                        start=(h == 0), stop=(h == KC - 1)))
                p.op("pe", fns, reads=[self.b_w2t[i2], b_gT[ts]], writes=[bb[by]])
                if pend is not None:
                    stats(*pend)
                r2 = self.kr % 2; self.kr += 1
                p.dma("sp", lambda e, r2=r2, c=c, gsl=gsl: e.dma_start(out=self.xres[r2], in_=xTv[:, c, gsl]),
                      self.b_xres[r2], reads=[b_x], writes=[self.b_xres[r2]])
                p.op("dve", lambda e, r2=r2, by=by, c=c: e.scalar_tensor_tensor(
                    self.rt[r2], banks[by][:, 0:NS], gw[:, c:c + 1], self.xres[r2], ALU.mult, ALU.add),
                    reads=[self.b_xres[r2], b_gw], writes=[self.b_rt[r2], bb[by]])
                p.op("act", lambda e, r2=r2: e.activation(out=self.rbf[r2], in_=self.rt[r2], func=AF.Copy), reads=[self.b_rt[r2]], writes=[self.b_rbf[r2]])
                p.op("act", lambda e, r2=r2: e.activation(out=self.sqbf[r2], in_=self.rt[r2], func=AF.Square),
                     reads=[self.b_rt[r2]], writes=[self.b_sqbf[r2]])
                p.dma("sp", lambda e, r2=r2, c=c, gsl=gsl: e.dma_start(out=rTv[:, c, gsl], in_=self.rt[r2]),
                      self.b_rt[r2], reads=[self.b_rt[r2]], writes=[b_r])
                pend = (c, ts, r2)
        stats(*pend)
        for ts in range(nts):
            sl = slice(ts * NS, (ts + 1) * NS)
            A, Q, M = self.asum[:, sl], self.asq[:, sl], self.nmr[:, sl]
            rd = [self.b_asum[ts], self.b_asq[ts], self.b_nmr[ts]]
            p.op("dve", lambda e, A=A: e.tensor_scalar(A, A, 1.0 / Dtot, None, ALU.mult), reads=rd, writes=rd)
            p.op("dve", lambda e, A=A, M=M: e.tensor_tensor(M, A, A, ALU.mult), reads=rd, writes=rd)
            p.op("dve", lambda e, Q=Q: e.tensor_scalar(Q, Q, 1.0 / Dtot, eps, ALU.mult, ALU.add), reads=rd, writes=rd)
            p.op("dve", lambda e, Q=Q, M=M: e.tensor_tensor(Q, Q, M, ALU.subtract), reads=rd, writes=rd)
            p.op("act", lambda e, Q=Q: e.activation(out=Q, in_=Q, func=AF.Sqrt), reads=rd, writes=rd)
            p.op("dve", lambda e, Q=Q: e.reciprocal(Q, Q), reads=rd, writes=rd)
            p.op("dve", lambda e, A=A, Q=Q, M=M: e.scalar_tensor_tensor(M, A, -1.0, Q, ALU.mult, ALU.mult), reads=rd, writes=rd)
        items = []

        def norm_item(ts, c):
            sl = slice(ts * NS, (ts + 1) * NS)
            gsl = slice(t0 + ts * NS, t0 + (ts + 1) * NS)
            rd = [self.b_asum[ts], self.b_asq[ts], self.b_nmr[ts]]
            n3 = self.kn % 3; self.kn += 1
            nl = self.nl[n3]; bnl = self.b_nl[n3]
            p.dma("sp", lambda e: e.dma_start(out=nl, in_=rTv[:, c, gsl]), bnl, reads=[b_r], writes=[bnl])
            p.op("dve", lambda e: e.tensor_tensor(nl, nl, self.asq[:, sl], ALU.mult), reads=[bnl] + rd, writes=[bnl])
            p.op("dve", lambda e: e.tensor_tensor(nl, nl, self.nmr[:, sl], ALU.add), reads=[bnl] + rd, writes=[bnl])
            p.op("act", lambda e: e.activation(out=nl, in_=nl, func=AF.Identity, bias=vb[:, c:c + 1], scale=vg[:, c:c + 1]),
                 reads=[bnl, b_vgb], writes=[bnl])
            p.dma("act", lambda e: e.dma_start(out=oTv[:, c, gsl], in_=nl), bnl, reads=[bnl], writes=[b_xo])

        for ts in range(nts):
            for c in range(DC):
                items.append(lambda ts=ts, c=c: norm_item(ts, c))
        return items


def emit_modulate(cx, xTv, xbf, b_xbf, s1, sh, b_vec, xin, b_xin, kx, t0, nts, NS, DC, b_x):
    p = cx.p
    for ts in range(nts):
        for c in range(DC):
            i3 = kx[0] % 3; kx[0] += 1
            sl = slice(t0 + ts * NS, t0 + (ts + 1) * NS)
            p.dma("sp", lambda e, i3=i3, c=c, sl=sl: e.dma_start(out=xin[i3], in_=xTv[:, c, sl]),
                  b_xin[i3], reads=[b_x], writes=[b_xin[i3]])
            p.op("act", lambda e, i3=i3, c=c, ts=ts: e.activation(
                out=xbf[:, c, ts * NS:(ts + 1) * NS], in_=xin[i3], func=AF.Identity,
                bias=sh[:, c:c + 1], scale=s1[:, c:c + 1]),
                reads=[b_xin[i3], b_vec], writes=[b_xbf[ts]])


def emit_ffn(cx, xT, w1r, w2r, vecs, rT, xoT, b_x, b_r, b_xo, D, DFF, NT, T, resw, alpha, ln_eps, tag="f"):
    nc, p = cx.nc, cx.p
    DC, HC = D // 128, DFF // 128
    NS = min(512, T)
    assert T % NS == 0 and NT % T == 0
    nts = T // NS
    vec = cx.sb([128, 5 * DC], F32, tag + "_vec"); b_vec = p.buf(tag + "vec")
    der = cx.sb([128, 2 * DC], F32, tag + "_der")
    p.dma("sp", lambda e: e.dma_start(out=vec, in_=vecs), b_vec, writes=[b_vec])
    p.op("dve", lambda e: e.tensor_scalar(der[:, 0:DC], vec[:, 0:DC], 1.0, None, ALU.add), reads=[b_vec], writes=[b_vec])
    p.op("dve", lambda e: e.tensor_scalar(der[:, DC:2 * DC], vec[:, 2 * DC:3 * DC], 1.0, resw / alpha, ALU.add, ALU.mult),
         reads=[b_vec], writes=[b_vec])
    xin = [cx.sb([128, NS], F32, f"{tag}_xin{i}") for i in range(3)]; b_xin = p.bufs(3, tag + "xin")
    xbf = cx.sb([128, DC, T], BF16, tag + "_xbf"); b_xbf = p.bufs(nts, tag + "xbf")
    NW1, NW2 = 5, 3
    s1, s2 = DC * 256, HC * 128
    wreg = cx.sb([128, max(NW1 * s1, NW2 * s2)], BF16, tag + "_wreg")
    w1t = [wreg[:, i * s1:(i + 1) * s1] for i in range(NW1)]; b_w1t = p.bufs(NW1, tag + "w1t")
    w2v = [wreg[:, k * s2:(k + 1) * s2] for k in range(NW2)]; b_w2v = p.bufs(NW2, tag + "w2t")
    ov = lambda i, k: i * s1 < (k + 1) * s2 and k * s2 < (i + 1) * s1
    w1wr = [[b_w1t[i]] + [b_w2v[k] for k in range(NW2) if ov(i, k)] for i in range(NW1)]
    w2wr = [[b_w2v[k]] + [b_w1t[i] for i in range(NW1) if ov(i, k)] for k in range(NW2)]
    gT = cx.sb([128, HC, T], BF16, tag + "_gT"); b_gT = p.bufs(nts, tag + "gT")
    sa = [cx.sb([128, NS], F32, f"{tag}_sa{i}") for i in range(2)]; b_sa = p.bufs(2, tag + "sa")
    stc = StageC(cx, HC, DC, T, NS, tag + "C", w2ext=(w2v, b_w2v, w2wr))
    xTv = xT.rearrange("(c p) t -> p c t", p=128)
    rTv = rT.rearrange("(c p) t -> p c t", p=128)
    oTv = xoT.rearrange("(c p) t -> p c t", p=128)
    kx = [0]
    ks = kw1 = kb = 0
    bank_a, bank_u = (0, 1), (2, 3)
    ntt = NT // T
    emit_modulate(cx, xTv, xbf, b_xbf, der[:, 0:DC], vec[:, DC:2 * DC], b_vec, xin, b_xin, kx, 0, nts, NS, DC, b_x)
    pending = []
    for tt in range(ntt):
        t0 = tt * T
        for j in range(HC):
            for _ in range(2):
                if pending:
                    pending.pop(0)()
            i2 = kw1 % NW1; kw1 += 1
            p.dma("pool", lambda e, i2=i2, j=j: cdma(e, w1t[i2], w1r[j]), b_w1t[i2], writes=w1wr[i2])
            for ts in range(nts):
                ba, bu = bank_a[kb % 2], bank_u[kb % 2]; kb += 1
                sl = slice(ts * NS, (ts + 1) * NS)
                fns = []
                for c in range(DC):
                    fns.append(lambda e, i2=i2, c=c, sl=sl, ba=ba: e.matmul(
                        cx.banks[ba][:, 0:NS], lhsT=w1t[i2][:, c * 256:c * 256 + 128], rhs=xbf[:, c, sl],
                        start=(c == 0), stop=(c == DC - 1)))
                p.op("pe", fns, reads=[b_w1t[i2], b_xbf[ts]], writes=[cx.bbufs[ba]])
                fns = []
                for c in range(DC):
                    fns.append(lambda e, i2=i2, c=c, sl=sl, bu=bu: e.matmul(
                        cx.banks[bu][:, 0:NS], lhsT=w1t[i2][:, c * 256 + 128:c * 256 + 256], rhs=xbf[:, c, sl],
                        start=(c == 0), stop=(c == DC - 1)))
                p.op("pe", fns, reads=[b_w1t[i2], b_xbf[ts]], writes=[cx.bbufs[bu]])
                s2 = ks % 2; ks += 1
                p.op("act", lambda e, s2=s2, ba=ba: e.activation(out=sa[s2], in_=cx.banks[ba][:, 0:NS], func=AF.Silu),
                     writes=[b_sa[s2], cx.bbufs[ba]])
                p.op("dve", lambda e, s2=s2, bu=bu, j=j, sl=sl: e.tensor_tensor(gT[:, j, sl], sa[s2], cx.banks[bu][:, 0:NS], ALU.mult),
                     reads=[b_sa[s2]], writes=[b_gT[ts], cx.bbufs[bu]])
        if tt + 1 < ntt:
            emit_modulate(cx, xTv, xbf, b_xbf, der[:, 0:DC], vec[:, DC:2 * DC], b_vec, xin, b_xin, kx, t0 + T, nts, NS, DC, b_x)
        while pending:
            pending.pop(0)()
        pending = stc.run(gT, b_gT, w2r, xTv, rTv, oTv, der[:, DC:2 * DC], b_vec, vec[:, 3 * DC:4 * DC], vec[:, 4 * DC:5 * DC], b_vec,
                          t0, ln_eps / (alpha * alpha), b_x, b_r, b_xo)
    while pending:
        pending.pop(0)()


def emit_inproj(cx, xT, vecs, wfm, wtm, qkT, vsb, mqkT, vml, oT, gTd, b_x, b_out, D, NT, qscale, tag="ip"):
    nc, p = cx.nc, cx.p
    DC = D // 128
    NS = min(512, NT)
    nts = NT // NS
    vec = cx.sb([128, 2 * DC], F32, tag + "_vec"); b_vec = p.buf(tag + "vec")
    s1 = cx.sb([128, DC], F32, tag + "_s1")
    p.dma("sp", lambda e: e.dma_start(out=vec, in_=vecs), b_vec, writes=[b_vec])
    p.op("dve", lambda e: e.tensor_scalar(s1, vec[:, 0:DC], 1.0, None, ALU.add), reads=[b_vec], writes=[b_vec])
    xin = [cx.sb([128, NS], F32, f"{tag}_xin{i}") for i in range(3)]; b_xin = p.bufs(3, tag + "xin")
    xbf = cx.sb([128, DC, NT], BF16, tag + "_xbf"); b_xbf = p.bufs(nts, tag + "xbf")
    xTv = xT.rearrange("(c p) t -> p c t", p=128)
    emit_modulate(cx, xTv, xbf, b_xbf, s1, vec[:, DC:2 * DC], b_vec, xin, b_xin, [0], 0, nts, NS, DC, b_x)
    wf = [cx.sb([128, DC * 128], BF16, f"{tag}_wf{i}") for i in range(3)]; b_wf = p.bufs(3, tag + "wf")
    obf = [cx.sb([128, NS], BF16, f"{tag}_obf{i}") for i in range(3)]; b_obf = p.bufs(3, tag + "obf")
    of32 = [cx.sb([128, NS], F32, f"{tag}_of{i}") for i in range(3)]; b_of = p.bufs(3, tag + "of")
    kbk = ko = 0
    for j in range(33):
        i3 = j % 3
        p.dma("pool", lambda e, i3=i3, j=j: cdma(e, wf[i3], wfm[j]), b_wf[i3], writes=[b_wf[i3]])
        for ts in range(nts):
            bk = kbk % 4; kbk += 1
            sl = slice(ts * NS, (ts + 1) * NS)
            fns = []
            for c in range(DC):
                fns.append(lambda e, i3=i3, c=c, sl=sl, bk=bk: e.matmul(
                    cx.banks[bk][:, 0:NS], lhsT=wf[i3][:, c * 128:(c + 1) * 128], rhs=xbf[:, c, sl],
                    start=(c == 0), stop=(c == DC - 1)))
            p.op("pe", fns, reads=[b_wf[i3], b_xbf[ts]], writes=[cx.bbufs[bk]])
            o3 = ko % 3; ko += 1
            eng = "act" if (ko % 2 == 0) else "dve"
            src = cx.banks[bk][:, 0:NS]
            if j < 16:
                dst, bd, ddst = obf[o3], b_obf[o3], qkT[j][:, sl]
                sc = qscale if j < 8 else 1.0
            elif j < 24:
                dst, bd, ddst, sc = of32[o3], b_of[o3], mqkT[j - 16][:, sl], 1.0
            elif j < 32:
                dst, bd, ddst, sc = of32[o3], b_of[o3], oT[j - 24][:, sl], 1.0
            else:
                dst, bd, ddst, sc = of32[o3][0:8, :], b_of[o3], gTd[:, sl], 1.0
                src = cx.banks[bk][0:8, 0:NS]
            if eng == "act":
                p.op("act", lambda e, dst=dst, src=src, sc=sc: e.activation(out=dst, in_=src, func=AF.Copy, scale=sc),
                     writes=[bd, cx.bbufs[bk]])
            else:
                p.op("dve", lambda e, dst=dst, src=src, sc=sc: e.tensor_scalar(dst, src, sc, None, ALU.mult),
                     writes=[bd, cx.bbufs[bk]])
            p.dma("sp", lambda e, dst=dst, ddst=ddst: e.dma_start(out=ddst, in_=dst), bd, reads=[bd], writes=[b_out])
    wt = [cx.sb([128, DC * 512], BF16, f"{tag}_wt{i}") for i in range(2)]; b_wt = p.bufs(2, tag + "wt")
    otm = [cx.sb([128, 512], BF16, f"{tag}_otm{i}") for i in range(3)]; b_otm = p.bufs(3, tag + "otm")
    for s in range(4):
        i2 = s % 2
        p.dma("pool", lambda e, i2=i2, s=s: cdma(e, wt[i2], wtm[s]), b_wt[i2], writes=[b_wt[i2]])
        dd = vsb if s < 2 else vml
        for tc in range(NT // 128):
            bk = 4 + kbk % 4; kbk += 1
            fns = []
            for c in range(DC):
                fns.append(lambda e, i2=i2, c=c, tc=tc, bk=bk: e.matmul(
                    cx.banks[bk], lhsT=xbf[:, c, tc * 128:(tc + 1) * 128], rhs=wt[i2][:, c * 512:(c + 1) * 512],
                    start=(c == 0), stop=(c == DC - 1)))
            p.op("pe", fns, reads=[b_wt[i2]] + b_xbf, writes=[cx.bbufs[bk]])
            o3 = ko % 3; ko += 1
            eng = "act" if (ko % 2 == 0) else "dve"
            if eng == "act":
                p.op("act", lambda e, o3=o3, bk=bk: e.activation(out=otm[o3], in_=cx.banks[bk], func=AF.Copy),
                     writes=[b_otm[o3], cx.bbufs[bk]])
            else:
                p.op("dve", lambda e, o3=o3, bk=bk: e.tensor_copy(otm[o3], cx.banks[bk]), writes=[b_otm[o3], cx.bbufs[bk]])
            p.dma("sp", lambda e, o3=o3, tc=tc, s=s, dd=dd: e.dma_start(
                out=dd[tc * 128:(tc + 1) * 128, (s % 2) * 512:(s % 2 + 1) * 512], in_=otm[o3]),
                b_otm[o3], reads=[b_otm[o3]], writes=[b_out])


def emit_sb_attn(cx, qT_d, kT_d, v_d, yT_d, b_in, b_out, S, bigQ, bigK, bigV, b_bigQ, b_bigK, b_bigV, tag="sb"):
    nc, p = cx.nc, cx.p
    NB, NQ = S // 128, S // 512
    Ui = cx.sb([128, 128], BF16, tag + "_Ui"); Ls = cx.sb([128, 128], BF16, tag + "_Ls")
    b_c = p.buf(tag + "const")
    p.op("pool", lambda e: e.memset(Ui, 1.0), writes=[b_c])
    p.op("pool", lambda e: e.memset(Ls, 1.0), writes=[b_c])
    p.op("pool", lambda e: e.affine_select(Ui, Ui, [[-1, 128]], ALU.is_ge, 0.0, base=0, channel_multiplier=1),
         reads=[b_c], writes=[b_c])
    p.op("pool", lambda e: e.affine_select(Ls, Ls, [[1, 128]], ALU.is_gt, 0.0, base=0, channel_multiplier=-1),
         reads=[b_c], writes=[b_c])
    npc = max(1, S // 4096)
    pc = S // npc
    for i in range(npc):
        sl = slice(i * pc, (i + 1) * pc)
        p.dma("sp", lambda e, sl=sl: e.dma_start(out=bigQ[:, sl], in_=qT_d[:, sl]), b_bigQ, reads=[b_in], writes=[b_bigQ])
        p.dma("sp", lambda e, sl=sl: e.dma_start(out=bigK[:, sl], in_=kT_d[:, sl]), b_bigK, reads=[b_in], writes=[b_bigK])
    v_v = v_d.rearrange("(b p) d -> p b d", p=128)
    nvb = max(1, NB // 32)
    for i in range(nvb):
        sl = slice(i * (NB // nvb), (i + 1) * (NB // nvb))
        p.dma("sp", lambda e, sl=sl: e.dma_start(out=bigV[:, sl, :], in_=v_v[:, sl, :]), b_bigV, reads=[b_in], writes=[b_bigV])
    NR = 4
    e32 = [cx.sb([128, 512], F32, f"{tag}_e{i}") for i in range(NR)]; b_e = p.bufs(NR, tag + "e")
    spb = [cx.sb([128, 512], BF16, f"{tag}_spb{i}") for i in range(NR)]; b_spb = p.bufs(NR, tag + "spb")
    t32 = [cx.sb([128, 512], F32, f"{tag}_t{i}") for i in range(NR)]; b_t = p.bufs(NR, tag + "t")
    ab = [cx.sb([128, 512], BF16, f"{tag}_ab{i}") for i in range(NR)]; b_ab = p.bufs(NR, tag + "ab")
    ot = [cx.sb([128, 512], F32, f"{tag}_ot{i}") for i in range(2)]; b_ot = p.bufs(2, tag + "ot")
    ZB = (0, 1, 2)
    RB = 3
    OB = (4, 5)
    banks, bb = cx.banks, cx.bbufs
    steps = []
    for tq in range(NQ):
        kbs = list(range(4 * tq + 3, -1, -1))
        for n, kb in enumerate(kbs):
            o = kb - 4 * tq
            steps.append(dict(tq=tq, kb=kb, first=(n == 0), last=(n == len(kbs) - 1), diag=(o >= 0), cs=max(o, 0) * 128))
    N = len(steps)

    def QK(i):
        s = steps[i]; z = ZB[i % 3]; cs = s["cs"]; tq, kb = s["tq"], s["kb"]
        p.op("pe", lambda e: e.matmul(banks[z][:, cs:512], lhsT=bigK[:, kb * 128:(kb + 1) * 128],
                                      rhs=bigQ[:, tq * 512 + cs:(tq + 1) * 512], start=True, stop=True),
             reads=[b_bigK, b_bigQ], writes=[bb[z]])

    def A(i):
        s = steps[i]; z = ZB[i % 3]; r = i % NR; cs = s["cs"]
        p.op("act", lambda e: e.activation(out=e32[r][:, cs:], in_=banks[z][:, cs:512], func=AF.Exp),
             writes=[b_e[r], bb[z]])
        p.op("act", lambda e: e.activation(out=spb[r][:, cs:], in_=e32[r][:, cs:], func=AF.Ln, bias=1.0),
             reads=[b_e[r]], writes=[b_spb[r]])
        if s["diag"]:
            p.op("pool", lambda e: e.affine_select(spb[r][:, cs:], spb[r][:, cs:], [[1, 512 - cs]], ALU.is_gt, 0.0,
                                                   base=0, channel_multiplier=-1),
                 reads=[b_spb[r]], writes=[b_spb[r]])

    def Pa(i):
        s = steps[i]; r = i % NR; cs = s["cs"]
        p.op("pe", lambda e: e.matmul(banks[RB][:, cs:512], lhsT=Ui, rhs=spb[r][:, cs:], start=s["first"], stop=False,
                                      skip_group_check=True),
             reads=[b_spb[r], b_c], writes=[bb[RB]])

    def B(i):
        s = steps[i]; r = i % NR; cs = s["cs"]
        p.op("act", lambda e: e.activation(out=t32[r][:, cs:], in_=banks[RB][:, cs:512], func=AF.Exp, scale=-1.0),
             writes=[b_t[r], bb[RB]])
        p.op("dve", lambda e: e.tensor_tensor(ab[r][:, cs:], e32[r][:, cs:], t32[r][:, cs:], ALU.mult),
             reads=[b_e[r], b_t[r]], writes=[b_ab[r]])
        if s["diag"]:
            p.op("pool", lambda e: e.affine_select(ab[r][:, cs:], ab[r][:, cs:], [[1, 512 - cs]], ALU.is_gt, 0.0,
                                                   base=0, channel_multiplier=-1),
                 reads=[b_ab[r]], writes=[b_ab[r]])

    def Pb(i):
        s = steps[i]; r = i % NR; cs = s["cs"]
        p.op("pe", lambda e: e.matmul(banks[RB][:, cs:512], lhsT=Ls, rhs=spb[r][:, cs:], start=False, stop=s["last"],
                                      skip_group_check=True),
             reads=[b_spb[r], b_c], writes=[bb[RB]])

    def AV(i):
        s = steps[i]; r = i % NR; cs = s["cs"]; tq, kb = s["tq"], s["kb"]
        ob = OB[tq % 2]
        p.op("pe", lambda e: e.matmul(banks[ob][:, cs:512], lhsT=bigV[:, kb, :], rhs=ab[r][:, cs:], start=s["first"],
                                      stop=s["last"], skip_group_check=True),
             reads=[b_bigV, b_ab[r]], writes=[bb[ob]])
        if s["last"]:
            o2 = tq % 2
            p.op("dve", lambda e: e.tensor_copy(ot[o2], banks[ob]), writes=[b_ot[o2], bb[ob]])
            p.dma("sp", lambda e: e.dma_start(out=yT_d[:, tq * 512:(tq + 1) * 512], in_=ot[o2]),
                  b_ot[o2], reads=[b_ot[o2]], writes=[b_out])

    QK(0)
    if N > 1:
        QK(1)
    A(0)
    for i in range(N):
        if i + 1 < N:
            A(i + 1)
        Pa(i)
        if i + 2 < N:
            QK(i + 2)
        B(i)
        Pb(i)
        if i >= 1:
            AV(i - 1)
    AV(N - 1)


def emit_mlstm(cx, uqT_d, ukT_d, cw_d, v_d, ig_d, fg_d, gb_d, hT_d, b_in, b_out, S,
               bigQ, bigK, bigV, b_bigQ, b_bigK, b_bigV, bank_bf, bbuf_bf, qscale, tag="ml"):
    nc, p = cx.nc, cx.p
    NCH = S // 128
    assert NCH <= 128
    banks, bb = cx.banks, cx.bbufs
    onesB = cx.onesB
    TriF = cx.sb([128, 128], F32, tag + "_TriF")
    identF = cx.sb([128, 128], F32, tag + "_idF")
    identB = cx.sb([128, 128], BF16, tag + "_idB")
    b_c = p.buf(tag + "const")
    for t_ in (TriF, identF, identB):
        p.op("pool", lambda e, t_=t_: e.memset(t_, 1.0), writes=[b_c])
    p.op("pool", lambda e: e.affine_select(TriF, TriF, [[1, 128]], ALU.is_ge, 0.0, base=0, channel_multiplier=-1),
         reads=[b_c], writes=[b_c])
    p.op("pool", lambda e: e.affine_select(identF, identF, [[1, 128]], ALU.is_equal, 0.0, base=0, channel_multiplier=-1),
         reads=[b_c], writes=[b_c])
    p.op("pool", lambda e: e.affine_select(identB, identB, [[1, 128]], ALU.is_equal, 0.0, base=0, channel_multiplier=-1),
         reads=[b_c], writes=[b_c])
    cw = cx.sb([128, 10], F32, tag + "_cw"); gb = cx.sb([128, 2], F32, tag + "_gb"); ngb = cx.sb([128, 1], F32, tag + "_ngb")
    b_small = p.buf(tag + "small")
    p.dma("sp", lambda e: e.dma_start(out=cw, in_=cw_d), b_small, reads=[b_in], writes=[b_small])
    p.dma("sp", lambda e: e.dma_start(out=gb, in_=gb_d), b_small, reads=[b_in], writes=[b_small])
    p.op("dve", lambda e: e.tensor_scalar(ngb, gb[:, 1:2], -1.0, None, ALU.mult), reads=[b_small], writes=[b_small])
    Gi = cx.sb([128, 128], F32, tag + "_Gi"); Gf = cx.sb([128, 128], F32, tag + "_Gf")
    b_G = p.buf(tag + "G")
    if NCH < 128:
        p.op("pool", lambda e: e.memset(Gi, 0.0), writes=[b_G])
        p.op("pool", lambda e: e.memset(Gf, 0.0), writes=[b_G])
    p.dma("sp", lambda e: e.dma_start(out=Gi[0:NCH, :], in_=ig_d), b_G, reads=[b_in], writes=[b_G])
    p.dma("sp", lambda e: e.dma_start(out=Gf[0:NCH, :], in_=fg_d), b_G, reads=[b_in], writes=[b_G])
    p.op("act", lambda e: e.activation(out=Gf, in_=Gf, func=AF.Exp, bias=ngb[:, 0:1], scale=-1.0), reads=[b_G, b_small], writes=[b_G])
    p.op("act", lambda e: e.activation(out=Gf, in_=Gf, func=AF.Ln, bias=1.0), reads=[b_G], writes=[b_G])
    p.op("dve", lambda e: e.tensor_scalar(Gf, Gf, -1.0, None, ALU.mult), reads=[b_G], writes=[b_G])
    p.op("dve", lambda e: e.tensor_scalar(Gi, Gi, gb[:, 0:1], None, ALU.add), reads=[b_G, b_small], writes=[b_G])
    FT = cx.sb([128, 128], F32, tag + "_FT"); IT = cx.sb([128, 128], F32, tag + "_IT")
    b_FT = p.buf(tag + "FT")
    GB = 6
    p.op("pe", lambda e: e.transpose(banks[GB][:, 0:128], Gf, identF), reads=[b_G, b_c], writes=[bb[GB]])
    p.op("pe", lambda e: e.transpose(banks[GB][:, 128:256], Gi, identF), reads=[b_G, b_c], writes=[bb[GB]])
    p.op("dve", lambda e: e.tensor_copy(FT, banks[GB][:, 0:128]), writes=[b_FT, bb[GB]])
    p.op("dve", lambda e: e.tensor_copy(IT, banks[GB][:, 128:256]), writes=[b_FT, bb[GB]])
    p.op("pe", lambda e: e.matmul(banks[GB][:, 256:384], lhsT=TriF, rhs=FT, start=True, stop=True), reads=[b_FT, b_c], writes=[bb[GB]])
    p.op("pe", lambda e: e.matmul(banks[GB][:, 384:512], lhsT=cx.ones, rhs=FT, start=True, stop=True),
         reads=[b_FT, b_c, cx.b_ones], writes=[bb[GB]])
    biasS = cx.sb([128, 128], F32, tag + "_biasS"); Wcol = cx.sb([128, 128], F32, tag + "_Wcol")
    decay = cx.sb([128, 128], F32, tag + "_decay")
    b_gs = p.buf(tag + "gs")
    p.op("dve", lambda e: e.tensor_tensor(biasS, IT, banks[GB][:, 256:384], ALU.subtract), reads=[b_FT], writes=[b_gs, bb[GB]])
    p.op("dve", lambda e: e.tensor_tensor(Wcol, biasS, banks[GB][:, 384:512], ALU.add), reads=[b_gs], writes=[b_gs, bb[GB]])
    p.op("act", lambda e: e.activation(out=Wcol, in_=Wcol, func=AF.Exp), reads=[b_gs], writes=[b_gs])
    p.op("act", lambda e: e.activation(out=decay, in_=banks[GB][:, 384:512], func=AF.Exp), writes=[b_gs, bb[GB]])
    PC = min(2048, S)
    stg = [cx.sb([128, PC + 3], F32, f"{tag}_stg{i}") for i in range(2)]; b_stg = p.bufs(2, tag + "stg")
    acc = [cx.sb([128, PC], F32, f"{tag}_acc{i}") for i in range(2)]; b_acc = p.bufs(2, tag + "acc")
    k2 = 0
    for which, (u_d, big, b_big) in enumerate(((uqT_d, bigQ, b_bigQ), (ukT_d, bigK, b_bigK))):
        wo = which * 4
        for pi in range(S // PC):
            i2 = k2 % 2; k2 += 1
            t0 = pi * PC
            if pi == 0:
                p.op("pool", lambda e, i2=i2: e.memset(stg[i2][:, 0:3], 0.0), writes=[b_stg[i2]])
                p.dma("sp", lambda e, i2=i2, u_d=u_d: e.dma_start(out=stg[i2][:, 3:], in_=u_d[:, 0:PC]),
                      b_stg[i2], reads=[b_in], writes=[b_stg[i2]])
            else:
                p.dma("sp", lambda e, i2=i2, u_d=u_d, t0=t0: e.dma_start(out=stg[i2], in_=u_d[:, t0 - 3:t0 + PC]),
                      b_stg[i2], reads=[b_in], writes=[b_stg[i2]])
            p.op("dve", lambda e, i2=i2, wo=wo, which=which: e.tensor_scalar(
                acc[i2], stg[i2][:, 3:3 + PC], cw[:, wo + 3:wo + 4], cw[:, 8 + which:9 + which], ALU.mult, ALU.add),
                reads=[b_stg[i2], b_small], writes=[b_acc[i2]])
            for kk in (2, 1, 0):
                p.op("dve", lambda e, i2=i2, wo=wo, kk=kk: e.scalar_tensor_tensor(
                    acc[i2], stg[i2][:, kk:kk + PC], cw[:, wo + kk:wo + kk + 1], acc[i2], ALU.mult, ALU.add),
                    reads=[b_stg[i2], b_small, b_acc[i2]], writes=[b_acc[i2]])
            if which == 0:
                p.op("act", lambda e, i2=i2: e.activation(out=acc[i2], in_=acc[i2], func=AF.Silu),
                     reads=[b_acc[i2]], writes=[b_acc[i2]])
                p.op("pool", lambda e, i2=i2, t0=t0, big=big: e.tensor_scalar(big[:, t0:t0 + PC], acc[i2], qscale, None, ALU.mult),
                     reads=[b_acc[i2]], writes=[b_big])
            else:
                p.op("act", lambda e, i2=i2, t0=t0, big=big: e.activation(out=big[:, t0:t0 + PC], in_=acc[i2], func=AF.Silu),
                     reads=[b_acc[i2]], writes=[b_big])
    v_v = v_d.rearrange("(b p) d -> p b d", p=128)
    nvb = max(1, NCH // 32)
    for i in range(nvb):
        sl = slice(i * (NCH // nvb), (i + 1) * (NCH // nvb))
        p.dma("sp", lambda e, sl=sl: e.dma_start(out=bigV[:, sl, :], in_=v_v[:, sl, :]), b_bigV, reads=[b_in], writes=[b_bigV])
    kw = [cx.sb([128, 128], BF16, f"{tag}_kw{i}") for i in range(2)]; b_kw = p.bufs(2, tag + "kw")
    FTri = [cx.sb([128, 128], F32, f"{tag}_FTri{i}") for i in range(2)]; b_FTri = p.bufs(2, tag + "FTri")
    DT = [cx.sb([128, 128], F32, f"{tag}_DT{i}") for i in range(2)]; b_DT = p.bufs(2, tag + "DT")
    PT = [cx.sb([128, 128], BF16, f"{tag}_PT{i}") for i in range(2)]; b_PT = p.bufs(2, tag + "PT")
    Gx = [cx.sb([128, 128], F32, f"{tag}_Gx{i}") for i in range(2)]; b_Gx = p.bufs(2, tag + "Gx")
    qg = [cx.sb([128, 128], BF16, f"{tag}_qg{i}") for i in range(2)]; b_qg = p.bufs(2, tag + "qg")
    CN = cx.sb([128, 256], F32, tag + "_CN"); b_CN = p.buf(tag + "CN")
    CNb = [cx.sb([128, 256], BF16, f"{tag}_CNb{i}") for i in range(2)]; b_CNb = p.bufs(2, tag + "CNb")
    dn = [cx.sb([128, 128], F32, f"{tag}_dn{i}") for i in range(2)]; b_dn = p.bufs(2, tag + "dn")
    hst = [cx.sb([128, 512], F32, f"{tag}_hst{i}") for i in range(2)]; b_hst = p.bufs(2, tag + "hst")
    SB_, NB_, CB_ = (0, 1), (2, 3), (4, 5)
    p.op("pool", lambda e: e.memset(CN, 0.0), writes=[b_CN])

    def pre(c):
        i2 = c % 2
        csl = slice(c * 128, (c + 1) * 128)
        p.op("pe", lambda e: e.transpose(bank_bf[:, 0:128], bigK[:, csl], identB), reads=[b_bigK, b_c], writes=[bbuf_bf])
        p.op("act", lambda e: e.activation(out=kw[i2], in_=bank_bf[:, 0:128], func=AF.Identity, scale=Wcol[:, c:c + 1]),
             reads=[b_gs], writes=[b_kw[i2], bbuf_bf])
        cb = CB_[i2]
        p.op("pe", [lambda e: e.matmul(banks[cb][:, 0:128], lhsT=kw[i2], rhs=bigV[:, c, :], start=True, stop=True),
                    lambda e: e.matmul(banks[cb][:, 128:256], lhsT=kw[i2], rhs=onesB, start=True, stop=True)],
             reads=[b_kw[i2], b_bigV, cx.b_ones], writes=[bb[cb]])
        p.op("dve", lambda e: e.tensor_scalar(FTri[i2], TriF, FT[:, c:c + 1], None, ALU.mult), reads=[b_c, b_FT], writes=[b_FTri[i2]])
        sb_ = SB_[i2]
        p.op("pe", [lambda e: e.matmul(banks[sb_][:, 0:128], lhsT=bigK[:, csl], rhs=bigQ[:, csl], start=True, stop=True),
                    lambda e: e.matmul(banks[sb_][:, 128:256], lhsT=cx.ones, rhs=FTri[i2], start=True, stop=True)],
             reads=[b_bigK, b_bigQ, b_FTri[i2], cx.b_ones], writes=[bb[sb_]])
        p.op("act", lambda e: e.activation(out=DT[i2], in_=banks[sb_][:, 128:256], func=AF.Exp, bias=biasS[:, c:c + 1]),
             reads=[b_gs], writes=[b_DT[i2], bb[sb_]])
        p.op("act", lambda e: e.activation(out=Gx[i2], in_=banks[sb_][:, 128:256], func=AF.Exp),
             writes=[b_Gx[i2], bb[sb_]])
        p.op("pool", lambda e: e.affine_select(DT[i2], DT[i2], [[1, 128]], ALU.is_ge, 0.0, base=0, channel_multiplier=-1),
             reads=[b_DT[i2]], writes=[b_DT[i2]])
        p.op("dve", lambda e: e.tensor_tensor(PT[i2], banks[sb_][:, 0:128], DT[i2], ALU.mult),
             reads=[b_DT[i2]], writes=[b_PT[i2], bb[sb_]])
        p.op("pool", lambda e: e.tensor_tensor(qg[i2], bigQ[:, csl], Gx[i2], ALU.mult),
             reads=[b_bigQ, b_Gx[i2]], writes=[b_qg[i2]])

    def post(c):
        i2 = c % 2
        nb_ = NB_[i2]
        sprev = (c - 1) % 2
        fns = [lambda e: e.matmul(banks[nb_][:, 0:128], lhsT=bigV[:, c, :], rhs=PT[i2], start=True, stop=(c == 0))]
        if c > 0:
            fns.append(lambda e: e.matmul(banks[nb_][:, 0:128], lhsT=CNb[sprev][:, 0:128], rhs=qg[i2], start=False, stop=True))
        fns.append(lambda e: e.matmul(banks[nb_][:, 128:256], lhsT=onesB, rhs=PT[i2], start=True, stop=(c == 0)))
        if c > 0:
            fns.append(lambda e: e.matmul(banks[nb_][:, 128:256], lhsT=CNb[sprev][:, 128:256], rhs=qg[i2], start=False, stop=True))
        p.op("pe", fns, reads=[b_bigV, b_PT[i2], b_qg[i2], b_CNb[sprev], cx.b_ones], writes=[bb[nb_]])
        p.op("act", lambda e: e.activation(out=dn[i2], in_=banks[nb_][:, 128:256], func=AF.Abs),
             writes=[b_dn[i2], bb[nb_]])
        p.op("dve", lambda e: e.tensor_scalar(dn[i2], dn[i2], 1.0, None, ALU.max), reads=[b_dn[i2]], writes=[b_dn[i2]])
        p.op("dve", lambda e: e.reciprocal(dn[i2], dn[i2]), reads=[b_dn[i2]], writes=[b_dn[i2]])
        h2 = (c // 4) % 2
        hs = (c % 4) * 128
        p.op("dve", lambda e: e.tensor_tensor(hst[h2][:, hs:hs + 128], banks[nb_][:, 0:128], dn[i2], ALU.mult),
             reads=[b_dn[i2]], writes=[b_hst[h2], bb[nb_]])
        if c % 4 == 3 or c == NCH - 1:
            g0 = (c // 4) * 512
            n = (c % 4 + 1) * 128
            p.dma("sp", lambda e: e.dma_start(out=hT_d[:, g0:g0 + n], in_=hst[h2][:, 0:n]), b_hst[h2],
                  reads=[b_hst[h2]], writes=[b_out])
        cb = CB_[i2]
        p.op("dve", lambda e: e.scalar_tensor_tensor(CN, CN, decay[:, c:c + 1], banks[cb][:, 0:256], ALU.mult, ALU.add),
             reads=[b_CN, b_gs], writes=[b_CN, bb[cb]])
        p.op("act", lambda e: e.activation(out=CNb[i2], in_=CN, func=AF.Copy), reads=[b_CN], writes=[b_CNb[i2]])

    pre(0)
    for c in range(NCH):
        if c + 1 < NCH:
            pre(c + 1)
        post(c)


def emit_outproj(cx, yT_all, oT, xT, woutr, vecs, ngv, ynT, rT, xoT, b_in, b_yn, b_r, b_xo, D, NT, T, ln_eps, alpha, tag="op"):
    nc, p = cx.nc, cx.p
    DC = D // 128
    NS = min(512, T)
    nts = T // NS
    vec = cx.sb([128, 3 * DC], F32, tag + "_vec"); b_vec = p.buf(tag + "vec")
    gw = cx.sb([128, DC], F32, tag + "_gw")
    ng = cx.sb([128, 32], F32, tag + "_ng")
    p.dma("sp", lambda e: e.dma_start(out=vec, in_=vecs), b_vec, writes=[b_vec])
    p.dma("sp", lambda e: e.dma_start(out=ng, in_=ngv), b_vec, writes=[b_vec])
    p.op("dve", lambda e: e.tensor_scalar(gw, vec[:, 0:DC], 1.0, 1.0 / alpha, ALU.add, ALU.mult), reads=[b_vec], writes=[b_vec])
    LS = min(512, NT)
    emit_ln_phase(cx, yT_all[0:1024, :], ynT[0:1024, :], ng[:, 0:8], ng[:, 16:24], b_vec, b_in, b_yn, GC=1, NT=NT, LS=LS,
                  eps=ln_eps, ngroups=8, tag=tag + "hn1")
    emit_ln_phase(cx, yT_all[1024:2048, :], ynT[1024:2048, :], ng[:, 8:16], ng[:, 16:24], b_vec, b_in, b_yn, GC=2, NT=NT, LS=LS,
                  eps=ln_eps, ngroups=4, tag=tag + "hn2")
    ybf = cx.sb([128, 16, T], BF16, tag + "_ybf"); b_ybf = p.bufs(nts, tag + "ybf")
    yin = [cx.sb([128, NS], F32, f"{tag}_yin{i}") for i in range(3)]; b_yin = p.bufs(3, tag + "yin")
    oin = [cx.sb([128, NS], F32, f"{tag}_oin{i}") for i in range(3)]; b_oin = p.bufs(3, tag + "oin")
    stc = StageC(cx, 16, DC, T, NS, tag + "C")
    ynv = ynT.rearrange("(c p) t -> p c t", p=128)
    ov = oT.rearrange("(c p) t -> p c t", p=128)
    xTv = xT.rearrange("(c p) t -> p c t", p=128)
    rTv = rT.rearrange("(c p) t -> p c t", p=128)
    oTv = xoT.rearrange("(c p) t -> p c t", p=128)
    ky = 0
    for tt in range(NT // T):
        t0 = tt * T
        for ts in range(nts):
            gsl = slice(t0 + ts * NS, t0 + (ts + 1) * NS)
            sl = slice(ts * NS, (ts + 1) * NS)
            for c in range(16):
                i3 = ky % 3; ky += 1
                p.dma("sp", lambda e, i3=i3, c=c, gsl=gsl: e.dma_start(out=yin[i3], in_=ynv[:, c, gsl]), b_yin[i3],
                      reads=[b_yn], writes=[b_yin[i3]])
                if c < 8:
                    p.op("pool", lambda e, i3=i3, c=c, sl=sl: e.tensor_copy(ybf[:, c, sl], yin[i3]), reads=[b_yin[i3]], writes=[b_ybf[ts]])
                else:
                    p.dma("sp", lambda e, i3=i3, c=c, gsl=gsl: e.dma_start(out=oin[i3], in_=ov[:, c - 8, gsl]), b_oin[i3],
                          reads=[b_in], writes=[b_oin[i3]])
                    p.op("act", lambda e, i3=i3: e.activation(out=oin[i3], in_=oin[i3], func=AF.Sigmoid), reads=[b_oin[i3]], writes=[b_oin[i3]])
                    p.op("dve", lambda e, i3=i3, c=c, sl=sl: e.tensor_tensor(ybf[:, c, sl], yin[i3], oin[i3], ALU.mult),
                         reads=[b_yin[i3], b_oin[i3]], writes=[b_ybf[ts]])
        for it_ in stc.run(ybf, b_ybf, woutr, xTv, rTv, oTv, gw, b_vec, vec[:, DC:2 * DC], vec[:, 2 * DC:3 * DC], b_vec,
                           t0, ln_eps / (alpha * alpha), b_in, b_r, b_xo):
            it_()


def emit_mod(cx, cl, wr, br, outd, b_out, D, NCOL, nlayers, tag="md"):
    nc, p = cx.nc, cx.p
    DC = D // 128
    cs = cx.sb([128, DC], F32, tag + "_c"); cb = cx.sb([128, DC], BF16, tag + "_cb"); b_c = p.buf(tag + "c")
    p.dma("sp", lambda e: e.dma_start(out=cs, in_=cl), b_c, writes=[b_c])
    p.op("act", lambda e: e.activation(out=cb, in_=cs, func=AF.Silu), reads=[b_c], writes=[b_c])
    wt = [cx.sb([128, DC * NCOL], BF16, f"{tag}_w{i}") for i in range(2)]; b_wt = p.bufs(2, tag + "w")
    bt = [cx.sb([1, NCOL], F32, f"{tag}_b{i}") for i in range(2)]; b_bt = p.bufs(2, tag + "b")
    ob = [cx.sb([1, NCOL], F32, f"{tag}_o{i}") for i in range(2)]; b_ob = p.bufs(2, tag + "o")
    kb = 0
    for l in range(nlayers):
        i2 = l % 2
        p.dma("pool", lambda e, i2=i2, l=l: cdma(e, wt[i2], wr[l]), b_wt[i2], writes=[b_wt[i2]])
        p.dma("sp", lambda e, i2=i2, l=l: e.dma_start(out=bt[i2], in_=br[l]), b_bt[i2], writes=[b_bt[i2]])
        n0 = 0
        while n0 < NCOL:
            n = min(512, NCOL - n0)
            bk = kb % 2; kb += 1
            fns = []
            for c in range(DC):
                fns.append(lambda e, i2=i2, c=c, n0=n0, n=n, bk=bk: e.matmul(
                    cx.banks[bk][0:1, 0:n], lhsT=cb[:, c:c + 1], rhs=wt[i2][:, c * NCOL + n0:c * NCOL + n0 + n],
                    start=(c == 0), stop=(c == DC - 1)))
            p.op("pe", fns, reads=[b_c, b_wt[i2]], writes=[cx.bbufs[bk]])
            p.op("dve", lambda e, i2=i2, n0=n0, n=n, bk=bk: e.tensor_tensor(ob[i2][:, n0:n0 + n], cx.banks[bk][0:1, 0:n], bt[i2][:, n0:n0 + n], ALU.add),
                 reads=[b_bt[i2]], writes=[b_ob[i2], cx.bbufs[bk]])
            n0 += n
        p.dma("sp", lambda e, i2=i2, l=l: e.dma_start(out=outd[l], in_=ob[i2]), b_ob[i2], reads=[b_ob[i2]], writes=[b_out])


class Cfg:
    def __init__(self, D=2048, DFF=5632, S=16384, depth=2):
        self.D, self.DFF, self.S, self.depth = D, DFF, S, depth
        self.NT = S // NCORES
        self.T = min(1024, self.NT)
        self.alpha = (2 * depth) ** 0.25
        self.eps = 1e-5
        self.NMOD = 9 * D
        assert self.NMOD % NCORES == 0
        self.NCOL = self.NMOD // NCORES


_CACHE = {}


def _new_nc():
    return bass.Bass("TRN2", target_bir_lowering=False)


def _din(nc, name, shape, dt=F32):
    return nc.dram_tensor(name, list(shape), dt, kind="ExternalInput").ap()


def _dout(nc, name, shape, dt=F32):
    return nc.dram_tensor(name, list(shape), dt, kind="ExternalOutput").ap()


def _dscr(nc, name, shape, dt=F32):
    return nc.dram_tensor(name, list(shape), dt).ap()


def build_mod(cfg):
    nc = _new_nc(); p = Prog(nc); cx = Ctx(nc, p)
    DC = cfg.D // 128
    cl = _din(nc, "cl", [128, DC]); wr = _din(nc, "wr", [cfg.depth, 128, DC * cfg.NCOL]); br = _din(nc, "br", [cfg.depth, 1, cfg.NCOL])
    outd = _dout(nc, "mod", [cfg.depth, 1, cfg.NCOL])
    b_out = p.buf("out")
    emit_mod(cx, cl, wr, br, outd, b_out, cfg.D, cfg.NCOL, cfg.depth)
    p.wait_all("sp", [b_out]); p.emit()
    return nc


def build_ffn(cfg):
    nc = _new_nc(); p = Prog(nc); cx = Ctx(nc, p)
    D, DFF, NT = cfg.D, cfg.DFF, cfg.NT
    DC, HC = D // 128, DFF // 128
    xT = _din(nc, "xT", [D, NT]); w1r = _din(nc, "w1r", [HC, 128, DC * 256]); w2r = _din(nc, "w2r", [DC, 128, HC * 128])
    vecs = _din(nc, "vecs", [128, 5 * DC]); rT = _dscr(nc, "rT", [D, NT]); xoT = _dout(nc, "xoT", [D, NT])
    b_x, b_r, b_xo = p.bufs(3, "dram")
    emit_ffn(cx, xT, w1r, w2r, vecs, rT, xoT, b_x, b_r, b_xo, D, DFF, NT, cfg.T, 0.5, cfg.alpha, cfg.eps)
    p.wait_all("sp", [b_xo]); p.emit()
    return nc


def build_inproj(cfg):
    nc = _new_nc(); p = Prog(nc); cx = Ctx(nc, p)
    D, NT = cfg.D, cfg.NT
    DC = D // 128
    xT = _din(nc, "xT", [D, NT]); vecs = _din(nc, "vecs", [128, 2 * DC])
    wfm = _din(nc, "wfm", [33, 128, DC * 128]); wtm = _din(nc, "wtm", [4, 128, DC * 512])
    qkT = _dout(nc, "qkT", [16, 128, NT], BF16); vsb = _dout(nc, "vsb", [NT, 1024], BF16)
    mqkT = _dout(nc, "mqkT", [8, 128, NT]); vml = _dout(nc, "vml", [NT, 1024], BF16)
    oT = _dout(nc, "oT", [8, 128, NT]); gTd = _dout(nc, "gT", [8, NT])
    b_x, b_out = p.bufs(2, "dram")
    emit_inproj(cx, xT, vecs, wfm, wtm, qkT, vsb, mqkT, vml, oT, gTd, b_x, b_out, D, NT, 128 ** -0.5)
    p.wait_all("sp", [b_out]); p.emit()
    return nc


def build_attn(cfg):
    nc = _new_nc(); p = Prog(nc); cx = Ctx(nc, p, bf16_bank=7)
    S = cfg.S
    NCH = S // 128
    qT = _din(nc, "qT", [128, S], BF16); kT = _din(nc, "kT", [128, S], BF16); v = _din(nc, "v", [S, 128], BF16)
    uq = _din(nc, "uq", [128, S]); uk = _din(nc, "uk", [128, S]); cw = _din(nc, "cw", [128, 10])
    vm = _din(nc, "vm", [S, 128], BF16); ig = _din(nc, "ig", [NCH, 128]); fg = _din(nc, "fg", [NCH, 128]); gb = _din(nc, "gb", [128, 2])
    yT = _dout(nc, "yT", [128, S]); hT = _dout(nc, "hT", [128, S])
    bigQ = cx.sb([128, S], BF16, "bigQ"); bigK = cx.sb([128, S], BF16, "bigK"); bigV = cx.sb([128, S // 128, 128], BF16, "bigV")
    bQ, bK, bV, b_in, b_out = p.bufs(5, "x")
    emit_mlstm(cx, uq, uk, cw, vm, ig, fg, gb, hT, b_in, b_out, S, bigQ, bigK, bigV, bQ, bK, bV, cx.banks[7], cx.bbufs[7], 128 ** -0.5)
    emit_sb_attn(cx, qT, kT, v, yT, b_in, b_out, S, bigQ, bigK, bigV, bQ, bK, bV)
    p.wait_all("sp", [b_out]); p.emit()
    return nc


def build_outproj(cfg):
    nc = _new_nc(); p = Prog(nc); cx = Ctx(nc, p)
    D, NT = cfg.D, cfg.NT
    DC = D // 128
    yT_all = _din(nc, "yT_all", [2048, NT]); oT = _din(nc, "oT", [1024, NT]); xT = _din(nc, "xT", [D, NT])
    woutr = _din(nc, "woutr", [DC, 128, 16 * 128]); vecs = _din(nc, "vecs", [128, 3 * DC]); ngv = _din(nc, "ngv", [128, 32])
    ynT = _dscr(nc, "ynT", [2048, NT]); rT = _dscr(nc, "rT", [D, NT]); xoT = _dout(nc, "xoT", [D, NT])
    b_in, b_yn, b_r, b_xo = p.bufs(4, "dram")
    emit_outproj(cx, yT_all, oT, xT, woutr, vecs, ngv, ynT, rT, xoT, b_in, b_yn, b_r, b_xo, D, NT, cfg.T, cfg.eps, cfg.alpha)
    p.wait_all("sp", [b_xo]); p.emit()
    return nc


def _get(cfg, name, builder):
    key = (name, cfg.D, cfg.DFF, cfg.S, cfg.depth)
    if key not in _CACHE:
        _CACHE[key] = builder(cfg)
    return _CACHE[key]


def _run(nc, in_maps):
    res = run_bass_kernel_spmd(nc, in_maps, core_ids=list(range(NCORES)))
    return res.results


def ffn_host_layout(w1, w2, D, DFF):
    DC, HC = D // 128, DFF // 128
    a = w1[:, :DFF].reshape(DC, 128, HC, 128)
    u = w1[:, DFF:].reshape(DC, 128, HC, 128)
    au = np.stack([a, u], axis=3)
    w1r = np.ascontiguousarray(au.transpose(2, 1, 0, 3, 4)).reshape(HC, 128, DC * 256)
    w2r = np.ascontiguousarray(w2.reshape(HC, 128, DC, 128).transpose(2, 1, 0, 3)).reshape(DC, 128, HC * 128)
    return w1r, w2r


def vec_layout(vs, D):
    DC = D // 128
    return np.ascontiguousarray(np.stack([np.asarray(v, np.float32).reshape(DC, 128).T for v in vs], axis=1)).reshape(128, len(vs) * DC)


def kernel_cfg(cfg, x, c, ada_w, ada_b, ln_g, ln_b, ffn1_w1, ffn1_w2, mix_w_in, mlstm_conv_w, mlstm_conv_b,
               mlstm_gate_b, mix_norm_g, mix_w_out, ffn2_w1, ffn2_w2):
    D, DFF, S, NT, depth = cfg.D, cfg.DFF, cfg.S, cfg.NT, cfg.depth
    DC = D // 128
    f32 = np.float32
    x = np.asarray(x, f32); c = np.asarray(c, f32)
    NCOL = cfg.NCOL
    cl = vec_layout([c[0]], D)
    aw = np.asarray(ada_w, f32).reshape(depth, DC, 128, NCORES, NCOL)
    in_maps = []
    for i in range(NCORES):
        wr = np.ascontiguousarray(aw[:, :, :, i, :].transpose(0, 2, 1, 3)).reshape(depth, 128, DC * NCOL)
        br = np.ascontiguousarray(np.asarray(ada_b, f32).reshape(depth, NCORES, 1, NCOL)[:, i])
        in_maps.append({"cl": cl, "wr": wr, "br": br})
    res = _run(_get(cfg, "mod", build_mod), in_maps)
    mod = np.concatenate([r["mod"].reshape(depth, NCOL) for r in res], axis=1).reshape(depth, 3, 3, D)
    xTs = [np.ascontiguousarray(x[0, i * NT:(i + 1) * NT, :].T) for i in range(NCORES)]
    nc_ffn = _get(cfg, "ffn", build_ffn)
    nc_ip = _get(cfg, "inproj", build_inproj)
    nc_at = _get(cfg, "attn", build_attn)
    nc_op = _get(cfg, "outproj", build_outproj)
    zeros = np.zeros(D, f32)

    def run_ffn(xTs, w1, w2, l, sub):
        w1r, w2r = ffn_host_layout(np.asarray(w1, f32), np.asarray(w2, f32), D, DFF)
        vecs = vec_layout([mod[l, sub, 1], mod[l, sub, 0], mod[l, sub, 2], ln_g[l, sub], ln_b[l, sub]], D)
        res = _run(nc_ffn, [{"xT": xTs[i], "w1r": w1r, "w2r": w2r, "vecs": vecs} for i in range(NCORES)])
        return [r["xoT"] for r in res]

    for l in range(depth):
        xTs = run_ffn(xTs, ffn1_w1[l], ffn1_w2[l], l, 0)
        W = np.asarray(mix_w_in[l], f32)
        fm_cols = [W[:, 0:2048], W[:, 3072:4096], W[:, 5120:6144]]
        gpad = np.zeros((D, 128), f32); gpad[:, 0:8] = W[:, 6144:6152]
        Wfm = np.concatenate(fm_cols + [gpad], axis=1)
        wfm = np.ascontiguousarray(Wfm.reshape(DC, 128, 33, 128).transpose(2, 1, 0, 3)).reshape(33, 128, DC * 128)
        Wtm = np.concatenate([W[:, 2048:3072], W[:, 4096:5120]], axis=1)
        wtm = np.ascontiguousarray(Wtm.reshape(DC, 128, 4, 512).transpose(2, 1, 0, 3)).reshape(4, 128, DC * 512)
        vecs = vec_layout([mod[l, 1, 1], mod[l, 1, 0]], D)
        rip = _run(nc_ip, [{"xT": xTs[i], "vecs": vecs, "wfm": wfm, "wtm": wtm} for i in range(NCORES)])
        qkT = np.concatenate([r["qkT"] for r in rip], axis=2)
        vsb = np.concatenate([r["vsb"] for r in rip], axis=0)
        mqkT = np.concatenate([r["mqkT"] for r in rip], axis=2)
        vml = np.concatenate([r["vml"] for r in rip], axis=0)
        gT = np.concatenate([r["gT"] for r in rip], axis=1)
        cwl = np.asarray(mlstm_conv_w[l], f32); cbl = np.asarray(mlstm_conv_b[l], f32); gbl = np.asarray(mlstm_gate_b[l], f32)
        in_maps = []
        for h in range(NCORES):
            hm, vh = h // 2, h % 2
            cw = np.concatenate([cwl[:, hm * 128:(hm + 1) * 128].T, cwl[:, 512 + hm * 128:512 + (hm + 1) * 128].T,
                                 cbl[hm * 128:(hm + 1) * 128, None], cbl[512 + hm * 128:512 + (hm + 1) * 128, None]], axis=1)
            gb = np.ascontiguousarray(np.broadcast_to(np.array([gbl[hm], gbl[4 + hm]], f32), (128, 2)))
            in_maps.append({
                "qT": np.ascontiguousarray(qkT[h]), "kT": np.ascontiguousarray(qkT[8 + h]),
                "v": np.ascontiguousarray(vsb[:, h * 128:(h + 1) * 128]),
                "uq": np.ascontiguousarray(mqkT[hm]), "uk": np.ascontiguousarray(mqkT[4 + hm]),
                "cw": np.ascontiguousarray(cw, dtype=f32),
                "vm": np.ascontiguousarray(vml[:, hm * 256 + vh * 128:hm * 256 + (vh + 1) * 128]),
                "ig": np.ascontiguousarray(gT[hm].reshape(S // 128, 128)), "fg": np.ascontiguousarray(gT[4 + hm].reshape(S // 128, 128)),
                "gb": gb})
        rat = _run(nc_at, in_maps)
        yall = np.concatenate([r["yT"] for r in rat] + [r["hT"] for r in rat], axis=0)
        ng = np.asarray(mix_norm_g[l], f32)
        ngv = np.concatenate([vec_layout([ng], 2048), np.zeros((128, 16), f32)], axis=1)
        woutr = np.ascontiguousarray(np.asarray(mix_w_out[l], f32).reshape(16, 128, DC, 128).transpose(2, 1, 0, 3)).reshape(DC, 128, 16 * 128)
        vecs = vec_layout([mod[l, 1, 2], ln_g[l, 1], ln_b[l, 1]], D)
        rop = _run(nc_op, [{"yT_all": np.ascontiguousarray(yall[:, i * NT:(i + 1) * NT]),
                            "oT": rip[i]["oT"].reshape(1024, NT), "xT": xTs[i], "woutr": woutr, "vecs": vecs, "ngv": ngv}
                           for i in range(NCORES)])
        xTs = [r["xoT"] for r in rop]
        xTs = run_ffn(xTs, ffn2_w1[l], ffn2_w2[l], l, 2)
    out = np.concatenate([xt.T for xt in xTs], axis=0)[None]
    return np.ascontiguousarray(out, dtype=f32)


def kernel(**inputs):
    cfg = Cfg()
    return kernel_cfg(cfg, **inputs)
```

```python
import numpy as np
import ml_dtypes
import concourse.bass as bass
import concourse.mybir as mybir
from concourse.bass_utils import run_bass_kernel_spmd

F32 = mybir.dt.float32
BF16 = mybir.dt.bfloat16
AF = mybir.ActivationFunctionType
ALU = mybir.AluOpType
NPBF = ml_dtypes.bfloat16

ENGS = ("pe", "act", "dve", "pool", "sp")
NCORES = 8


class Buf:
    __slots__ = ("name", "w", "r", "dsem", "dcnt")

    def __init__(self, name):
        self.name = name
        self.w = {}
        self.r = {}
        self.dsem = None
        self.dcnt = 0


class Prog:
    def __init__(self, nc):
        self.nc = nc
        self.ops = {e: [] for e in ENGS}
        self.sems = {}
        self.cnt = {}
        self.seen = {e: {} for e in ENGS}
        for e in ENGS:
            self.sems[e] = nc.alloc_semaphore("sem_" + e)
            self.cnt[e] = 0
        self.nbuf = 0

    def buf(self, name=None):
        self.nbuf += 1
        return Buf((name or "b") + "_" + str(self.nbuf))

    def bufs(self, n, name="b"):
        return [self.buf(f"{name}{i}") for i in range(n)]

    def _deps(self, eng, reads, writes):
        need = {}

        def add(d, same_ok):
            for k, v in d.items():
                if k == eng and same_ok:
                    continue
                if need.get(k, 0) < v:
                    need[k] = v
        for b in reads:
            add(b.w, False)
        for b in writes:
            add(b.w, True)
            add(b.r, True)
        waits = []
        seen = self.seen[eng]
        for k, v in need.items():
            if seen.get(k, 0) < v:
                seen[k] = v
                waits.append((k, v))
        return waits

    def _record(self, tok, reads, writes):
        k, v = tok
        for b in reads:
            if b.r.get(k, 0) < v:
                b.r[k] = v
        for b in writes:
            if b.w.get(k, 0) < v:
                b.w[k] = v

    def op(self, eng, fns, reads=(), writes=()):
        if callable(fns):
            fns = [fns]
        waits = self._deps(eng, reads, writes)
        self.cnt[eng] += 1
        tok = (eng, self.cnt[eng])
        self._record(tok, reads, writes)
        self.ops[eng].append((waits, fns, (eng, 1)))
        return tok

    def dma(self, q, fn, prim, reads=(), writes=()):
        waits = self._deps(q, reads, writes)
        if prim.dsem is None:
            prim.dsem = "d_" + prim.name
            self.sems[prim.dsem] = self.nc.alloc_semaphore(prim.dsem)
        prim.dcnt += 16
        tok = (prim.dsem, prim.dcnt)
        self._record(tok, reads, writes)
        self.ops[q].append((waits, [fn], (prim.dsem, 16)))
        return tok

    def wait_all(self, eng, bufs):
        waits = self._deps(eng, (), bufs)
        self.ops[eng].append((waits, [], None))

    def emit(self):
        nc, sems, ops = self.nc, self.sems, self.ops
        self.ops = {e: [] for e in ENGS}

        def run(e, lst):
            for waits, fns, inc in lst:
                for k, v in waits:
                    e.wait_ge(sems[k], v)
                n = len(fns)
                for i, fn in enumerate(fns):
                    ins = fn(e)
                    if i == n - 1 and inc is not None:
                        ins.then_inc(sems[inc[0]], inc[1])

        with nc.Block() as block:
            @block.tensor
            def _(e):
                run(e, ops["pe"])

            @block.scalar
            def _(e):
                run(e, ops["act"])

            @block.vector
            def _(e):
                run(e, ops["dve"])

            @block.gpsimd
            def _(e):
                run(e, ops["pool"])

            @block.sync
            def _(e):
                run(e, ops["sp"])


def cdma(e, out, in_):
    n = out.shape[-1]
    if n > 2048:
        for b in (2048, 1024, 512, 256, 128):
            if n % b == 0:
                break
        out = out.rearrange("p (a b) -> p a b", b=b)
        in_ = in_.rearrange("p (a b) -> p a b", b=b)
    return e.dma_start(out=out, in_=in_)


class Ctx:
    def __init__(self, nc, p, bf16_bank=None):
        self.nc, self.p = nc, p
        self.banks = [(nc.alloc_psum_tensor(f"bank{i}", [128, 1024], BF16).ap() if i == bf16_bank else
                       nc.alloc_psum_tensor(f"bank{i}", [128, 512], F32).ap()) for i in range(8)]
        self.bbufs = p.bufs(8, "bank")
        self.ones = nc.alloc_sbuf_tensor("ones_f32", [128, 128], F32).ap()
        self.onesB = nc.alloc_sbuf_tensor("ones_bf16", [128, 128], BF16).ap()
        self.b_ones = p.buf("ones")
        p.op("pool", lambda e: e.memset(self.ones, 1.0), writes=[self.b_ones])
        p.op("pool", lambda e: e.memset(self.onesB, 1.0), writes=[self.b_ones])
        self.n_sb = 0

    def sb(self, shape, dt, name=None):
        self.n_sb += 1
        return self.nc.alloc_sbuf_tensor((name or "sb") + f"_{self.n_sb}", shape, dt).ap()


def emit_ln_phase(cx, rT, outT, vec_g, vec_b, b_vec, b_r_dram, b_out_dram, GC, NT, LS, eps, ngroups=1,
                  banks=(6, 7), tag="ln"):
    nc, p = cx.nc, cx.p
    nsub = NT // LS
    Dg = GC * 128
    rl = [cx.sb([128, GC, LS], F32, f"{tag}_rl{i}") for i in range(2)]
    b_rl = p.bufs(2, tag + "rl")
    sq = [cx.sb([128, LS], F32, f"{tag}_sq{i}") for i in range(2)]
    b_sq = p.bufs(2, tag + "sq")
    mean = cx.sb([128, LS], F32, tag + "_mean"); b_mean = p.buf(tag + "mean")
    m2 = cx.sb([128, LS], F32, tag + "_m2"); b_m2 = p.buf(tag + "m2")
    var = cx.sb([128, LS], F32, tag + "_var"); b_var = p.buf(tag + "var")
    rstd = cx.sb([128, LS], F32, tag + "_rstd"); b_rstd = p.buf(tag + "rstd")
    nmr = cx.sb([128, LS], F32, tag + "_nmr"); b_nmr = p.buf(tag + "nmr")
    t1 = [cx.sb([128, LS], F32, f"{tag}_t1{i}") for i in range(2)]; b_t1 = p.bufs(2, tag + "t1")
    t2 = [cx.sb([128, LS], F32, f"{tag}_t2{i}") for i in range(2)]; b_t2 = p.bufs(2, tag + "t2")
    ot = [cx.sb([128, GC, LS], F32, f"{tag}_ot{i}") for i in range(2)]; b_ot = p.bufs(2, tag + "ot")
    bs, bq = banks
    k = 0
    it = 0
    for g in range(ngroups):
        rTg = rT[g * Dg:(g + 1) * Dg, :].rearrange("(c p) t -> p c t", p=128)
        oTg = outT[g * Dg:(g + 1) * Dg, :].rearrange("(c p) t -> p c t", p=128)
        for s in range(nsub):
            i2 = it % 2
            it += 1
            tsl = slice(s * LS, (s + 1) * LS)
            p.dma("sp", lambda e, i2=i2, tsl=tsl, rTg=rTg: e.dma_start(out=rl[i2], in_=rTg[:, :, tsl]),
                  b_rl[i2], reads=[b_r_dram], writes=[b_rl[i2]])
            for c in range(GC):
                q2 = k % 2
                k += 1
                p.op("act", lambda e, i2=i2, c=c, q2=q2: e.activation(out=sq[q2], in_=rl[i2][:, c, :], func=AF.Square),
                     reads=[b_rl[i2]], writes=[b_sq[q2]])
                p.op("pe", [lambda e, i2=i2, c=c: e.matmul(cx.banks[bs][:, 0:LS], lhsT=cx.ones, rhs=rl[i2][:, c, :],
                                                           start=(c == 0), stop=(c == GC - 1)),
                            lambda e, q2=q2, c=c: e.matmul(cx.banks[bq][:, 0:LS], lhsT=cx.ones, rhs=sq[q2],
                                                           start=(c == 0), stop=(c == GC - 1))],
                     reads=[b_rl[i2], b_sq[q2], cx.b_ones], writes=[cx.bbufs[bs], cx.bbufs[bq]])
            p.op("dve", lambda e: e.tensor_scalar(mean, cx.banks[bs][:, 0:LS], 1.0 / Dg, None, ALU.mult),
                 writes=[b_mean, cx.bbufs[bs]])
            p.op("dve", lambda e: e.tensor_tensor(m2, mean, mean, ALU.mult), reads=[b_mean], writes=[b_m2])
            p.op("dve", lambda e: e.tensor_scalar(var, cx.banks[bq][:, 0:LS], 1.0 / Dg, eps, ALU.mult, ALU.add),
                 writes=[b_var, cx.bbufs[bq]])
            p.op("dve", lambda e: e.tensor_tensor(m2, var, m2, ALU.subtract), reads=[b_var, b_m2], writes=[b_m2])
            p.op("act", lambda e: e.activation(out=var, in_=m2, func=AF.Sqrt), reads=[b_m2], writes=[b_var])
            p.op("dve", lambda e: e.reciprocal(rstd, var), reads=[b_var], writes=[b_rstd])
            p.op("dve", lambda e: e.scalar_tensor_tensor(nmr, mean, -1.0, rstd, ALU.mult, ALU.mult),
                 reads=[b_mean, b_rstd], writes=[b_nmr])
            for c in range(GC):
                q2 = k % 2
                k += 1
                p.op("dve", lambda e, i2=i2, c=c, q2=q2: e.tensor_tensor(t1[q2], rl[i2][:, c, :], rstd, ALU.mult),
                     reads=[b_rl[i2], b_rstd], writes=[b_t1[q2]])
                p.op("pool", lambda e, q2=q2: e.tensor_tensor(t2[q2], t1[q2], nmr, ALU.add),
                     reads=[b_t1[q2], b_nmr], writes=[b_t2[q2]])
                vi = g * GC + c
                p.op("act", lambda e, i2=i2, c=c, q2=q2, vi=vi: e.activation(
                    out=ot[i2][:, c, :], in_=t2[q2], func=AF.Identity,
                    bias=vec_b[:, vi:vi + 1], scale=vec_g[:, vi:vi + 1]),
                    reads=[b_t2[q2], b_vec], writes=[b_ot[i2]])
            p.dma("sp", lambda e, i2=i2, tsl=tsl, oTg=oTg: e.dma_start(out=oTg[:, :, tsl], in_=ot[i2]),
                  b_ot[i2], reads=[b_ot[i2]], writes=[b_out_dram])


class StageC:
    def __init__(self, cx, KC, DC, T, NS, tag, w2ext=None, NW2=3):
        p = cx.p
        self.cx, self.KC, self.DC, self.T, self.NS, self.tag = cx, KC, DC, T, NS, tag
        nts = T // NS
        self.nts = nts
        if w2ext is None:
            self.w2t = [cx.sb([128, KC * 128], BF16, f"{tag}_w2t{i}") for i in range(NW2)]; self.b_w2t = p.bufs(NW2, tag + "w2t")
            self.w2wr = [[b] for b in self.b_w2t]
        else:
            self.w2t, self.b_w2t, self.w2wr = w2ext
        self.NW2 = len(self.w2t)
        self.xres = [cx.sb([128, NS], F32, f"{tag}_xres{i}") for i in range(2)]; self.b_xres = p.bufs(2, tag + "xres")
        self.rt = [cx.sb([128, NS], F32, f"{tag}_rt{i}") for i in range(2)]; self.b_rt = p.bufs(2, tag + "rt")
        self.rbf = [cx.sb([128, NS], BF16, f"{tag}_rbf{i}") for i in range(2)]; self.b_rbf = p.bufs(2, tag + "rbf")
        self.sqbf = [cx.sb([128, NS], BF16, f"{tag}_sqbf{i}") for i in range(2)]; self.b_sqbf = p.bufs(2, tag + "sqbf")
        self.asum = cx.sb([128, T], F32, tag + "_asum"); self.b_asum = p.bufs(nts, tag + "asum")
        self.asq = cx.sb([128, T], F32, tag + "_asq"); self.b_asq = p.bufs(nts, tag + "asq")
        self.nmr = cx.sb([128, T], F32, tag + "_nmr"); self.b_nmr = p.bufs(nts, tag + "nmr")
        self.nl = [cx.sb([128, NS], F32, f"{tag}_nl{i}") for i in range(3)]; self.b_nl = p.bufs(3, tag + "nl")
        self.kw2 = self.ky = self.kr = self.kn = 0

    def run(self, gT, b_gT, w2r, xTv, rTv, oTv, gw, b_gw, vg, vb, b_vgb, t0, eps, b_x, b_r, b_xo,
            bank_y=(4, 5), bank_s=(6, 7)):
        cx, p = self.cx, self.cx.p
        KC, DC, NS, nts = self.KC, self.DC, self.NS, self.nts
        banks, bb = cx.banks, cx.bbufs
        Dtot = DC * 128
        pend = None

        def stats(c, ts, r2):
            sl = slice(ts * NS, (ts + 1) * NS)
            bs, bq = bank_s
            p.op("pe", [lambda e: e.matmul(banks[bs][:, 0:NS], lhsT=cx.onesB, rhs=self.rbf[r2], start=True, stop=True),
                        lambda e: e.matmul(banks[bq][:, 0:NS], lhsT=cx.onesB, rhs=self.sqbf[r2], start=True, stop=True)],
                 reads=[self.b_rbf[r2], self.b_sqbf[r2], cx.b_ones], writes=[bb[bs], bb[bq]])
            if c == 0:
                p.op("dve", lambda e: e.tensor_copy(self.asum[:, sl], banks[bs][:, 0:NS]), writes=[self.b_asum[ts], bb[bs]])
                p.op("dve", lambda e: e.tensor_copy(self.asq[:, sl], banks[bq][:, 0:NS]), writes=[self.b_asq[ts], bb[bq]])
            else:
                p.op("dve", lambda e: e.tensor_tensor(self.asum[:, sl], self.asum[:, sl], banks[bs][:, 0:NS], ALU.add),
                     reads=[self.b_asum[ts]], writes=[self.b_asum[ts], bb[bs]])
                p.op("dve", lambda e: e.tensor_tensor(self.asq[:, sl], self.asq[:, sl], banks[bq][:, 0:NS], ALU.add),
                     reads=[self.b_asq[ts]], writes=[self.b_asq[ts], bb[bq]])

        for c in range(DC):
            i2 = self.kw2 % self.NW2; self.kw2 += 1
            p.dma("pool", lambda e, i2=i2, c=c: cdma(e, self.w2t[i2], w2r[c]), self.b_w2t[i2], writes=self.w2wr[i2])
            for ts in range(nts):
                by = bank_y[self.ky % 2]; self.ky += 1
                sl = slice(ts * NS, (ts + 1) * NS)
                gsl = slice(t0 + ts * NS, t0 + (ts + 1) * NS)
                fns = []
                for h in range(KC):
                    fns.append(lambda e, i2=i2, h=h, sl=sl, by=by: e.matmul(
                        banks[by][:, 0:NS], lhsT=self.w2t[i2][:, h * 128:(h + 1) * 128], rhs=gT[:, h, sl],
                        start=(h == 0), stop=(h == KC - 1)))
                p.op("pe", fns, reads=[self.b_w2t[i2], b_gT[ts]], writes=[bb[by]])
                if pend is not None:
                    stats(*pend)
                r2 = self.kr % 2; self.kr += 1
                p.dma("sp", lambda e, r2=r2, c=c, gsl=gsl: e.dma_start(out=self.xres[r2], in_=xTv[:, c, gsl]),
                      self.b_xres[r2], reads=[b_x], writes=[self.b_xres[r2]])
                p.op("dve", lambda e, r2=r2, by=by, c=c: e.scalar_tensor_tensor(
                    self.rt[r2], banks[by][:, 0:NS], gw[:, c:c + 1], self.xres[r2], ALU.mult, ALU.add),
                    reads=[self.b_xres[r2], b_gw], writes=[self.b_rt[r2], bb[by]])
                p.op("act", lambda e, r2=r2: e.activation(out=self.rbf[r2], in_=self.rt[r2], func=AF.Copy), reads=[self.b_rt[r2]], writes=[self.b_rbf[r2]])
                p.op("act", lambda e, r2=r2: e.activation(out=self.sqbf[r2], in_=self.rt[r2], func=AF.Square),
                     reads=[self.b_rt[r2]], writes=[self.b_sqbf[r2]])
                p.dma("sp", lambda e, r2=r2, c=c, gsl=gsl: e.dma_start(out=rTv[:, c, gsl], in_=self.rt[r2]),
                      self.b_rt[r2], reads=[self.b_rt[r2]], writes=[b_r])
                pend = (c, ts, r2)
        stats(*pend)
        for ts in range(nts):
            sl = slice(ts * NS, (ts + 1) * NS)
            A, Q, M = self.asum[:, sl], self.asq[:, sl], self.nmr[:, sl]
            rd = [self.b_asum[ts], self.b_asq[ts], self.b_nmr[ts]]
            p.op("dve", lambda e, A=A: e.tensor_scalar(A, A, 1.0 / Dtot, None, ALU.mult), reads=rd, writes=rd)
            p.op("dve", lambda e, A=A, M=M: e.tensor_tensor(M, A, A, ALU.mult), reads=rd, writes=rd)
            p.op("dve", lambda e, Q=Q: e.tensor_scalar(Q, Q, 1.0 / Dtot, eps, ALU.mult, ALU.add), reads=rd, writes=rd)
            p.op("dve", lambda e, Q=Q, M=M: e.tensor_tensor(Q, Q, M, ALU.subtract), reads=rd, writes=rd)
            p.op("act", lambda e, Q=Q: e.activation(out=Q, in_=Q, func=AF.Sqrt), reads=rd, writes=rd)
            p.op("dve", lambda e, Q=Q: e.reciprocal(Q, Q), reads=rd, writes=rd)
            p.op("dve", lambda e, A=A, Q=Q, M=M: e.scalar_tensor_tensor(M, A, -1.0, Q, ALU.mult, ALU.mult), reads=rd, writes=rd)
        items = []

        def norm_item(ts, c):
            sl = slice(ts * NS, (ts + 1) * NS)
            gsl = slice(t0 + ts * NS, t0 + (ts + 1) * NS)
            rd = [self.b_asum[ts], self.b_asq[ts], self.b_nmr[ts]]
            n3 = self.kn % 3; self.kn += 1
            nl = self.nl[n3]; bnl = self.b_nl[n3]
            p.dma("sp", lambda e: e.dma_start(out=nl, in_=rTv[:, c, gsl]), bnl, reads=[b_r], writes=[bnl])
            p.op("dve", lambda e: e.tensor_tensor(nl, nl, self.asq[:, sl], ALU.mult), reads=[bnl] + rd, writes=[bnl])
            p.op("dve", lambda e: e.tensor_tensor(nl, nl, self.nmr[:, sl], ALU.add), reads=[bnl] + rd, writes=[bnl])
            p.op("act", lambda e: e.activation(out=nl, in_=nl, func=AF.Identity, bias=vb[:, c:c + 1], scale=vg[:, c:c + 1]),
                 reads=[bnl, b_vgb], writes=[bnl])
            p.dma("act", lambda e: e.dma_start(out=oTv[:, c, gsl], in_=nl), bnl, reads=[bnl], writes=[b_xo])

        for ts in range(nts):
            for c in range(DC):
                items.append(lambda ts=ts, c=c: norm_item(ts, c))
        return items


def emit_modulate(cx, xTv, xbf, b_xbf, s1, sh, b_vec, xin, b_xin, kx, t0, nts, NS, DC, b_x):
    p = cx.p
    for ts in range(nts):
        for c in range(DC):
            i3 = kx[0] % 3; kx[0] += 1
            sl = slice(t0 + ts * NS, t0 + (ts + 1) * NS)
            p.dma("sp", lambda e, i3=i3, c=c, sl=sl: e.dma_start(out=xin[i3], in_=xTv[:, c, sl]),
                  b_xin[i3], reads=[b_x], writes=[b_xin[i3]])
            p.op("act", lambda e, i3=i3, c=c, ts=ts: e.activation(
                out=xbf[:, c, ts * NS:(ts + 1) * NS], in_=xin[i3], func=AF.Identity,
                bias=sh[:, c:c + 1], scale=s1[:, c:c + 1]),
                reads=[b_xin[i3], b_vec], writes=[b_xbf[ts]])


def emit_ffn(cx, xT, w1r, w2r, vecs, rT, xoT, b_x, b_r, b_xo, D, DFF, NT, T, resw, alpha, ln_eps, tag="f"):
    nc, p = cx.nc, cx.p
    DC, HC = D // 128, DFF // 128
    NS = min(512, T)
    assert T % NS == 0 and NT % T == 0
    nts = T // NS
    vec = cx.sb([128, 5 * DC], F32, tag + "_vec"); b_vec = p.buf(tag + "vec")
    der = cx.sb([128, 2 * DC], F32, tag + "_der")
    p.dma("sp", lambda e: e.dma_start(out=vec, in_=vecs), b_vec, writes=[b_vec])
    p.op("dve", lambda e: e.tensor_scalar(der[:, 0:DC], vec[:, 0:DC], 1.0, None, ALU.add), reads=[b_vec], writes=[b_vec])
    p.op("dve", lambda e: e.tensor_scalar(der[:, DC:2 * DC], vec[:, 2 * DC:3 * DC], 1.0, resw / alpha, ALU.add, ALU.mult),
         reads=[b_vec], writes=[b_vec])
    xin = [cx.sb([128, NS], F32, f"{tag}_xin{i}") for i in range(3)]; b_xin = p.bufs(3, tag + "xin")
    xbf = cx.sb([128, DC, T], BF16, tag + "_xbf"); b_xbf = p.bufs(nts, tag + "xbf")
    NW1, NW2 = 5, 3
    s1, s2 = DC * 256, HC * 128
    wreg = cx.sb([128, max(NW1 * s1, NW2 * s2)], BF16, tag + "_wreg")
    w1t = [wreg[:, i * s1:(i + 1) * s1] for i in range(NW1)]; b_w1t = p.bufs(NW1, tag + "w1t")
    w2v = [wreg[:, k * s2:(k + 1) * s2] for k in range(NW2)]; b_w2v = p.bufs(NW2, tag + "w2t")
    ov = lambda i, k: i * s1 < (k + 1) * s2 and k * s2 < (i + 1) * s1
    w1wr = [[b_w1t[i]] + [b_w2v[k] for k in range(NW2) if ov(i, k)] for i in range(NW1)]
    w2wr = [[b_w2v[k]] + [b_w1t[i] for i in range(NW1) if ov(i, k)] for k in range(NW2)]
    gT = cx.sb([128, HC, T], BF16, tag + "_gT"); b_gT = p.bufs(nts, tag + "gT")
    sa = [cx.sb([128, NS], F32, f"{tag}_sa{i}") for i in range(2)]; b_sa = p.bufs(2, tag + "sa")
    stc = StageC(cx, HC, DC, T, NS, tag + "C", w2ext=(w2v, b_w2v, w2wr))
    xTv = xT.rearrange("(c p) t -> p c t", p=128)
    rTv = rT.rearrange("(c p) t -> p c t", p=128)
    oTv = xoT.rearrange("(c p) t -> p c t", p=128)
    kx = [0]
    ks = kw1 = kb = 0
    bank_a, bank_u = (0, 1), (2, 3)
    ntt = NT // T
    emit_modulate(cx, xTv, xbf, b_xbf, der[:, 0:DC], vec[:, DC:2 * DC], b_vec, xin, b_xin, kx, 0, nts, NS, DC, b_x)
    pending = []
    for tt in range(ntt):
        t0 = tt * T
        for j in range(HC):
            for _ in range(2):
                if pending:
                    pending.pop(0)()
            i2 = kw1 % NW1; kw1 += 1
            p.dma("pool", lambda e, i2=i2, j=j: cdma(e, w1t[i2], w1r[j]), b_w1t[i2], writes=w1wr[i2])
            for ts in range(nts):
                ba, bu = bank_a[kb % 2], bank_u[kb % 2]; kb += 1
                sl = slice(ts * NS, (ts + 1) * NS)
                fns = []
                for c in range(DC):
                    fns.append(lambda e, i2=i2, c=c, sl=sl, ba=ba: e.matmul(
                        cx.banks[ba][:, 0:NS], lhsT=w1t[i2][:, c * 256:c * 256 + 128], rhs=xbf[:, c, sl],
                        start=(c == 0), stop=(c == DC - 1)))
                p.op("pe", fns, reads=[b_w1t[i2], b_xbf[ts]], writes=[cx.bbufs[ba]])
                fns = []
                for c in range(DC):
                    fns.append(lambda e, i2=i2, c=c, sl=sl, bu=bu: e.matmul(
                        cx.banks[bu][:, 0:NS], lhsT=w1t[i2][:, c * 256 + 128:c * 256 + 256], rhs=xbf[:, c, sl],
                        start=(c == 0), stop=(c == DC - 1)))
                p.op("pe", fns, reads=[b_w1t[i2], b_xbf[ts]], writes=[cx.bbufs[bu]])
                s2 = ks % 2; ks += 1
                p.op("act", lambda e, s2=s2, ba=ba: e.activation(out=sa[s2], in_=cx.banks[ba][:, 0:NS], func=AF.Silu),
                     writes=[b_sa[s2], cx.bbufs[ba]])
                p.op("dve", lambda e, s2=s2, bu=bu, j=j, sl=sl: e.tensor_tensor(gT[:, j, sl], sa[s2], cx.banks[bu][:, 0:NS], ALU.mult),
                     reads=[b_sa[s2]], writes=[b_gT[ts], cx.bbufs[bu]])
        if tt + 1 < ntt:
            emit_modulate(cx, xTv, xbf, b_xbf, der[:, 0:DC], vec[:, DC:2 * DC], b_vec, xin, b_xin, kx, t0 + T, nts, NS, DC, b_x)
        while pending:
            pending.pop(0)()
        pending = stc.run(gT, b_gT, w2r, xTv, rTv, oTv, der[:, DC:2 * DC], b_vec, vec[:, 3 * DC:4 * DC], vec[:, 4 * DC:5 * DC], b_vec,
                          t0, ln_eps / (alpha * alpha), b_x, b_r, b_xo)
    while pending:
        pending.pop(0)()


def emit_inproj(cx, xT, vecs, wfm, wtm, qkT, vsb, mqkT, vml, oT, gTd, b_x, b_out, D, NT, qscale, tag="ip"):
    nc, p = cx.nc, cx.p
    DC = D // 128
    NS = min(512, NT)
    nts = NT // NS
    vec = cx.sb([128, 2 * DC], F32, tag + "_vec"); b_vec = p.buf(tag + "vec")
    s1 = cx.sb([128, DC], F32, tag + "_s1")
    p.dma("sp", lambda e: e.dma_start(out=vec, in_=vecs), b_vec, writes=[b_vec])
    p.op("dve", lambda e: e.tensor_scalar(s1, vec[:, 0:DC], 1.0, None, ALU.add), reads=[b_vec], writes=[b_vec])
    xin = [cx.sb([128, NS], F32, f"{tag}_xin{i}") for i in range(3)]; b_xin = p.bufs(3, tag + "xin")
    xbf = cx.sb([128, DC, NT], BF16, tag + "_xbf"); b_xbf = p.bufs(nts, tag + "xbf")
    xTv = xT.rearrange("(c p) t -> p c t", p=128)
    emit_modulate(cx, xTv, xbf, b_xbf, s1, vec[:, DC:2 * DC], b_vec, xin, b_xin, [0], 0, nts, NS, DC, b_x)
    wf = [cx.sb([128, DC * 128], BF16, f"{tag}_wf{i}") for i in range(3)]; b_wf = p.bufs(3, tag + "wf")
    obf = [cx.sb([128, NS], BF16, f"{tag}_obf{i}") for i in range(3)]; b_obf = p.bufs(3, tag + "obf")
    of32 = [cx.sb([128, NS], F32, f"{tag}_of{i}") for i in range(3)]; b_of = p.bufs(3, tag + "of")
    kbk = ko = 0
    for j in range(33):
        i3 = j % 3
        p.dma("pool", lambda e, i3=i3, j=j: cdma(e, wf[i3], wfm[j]), b_wf[i3], writes=[b_wf[i3]])
        for ts in range(nts):
            bk = kbk % 4; kbk += 1
            sl = slice(ts * NS, (ts + 1) * NS)
            fns = []
            for c in range(DC):
                fns.append(lambda e, i3=i3, c=c, sl=sl, bk=bk: e.matmul(
                    cx.banks[bk][:, 0:NS], lhsT=wf[i3][:, c * 128:(c + 1) * 128], rhs=xbf[:, c, sl],
                    start=(c == 0), stop=(c == DC - 1)))
            p.op("pe", fns, reads=[b_wf[i3], b_xbf[ts]], writes=[cx.bbufs[bk]])
            o3 = ko % 3; ko += 1
            eng = "act" if (ko % 2 == 0) else "dve"
            src = cx.banks[bk][:, 0:NS]
            if j < 16:
                dst, bd, ddst = obf[o3], b_obf[o3], qkT[j][:, sl]
                sc = qscale if j < 8 else 1.0
            elif j < 24:
                dst, bd, ddst, sc = of32[o3], b_of[o3], mqkT[j - 16][:, sl], 1.0
            elif j < 32:
                dst, bd, ddst, sc = of32[o3], b_of[o3], oT[j - 24][:, sl], 1.0
            else:
                dst, bd, ddst, sc = of32[o3][0:8, :], b_of[o3], gTd[:, sl], 1.0
                src = cx.banks[bk][0:8, 0:NS]
            if eng == "act":
                p.op("act", lambda e, dst=dst, src=src, sc=sc: e.activation(out=dst, in_=src, func=AF.Copy, scale=sc),
                     writes=[bd, cx.bbufs[bk]])
            else:
                p.op("dve", lambda e, dst=dst, src=src, sc=sc: e.tensor_scalar(dst, src, sc, None, ALU.mult),
                     writes=[bd, cx.bbufs[bk]])
            p.dma("sp", lambda e, dst=dst, ddst=ddst: e.dma_start(out=ddst, in_=dst), bd, reads=[bd], writes=[b_out])
    wt = [cx.sb([128, DC * 512], BF16, f"{tag}_wt{i}") for i in range(2)]; b_wt = p.bufs(2, tag + "wt")
    otm = [cx.sb([128, 512], BF16, f"{tag}_otm{i}") for i in range(3)]; b_otm = p.bufs(3, tag + "otm")
    for s in range(4):
        i2 = s % 2
        p.dma("pool", lambda e, i2=i2, s=s: cdma(e, wt[i2], wtm[s]), b_wt[i2], writes=[b_wt[i2]])
        dd = vsb if s < 2 else vml
        for tc in range(NT // 128):
            bk = 4 + kbk % 4; kbk += 1
            fns = []
            for c in range(DC):
                fns.append(lambda e, i2=i2, c=c, tc=tc, bk=bk: e.matmul(
                    cx.banks[bk], lhsT=xbf[:, c, tc * 128:(tc + 1) * 128], rhs=wt[i2][:, c * 512:(c + 1) * 512],
                    start=(c == 0), stop=(c == DC - 1)))
            p.op("pe", fns, reads=[b_wt[i2]] + b_xbf, writes=[cx.bbufs[bk]])
            o3 = ko % 3; ko += 1
            eng = "act" if (ko % 2 == 0) else "dve"
            if eng == "act":
                p.op("act", lambda e, o3=o3, bk=bk: e.activation(out=otm[o3], in_=cx.banks[bk], func=AF.Copy),
                     writes=[b_otm[o3], cx.bbufs[bk]])
            else:
                p.op("dve", lambda e, o3=o3, bk=bk: e.tensor_copy(otm[o3], cx.banks[bk]), writes=[b_otm[o3], cx.bbufs[bk]])
            p.dma("sp", lambda e, o3=o3, tc=tc, s=s, dd=dd: e.dma_start(
                out=dd[tc * 128:(tc + 1) * 128, (s % 2) * 512:(s % 2 + 1) * 512], in_=otm[o3]),
                b_otm[o3], reads=[b_otm[o3]], writes=[b_out])


def emit_sb_attn(cx, qT_d, kT_d, v_d, yT_d, b_in, b_out, S, bigQ, bigK, bigV, b_bigQ, b_bigK, b_bigV, tag="sb"):
    nc, p = cx.nc, cx.p
    NB, NQ = S // 128, S // 512
    Ui = cx.sb([128, 128], BF16, tag + "_Ui"); Ls = cx.sb([128, 128], BF16, tag + "_Ls")
    b_c = p.buf(tag + "const")
    p.op("pool", lambda e: e.memset(Ui, 1.0), writes=[b_c])
    p.op("pool", lambda e: e.memset(Ls, 1.0), writes=[b_c])
    p.op("pool", lambda e: e.affine_select(Ui, Ui, [[-1, 128]], ALU.is_ge, 0.0, base=0, channel_multiplier=1),
         reads=[b_c], writes=[b_c])
    p.op("pool", lambda e: e.affine_select(Ls, Ls, [[1, 128]], ALU.is_gt, 0.0, base=0, channel_multiplier=-1),
         reads=[b_c], writes=[b_c])
    npc = max(1, S // 4096)
    pc = S // npc
    for i in range(npc):
        sl = slice(i * pc, (i + 1) * pc)
        p.dma("sp", lambda e, sl=sl: e.dma_start(out=bigQ[:, sl], in_=qT_d[:, sl]), b_bigQ, reads=[b_in], writes=[b_bigQ])
        p.dma("sp", lambda e, sl=sl: e.dma_start(out=bigK[:, sl], in_=kT_d[:, sl]), b_bigK, reads=[b_in], writes=[b_bigK])
    v_v = v_d.rearrange("(b p) d -> p b d", p=128)
    nvb = max(1, NB // 32)
    for i in range(nvb):
        sl = slice(i * (NB // nvb), (i + 1) * (NB // nvb))
        p.dma("sp", lambda e, sl=sl: e.dma_start(out=bigV[:, sl, :], in_=v_v[:, sl, :]), b_bigV, reads=[b_in], writes=[b_bigV])
    NR = 4
    e32 = [cx.sb([128, 512], F32, f"{tag}_e{i}") for i in range(NR)]; b_e = p.bufs(NR, tag + "e")
    spb = [cx.sb([128, 512], BF16, f"{tag}_spb{i}") for i in range(NR)]; b_spb = p.bufs(NR, tag + "spb")
    t32 = [cx.sb([128, 512], F32, f"{tag}_t{i}") for i in range(NR)]; b_t = p.bufs(NR, tag + "t")
    ab = [cx.sb([128, 512], BF16, f"{tag}_ab{i}") for i in range(NR)]; b_ab = p.bufs(NR, tag + "ab")
    ot = [cx.sb([128, 512], F32, f"{tag}_ot{i}") for i in range(2)]; b_ot = p.bufs(2, tag + "ot")
    ZB = (0, 1, 2)
    RB = 3
    OB = (4, 5)
    banks, bb = cx.banks, cx.bbufs
    steps = []
    for tq in range(NQ):
        kbs = list(range(4 * tq + 3, -1, -1))
        for n, kb in enumerate(kbs):
            o = kb - 4 * tq
            steps.append(dict(tq=tq, kb=kb, first=(n == 0), last=(n == len(kbs) - 1), diag=(o >= 0), cs=max(o, 0) * 128))
    N = len(steps)

    def QK(i):
        s = steps[i]; z = ZB[i % 3]; cs = s["cs"]; tq, kb = s["tq"], s["kb"]
        p.op("pe", lambda e: e.matmul(banks[z][:, cs:512], lhsT=bigK[:, kb * 128:(kb + 1) * 128],
                                      rhs=bigQ[:, tq * 512 + cs:(tq + 1) * 512], start=True, stop=True),
             reads=[b_bigK, b_bigQ], writes=[bb[z]])

    def A(i):
        s = steps[i]; z = ZB[i % 3]; r = i % NR; cs = s["cs"]
        p.op("act", lambda e: e.activation(out=e32[r][:, cs:], in_=banks[z][:, cs:512], func=AF.Exp),
             writes=[b_e[r], bb[z]])
        p.op("act", lambda e: e.activation(out=spb[r][:, cs:], in_=e32[r][:, cs:], func=AF.Ln, bias=1.0),
             reads=[b_e[r]], writes=[b_spb[r]])
        if s["diag"]:
            p.op("pool", lambda e: e.affine_select(spb[r][:, cs:], spb[r][:, cs:], [[1, 512 - cs]], ALU.is_gt, 0.0,
                                                   base=0, channel_multiplier=-1),
                 reads=[b_spb[r]], writes=[b_spb[r]])

    def Pa(i):
        s = steps[i]; r = i % NR; cs = s["cs"]
        p.op("pe", lambda e: e.matmul(banks[RB][:, cs:512], lhsT=Ui, rhs=spb[r][:, cs:], start=s["first"], stop=False,
                                      skip_group_check=True),
             reads=[b_spb[r], b_c], writes=[bb[RB]])

    def B(i):
        s = steps[i]; r = i % NR; cs = s["cs"]
        p.op("act", lambda e: e.activation(out=t32[r][:, cs:], in_=banks[RB][:, cs:512], func=AF.Exp, scale=-1.0),
             writes=[b_t[r], bb[RB]])
        p.op("dve", lambda e: e.tensor_tensor(ab[r][:, cs:], e32[r][:, cs:], t32[r][:, cs:], ALU.mult),
             reads=[b_e[r], b_t[r]], writes=[b_ab[r]])
        if s["diag"]:
            p.op("pool", lambda e: e.affine_select(ab[r][:, cs:], ab[r][:, cs:], [[1, 512 - cs]], ALU.is_gt, 0.0,
                                                   base=0, channel_multiplier=-1),
                 reads=[b_ab[r]], writes=[b_ab[r]])

    def Pb(i):
        s = steps[i]; r = i % NR; cs = s["cs"]
        p.op("pe", lambda e: e.matmul(banks[RB][:, cs:512], lhsT=Ls, rhs=spb[r][:, cs:], start=False, stop=s["last"],
                                      skip_group_check=True),
             reads=[b_spb[r], b_c], writes=[bb[RB]])

    def AV(i):
        s = steps[i]; r = i % NR; cs = s["cs"]; tq, kb = s["tq"], s["kb"]
        ob = OB[tq % 2]
        p.op("pe", lambda e: e.matmul(banks[ob][:, cs:512], lhsT=bigV[:, kb, :], rhs=ab[r][:, cs:], start=s["first"],
                                      stop=s["last"], skip_group_check=True),
             reads=[b_bigV, b_ab[r]], writes=[bb[ob]])
        if s["last"]:
            o2 = tq % 2
            p.op("dve", lambda e: e.tensor_copy(ot[o2], banks[ob]), writes=[b_ot[o2], bb[ob]])
            p.dma("sp", lambda e: e.dma_start(out=yT_d[:, tq * 512:(tq + 1) * 512], in_=ot[o2]),
                  b_ot[o2], reads=[b_ot[o2]], writes=[b_out])

    QK(0)
    if N > 1:
        QK(1)
    A(0)
    for i in range(N):
        if i + 1 < N:
            A(i + 1)
        Pa(i)
        if i + 2 < N:
            QK(i + 2)
        B(i)
        Pb(i)
        if i >= 1:
            AV(i - 1)
    AV(N - 1)


def emit_mlstm(cx, uqT_d, ukT_d, cw_d, v_d, ig_d, fg_d, gb_d, hT_d, b_in, b_out, S,
               bigQ, bigK, bigV, b_bigQ, b_bigK, b_bigV, bank_bf, bbuf_bf, qscale, tag="ml"):
    nc, p = cx.nc, cx.p
    NCH = S // 128
    assert NCH <= 128
    banks, bb = cx.banks, cx.bbufs
    onesB = cx.onesB
    TriF = cx.sb([128, 128], F32, tag + "_TriF")
    identF = cx.sb([128, 128], F32, tag + "_idF")
    identB = cx.sb([128, 128], BF16, tag + "_idB")
    b_c = p.buf(tag + "const")
    for t_ in (TriF, identF, identB):
        p.op("pool", lambda e, t_=t_: e.memset(t_, 1.0), writes=[b_c])
    p.op("pool", lambda e: e.affine_select(TriF, TriF, [[1, 128]], ALU.is_ge, 0.0, base=0, channel_multiplier=-1),
         reads=[b_c], writes=[b_c])
    p.op("pool", lambda e: e.affine_select(identF, identF, [[1, 128]], ALU.is_equal, 0.0, base=0, channel_multiplier=-1),
         reads=[b_c], writes=[b_c])
    p.op("pool", lambda e: e.affine_select(identB, identB, [[1, 128]], ALU.is_equal, 0.0, base=0, channel_multiplier=-1),
         reads=[b_c], writes=[b_c])
    cw = cx.sb([128, 10], F32, tag + "_cw"); gb = cx.sb([128, 2], F32, tag + "_gb"); ngb = cx.sb([128, 1], F32, tag + "_ngb")
    b_small = p.buf(tag + "small")
    p.dma("sp", lambda e: e.dma_start(out=cw, in_=cw_d), b_small, reads=[b_in], writes=[b_small])
    p.dma("sp", lambda e: e.dma_start(out=gb, in_=gb_d), b_small, reads=[b_in], writes=[b_small])
    p.op("dve", lambda e: e.tensor_scalar(ngb, gb[:, 1:2], -1.0, None, ALU.mult), reads=[b_small], writes=[b_small])
    Gi = cx.sb([128, 128], F32, tag + "_Gi"); Gf = cx.sb([128, 128], F32, tag + "_Gf")
    b_G = p.buf(tag + "G")
    if NCH < 128:
        p.op("pool", lambda e: e.memset(Gi, 0.0), writes=[b_G])
        p.op("pool", lambda e: e.memset(Gf, 0.0), writes=[b_G])
    p.dma("sp", lambda e: e.dma_start(out=Gi[0:NCH, :], in_=ig_d), b_G, reads=[b_in], writes=[b_G])
    p.dma("sp", lambda e: e.dma_start(out=Gf[0:NCH, :], in_=fg_d), b_G, reads=[b_in], writes=[b_G])
    p.op("act", lambda e: e.activation(out=Gf, in_=Gf, func=AF.Exp, bias=ngb[:, 0:1], scale=-1.0), reads=[b_G, b_small], writes=[b_G])
    p.op("act", lambda e: e.activation(out=Gf, in_=Gf, func=AF.Ln, bias=1.0), reads=[b_G], writes=[b_G])
    p.op("dve", lambda e: e.tensor_scalar(Gf, Gf, -1.0, None, ALU.mult), reads=[b_G], writes=[b_G])
    p.op("dve", lambda e: e.tensor_scalar(Gi, Gi, gb[:, 0:1], None, ALU.add), reads=[b_G, b_small], writes=[b_G])
    FT = cx.sb([128, 128], F32, tag + "_FT"); IT = cx.sb([128, 128], F32, tag + "_IT")
    b_FT = p.buf(tag + "FT")
    GB = 6
    p.op("pe", lambda e: e.transpose(banks[GB][:, 0:128], Gf, identF), reads=[b_G, b_c], writes=[bb[GB]])
    p.op("pe", lambda e: e.transpose(banks[GB][:, 128:256], Gi, identF), reads=[b_G, b_c], writes=[bb[GB]])
    p.op("dve", lambda e: e.tensor_copy(FT, banks[GB][:, 0:128]), writes=[b_FT, bb[GB]])
    p.op("dve", lambda e: e.tensor_copy(IT, banks[GB][:, 128:256]), writes=[b_FT, bb[GB]])
    p.op("pe", lambda e: e.matmul(banks[GB][:, 256:384], lhsT=TriF, rhs=FT, start=True, stop=True), reads=[b_FT, b_c], writes=[bb[GB]])
    p.op("pe", lambda e: e.matmul(banks[GB][:, 384:512], lhsT=cx.ones, rhs=FT, start=True, stop=True),
         reads=[b_FT, b_c, cx.b_ones], writes=[bb[GB]])
    biasS = cx.sb([128, 128], F32, tag + "_biasS"); Wcol = cx.sb([128, 128], F32, tag + "_Wcol")
    decay = cx.sb([128, 128], F32, tag + "_decay")
    b_gs = p.buf(tag + "gs")
    p.op("dve", lambda e: e.tensor_tensor(biasS, IT, banks[GB][:, 256:384], ALU.subtract), reads=[b_FT], writes=[b_gs, bb[GB]])
    p.op("dve", lambda e: e.tensor_tensor(Wcol, biasS, banks[GB][:, 384:512], ALU.add), reads=[b_gs], writes=[b_gs, bb[GB]])
    p.op("act", lambda e: e.activation(out=Wcol, in_=Wcol, func=AF.Exp), reads=[b_gs], writes=[b_gs])
    p.op("act", lambda e: e.activation(out=decay, in_=banks[GB][:, 384:512], func=AF.Exp), writes=[b_gs, bb[GB]])
    PC = min(2048, S)
    stg = [cx.sb([128, PC + 3], F32, f"{tag}_stg{i}") for i in range(2)]; b_stg = p.bufs(2, tag + "stg")
    acc = [cx.sb([128, PC], F32, f"{tag}_acc{i}") for i in range(2)]; b_acc = p.bufs(2, tag + "acc")
    k2 = 0
    for which, (u_d, big, b_big) in enumerate(((uqT_d, bigQ, b_bigQ), (ukT_d, bigK, b_bigK))):
        wo = which * 4
        for pi in range(S // PC):
            i2 = k2 % 2; k2 += 1
            t0 = pi * PC
            if pi == 0:
                p.op("pool", lambda e, i2=i2: e.memset(stg[i2][:, 0:3], 0.0), writes=[b_stg[i2]])
                p.dma("sp", lambda e, i2=i2, u_d=u_d: e.dma_start(out=stg[i2][:, 3:], in_=u_d[:, 0:PC]),
                      b_stg[i2], reads=[b_in], writes=[b_stg[i2]])
            else:
                p.dma("sp", lambda e, i2=i2, u_d=u_d, t0=t0: e.dma_start(out=stg[i2], in_=u_d[:, t0 - 3:t0 + PC]),
                      b_stg[i2], reads=[b_in], writes=[b_stg[i2]])
            p.op("dve", lambda e, i2=i2, wo=wo, which=which: e.tensor_scalar(
                acc[i2], stg[i2][:, 3:3 + PC], cw[:, wo + 3:wo + 4], cw[:, 8 + which:9 + which], ALU.mult, ALU.add),
                reads=[b_stg[i2], b_small], writes=[b_acc[i2]])
            for kk in (2, 1, 0):
                p.op("dve", lambda e, i2=i2, wo=wo, kk=kk: e.scalar_tensor_tensor(
                    acc[i2], stg[i2][:, kk:kk + PC], cw[:, wo + kk:wo + kk + 1], acc[i2], ALU.mult, ALU.add),
                    reads=[b_stg[i2], b_small, b_acc[i2]], writes=[b_acc[i2]])
            if which == 0:
                p.op("act", lambda e, i2=i2: e.activation(out=acc[i2], in_=acc[i2], func=AF.Silu),
                     reads=[b_acc[i2]], writes=[b_acc[i2]])
                p.op("pool", lambda e, i2=i2, t0=t0, big=big: e.tensor_scalar(big[:, t0:t0 + PC], acc[i2], qscale, None, ALU.mult),
                     reads=[b_acc[i2]], writes=[b_big])
            else:
                p.op("act", lambda e, i2=i2, t0=t0, big=big: e.activation(out=big[:, t0:t0 + PC], in_=acc[i2], func=AF.Silu),
                     reads=[b_acc[i2]], writes=[b_big])
    v_v = v_d.rearrange("(b p) d -> p b d", p=128)
    nvb = max(1, NCH // 32)
    for i in range(nvb):
        sl = slice(i * (NCH // nvb), (i + 1) * (NCH // nvb))
        p.dma("sp", lambda e, sl=sl: e.dma_start(out=bigV[:, sl, :], in_=v_v[:, sl, :]), b_bigV, reads=[b_in], writes=[b_bigV])
    kw = [cx.sb([128, 128], BF16, f"{tag}_kw{i}") for i in range(2)]; b_kw = p.bufs(2, tag + "kw")
    FTri = [cx.sb([128, 128], F32, f"{tag}_FTri{i}") for i in range(2)]; b_FTri = p.bufs(2, tag + "FTri")
    DT = [cx.sb([128, 128], F32, f"{tag}_DT{i}") for i in range(2)]; b_DT = p.bufs(2, tag + "DT")
    PT = [cx.sb([128, 128], BF16, f"{tag}_PT{i}") for i in range(2)]; b_PT = p.bufs(2, tag + "PT")
    Gx = [cx.sb([128, 128], F32, f"{tag}_Gx{i}") for i in range(2)]; b_Gx = p.bufs(2, tag + "Gx")
    qg = [cx.sb([128, 128], BF16, f"{tag}_qg{i}") for i in range(2)]; b_qg = p.bufs(2, tag + "qg")
    CN = cx.sb([128, 256], F32, tag + "_CN"); b_CN = p.buf(tag + "CN")
    CNb = [cx.sb([128, 256], BF16, f"{tag}_CNb{i}") for i in range(2)]; b_CNb = p.bufs(2, tag + "CNb")
    dn = [cx.sb([128, 128], F32, f"{tag}_dn{i}") for i in range(2)]; b_dn = p.bufs(2, tag + "dn")
    hst = [cx.sb([128, 512], F32, f"{tag}_hst{i}") for i in range(2)]; b_hst = p.bufs(2, tag + "hst")
    SB_, NB_, CB_ = (0, 1), (2, 3), (4, 5)
    p.op("pool", lambda e: e.memset(CN, 0.0), writes=[b_CN])

    def pre(c):
        i2 = c % 2
        csl = slice(c * 128, (c + 1) * 128)
        p.op("pe", lambda e: e.transpose(bank_bf[:, 0:128], bigK[:, csl], identB), reads=[b_bigK, b_c], writes=[bbuf_bf])
        p.op("act", lambda e: e.activation(out=kw[i2], in_=bank_bf[:, 0:128], func=AF.Identity, scale=Wcol[:, c:c + 1]),
             reads=[b_gs], writes=[b_kw[i2], bbuf_bf])
        cb = CB_[i2]
        p.op("pe", [lambda e: e.matmul(banks[cb][:, 0:128], lhsT=kw[i2], rhs=bigV[:, c, :], start=True, stop=True),
                    lambda e: e.matmul(banks[cb][:, 128:256], lhsT=kw[i2], rhs=onesB, start=True, stop=True)],
             reads=[b_kw[i2], b_bigV, cx.b_ones], writes=[bb[cb]])
        p.op("dve", lambda e: e.tensor_scalar(FTri[i2], TriF, FT[:, c:c + 1], None, ALU.mult), reads=[b_c, b_FT], writes=[b_FTri[i2]])
        sb_ = SB_[i2]
        p.op("pe", [lambda e: e.matmul(banks[sb_][:, 0:128], lhsT=bigK[:, csl], rhs=bigQ[:, csl], start=True, stop=True),
                    lambda e: e.matmul(banks[sb_][:, 128:256], lhsT=cx.ones, rhs=FTri[i2], start=True, stop=True)],
             reads=[b_bigK, b_bigQ, b_FTri[i2], cx.b_ones], writes=[bb[sb_]])
        p.op("act", lambda e: e.activation(out=DT[i2], in_=banks[sb_][:, 128:256], func=AF.Exp, bias=biasS[:, c:c + 1]),
             reads=[b_gs], writes=[b_DT[i2], bb[sb_]])
        p.op("act", lambda e: e.activation(out=Gx[i2], in_=banks[sb_][:, 128:256], func=AF.Exp),
             writes=[b_Gx[i2], bb[sb_]])
        p.op("pool", lambda e: e.affine_select(DT[i2], DT[i2], [[1, 128]], ALU.is_ge, 0.0, base=0, channel_multiplier=-1),
             reads=[b_DT[i2]], writes=[b_DT[i2]])
        p.op("dve", lambda e: e.tensor_tensor(PT[i2], banks[sb_][:, 0:128], DT[i2], ALU.mult),
             reads=[b_DT[i2]], writes=[b_PT[i2], bb[sb_]])
        p.op("pool", lambda e: e.tensor_tensor(qg[i2], bigQ[:, csl], Gx[i2], ALU.mult),
             reads=[b_bigQ, b_Gx[i2]], writes=[b_qg[i2]])

    def post(c):
        i2 = c % 2
        nb_ = NB_[i2]
        sprev = (c - 1) % 2
        fns = [lambda e: e.matmul(banks[nb_][:, 0:128], lhsT=bigV[:, c, :], rhs=PT[i2], start=True, stop=(c == 0))]
        if c > 0:
            fns.append(lambda e: e.matmul(banks[nb_][:, 0:128], lhsT=CNb[sprev][:, 0:128], rhs=qg[i2], start=False, stop=True))
        fns.append(lambda e: e.matmul(banks[nb_][:, 128:256], lhsT=onesB, rhs=PT[i2], start=True, stop=(c == 0)))
        if c > 0:
            fns.append(lambda e: e.matmul(banks[nb_][:, 128:256], lhsT=CNb[sprev][:, 128:256], rhs=qg[i2], start=False, stop=True))
        p.op("pe", fns, reads=[b_bigV, b_PT[i2], b_qg[i2], b_CNb[sprev], cx.b_ones], writes=[bb[nb_]])
        p.op("act", lambda e: e.activation(out=dn[i2], in_=banks[nb_][:, 128:256], func=AF.Abs),
             writes=[b_dn[i2], bb[nb_]])
        p.op("dve", lambda e: e.tensor_scalar(dn[i2], dn[i2], 1.0, None, ALU.max), reads=[b_dn[i2]], writes=[b_dn[i2]])
        p.op("dve", lambda e: e.reciprocal(dn[i2], dn[i2]), reads=[b_dn[i2]], writes=[b_dn[i2]])
        h2 = (c // 4) % 2
        hs = (c % 4) * 128
        p.op("dve", lambda e: e.tensor_tensor(hst[h2][:, hs:hs + 128], banks[nb_][:, 0:128], dn[i2], ALU.mult),
             reads=[b_dn[i2]], writes=[b_hst[h2], bb[nb_]])
        if c % 4 == 3 or c == NCH - 1:
            g0 = (c // 4) * 512
            n = (c % 4 + 1) * 128
            p.dma("sp", lambda e: e.dma_start(out=hT_d[:, g0:g0 + n], in_=hst[h2][:, 0:n]), b_hst[h2],
                  reads=[b_hst[h2]], writes=[b_out])
        cb = CB_[i2]
        p.op("dve", lambda e: e.scalar_tensor_tensor(CN, CN, decay[:, c:c + 1], banks[cb][:, 0:256], ALU.mult, ALU.add),
             reads=[b_CN, b_gs], writes=[b_CN, bb[cb]])
        p.op("act", lambda e: e.activation(out=CNb[i2], in_=CN, func=AF.Copy), reads=[b_CN], writes=[b_CNb[i2]])

    pre(0)
    for c in range(NCH):
        if c + 1 < NCH:
            pre(c + 1)
        post(c)


def emit_outproj(cx, yT_all, oT, xT, woutr, vecs, ngv, ynT, rT, xoT, b_in, b_yn, b_r, b_xo, D, NT, T, ln_eps, alpha, tag="op"):
    nc, p = cx.nc, cx.p
    DC = D // 128
    NS = min(512, T)
    nts = T // NS
    banks, bb = cx.banks, cx.bbufs
    vec = cx.sb([128, 3 * DC], F32, tag + "_vec"); b_vec = p.buf(tag + "vec")
    gw = cx.sb([128, DC], F32, tag + "_gw")
    ng = cx.sb([128, 32], F32, tag + "_ng")
    p.dma("sp", lambda e: e.dma_start(out=vec, in_=vecs), b_vec, writes=[b_vec])
    p.dma("sp", lambda e: e.dma_start(out=ng, in_=ngv), b_vec, writes=[b_vec])
    p.op("dve", lambda e: e.tensor_scalar(gw, vec[:, 0:DC], 1.0, 1.0 / alpha, ALU.add, ALU.mult), reads=[b_vec], writes=[b_vec])
    ybf = cx.sb([128, 16, T], BF16, tag + "_ybf"); b_ybf = p.bufs(nts, tag + "ybf")
    NY = 4
    yin = [cx.sb([128, NS], F32, f"{tag}_yin{i}") for i in range(NY)]; b_yin = p.bufs(NY, tag + "yin")
    yb = [cx.sb([128, NS], BF16, f"{tag}_yb{i}") for i in range(NY)]; b_yb = p.bufs(NY, tag + "yb")
    ysq = [cx.sb([128, NS], BF16, f"{tag}_ysq{i}") for i in range(NY)]; b_ysq = p.bufs(NY, tag + "ysq")
    oin = [cx.sb([128, NS], F32, f"{tag}_oin{i}") for i in range(3)]; b_oin = p.bufs(3, tag + "oin")
    st = [[cx.sb([128, NS], F32, f"{tag}_st{j}_{i}") for i in range(2)] for j in range(3)]
    b_st = p.bufs(2, tag + "st")
    stc = StageC(cx, 16, DC, T, NS, tag + "C", NW2=6)
    yv = yT_all.rearrange("(c p) t -> p c t", p=128)
    ov = oT.rearrange("(c p) t -> p c t", p=128)
    xTv = xT.rearrange("(c p) t -> p c t", p=128)
    rTv = rT.rearrange("(c p) t -> p c t", p=128)
    oTv = xoT.rearrange("(c p) t -> p c t", p=128)
    groups = [(c, 1) for c in range(8)] + [(8 + 2 * h, 2) for h in range(4)]
    ky = ko = kg = 0
    SBK = (0, 1, 2, 3)
    for tt in range(NT // T):
        t0 = tt * T
        for ts in range(nts):
            gsl = slice(t0 + ts * NS, t0 + (ts + 1) * NS)
            sl = slice(ts * NS, (ts + 1) * NS)
            for (c0, GC) in groups:
                g2 = kg % 2; kg += 1
                bs, bq = SBK[2 * g2], SBK[2 * g2 + 1]
                mean, rstd, nmr = st[0][g2], st[1][g2], st[2][g2]
                Dg = GC * 128
                slots = []
                for cc in range(GC):
                    c = c0 + cc
                    i4 = ky % NY; ky += 1
                    slots.append(i4)
                    p.dma("sp", lambda e, i4=i4, c=c, gsl=gsl: e.dma_start(out=yin[i4], in_=yv[:, c, gsl]), b_yin[i4],
                          reads=[b_in], writes=[b_yin[i4]])
                    p.op("act", lambda e, i4=i4: e.activation(out=yb[i4], in_=yin[i4], func=AF.Copy), reads=[b_yin[i4]], writes=[b_yb[i4]])
                    p.op("act", lambda e, i4=i4: e.activation(out=ysq[i4], in_=yin[i4], func=AF.Square), reads=[b_yin[i4]], writes=[b_ysq[i4]])
                    p.op("pe", [lambda e, i4=i4, cc=cc, GC=GC, bs=bs: e.matmul(banks[bs][:, 0:NS], lhsT=cx.onesB, rhs=yb[i4], start=(cc == 0), stop=(cc == GC - 1)),
                                lambda e, i4=i4, cc=cc, GC=GC, bq=bq: e.matmul(banks[bq][:, 0:NS], lhsT=cx.onesB, rhs=ysq[i4], start=(cc == 0), stop=(cc == GC - 1))],
                         reads=[b_yb[i4], b_ysq[i4], cx.b_ones], writes=[bb[bs], bb[bq]])
                bst = b_st[g2]
                p.op("dve", lambda e, mean=mean, bs=bs, Dg=Dg: e.tensor_scalar(mean, banks[bs][:, 0:NS], 1.0 / Dg, None, ALU.mult),
                     writes=[bst, bb[bs]])
                p.op("dve", lambda e, nmr=nmr, mean=mean: e.tensor_tensor(nmr, mean, mean, ALU.mult), reads=[bst], writes=[bst])
                p.op("dve", lambda e, rstd=rstd, bq=bq, Dg=Dg: e.tensor_scalar(rstd, banks[bq][:, 0:NS], 1.0 / Dg, ln_eps, ALU.mult, ALU.add),
                     reads=[bst], writes=[bst, bb[bq]])
                p.op("dve", lambda e, rstd=rstd, nmr=nmr: e.tensor_tensor(rstd, rstd, nmr, ALU.subtract), reads=[bst], writes=[bst])
                p.op("act", lambda e, rstd=rstd: e.activation(out=rstd, in_=rstd, func=AF.Ln), reads=[bst], writes=[bst])
                p.op("act", lambda e, rstd=rstd: e.activation(out=rstd, in_=rstd, func=AF.Exp, scale=-0.5), reads=[bst], writes=[bst])
                p.op("dve", lambda e, nmr=nmr, mean=mean, rstd=rstd: e.scalar_tensor_tensor(nmr, mean, -1.0, rstd, ALU.mult, ALU.mult),
                     reads=[bst], writes=[bst])
                for cc in range(GC):
                    c = c0 + cc
                    i4 = slots[cc]
                    p.op("dve", lambda e, i4=i4, rstd=rstd: e.tensor_tensor(yin[i4], yin[i4], rstd, ALU.mult), reads=[b_yin[i4], bst], writes=[b_yin[i4]])
                    p.op("dve", lambda e, i4=i4, nmr=nmr: e.tensor_tensor(yin[i4], yin[i4], nmr, ALU.add), reads=[b_yin[i4], bst], writes=[b_yin[i4]])
                    if c < 8:
                        p.op("act", lambda e, i4=i4, c=c, sl=sl: e.activation(out=ybf[:, c, sl], in_=yin[i4], func=AF.Identity, scale=ng[:, c:c + 1]),
                             reads=[b_yin[i4], b_vec], writes=[b_ybf[ts]])
                    else:
                        o3 = ko % 3; ko += 1
                        p.dma("sp", lambda e, o3=o3, c=c, gsl=gsl: e.dma_start(out=oin[o3], in_=ov[:, c - 8, gsl]), b_oin[o3],
                              reads=[b_in], writes=[b_oin[o3]])
                        p.op("act", lambda e, o3=o3: e.activation(out=oin[o3], in_=oin[o3], func=AF.Sigmoid), reads=[b_oin[o3]], writes=[b_oin[o3]])
                        p.op("act", lambda e, i4=i4, c=c: e.activation(out=yin[i4], in_=yin[i4], func=AF.Identity, scale=ng[:, c:c + 1]),
                             reads=[b_yin[i4], b_vec], writes=[b_yin[i4]])
                        p.op("dve", lambda e, i4=i4, o3=o3, c=c, sl=sl: e.tensor_tensor(ybf[:, c, sl], yin[i4], oin[o3], ALU.mult),
                             reads=[b_yin[i4], b_oin[o3]], writes=[b_ybf[ts]])
        for it_ in stc.run(ybf, b_ybf, woutr, xTv, rTv, oTv, gw, b_vec, vec[:, DC:2 * DC], vec[:, 2 * DC:3 * DC], b_vec,
                           t0, ln_eps / (alpha * alpha), b_in, b_r, b_xo):
            it_()


def emit_mod(cx, cl, wr, br, outd, b_out, D, NCOL, nlayers, tag="md"):
    nc, p = cx.nc, cx.p
    DC = D // 128
    cs = cx.sb([128, DC], F32, tag + "_c"); cb = cx.sb([128, DC], BF16, tag + "_cb"); b_c = p.buf(tag + "c")
    p.dma("sp", lambda e: e.dma_start(out=cs, in_=cl), b_c, writes=[b_c])
    p.op("act", lambda e: e.activation(out=cb, in_=cs, func=AF.Silu), reads=[b_c], writes=[b_c])
    wt = [cx.sb([128, DC * NCOL], BF16, f"{tag}_w{i}") for i in range(2)]; b_wt = p.bufs(2, tag + "w")
    bt = [cx.sb([1, NCOL], F32, f"{tag}_b{i}") for i in range(2)]; b_bt = p.bufs(2, tag + "b")
    ob = [cx.sb([1, NCOL], F32, f"{tag}_o{i}") for i in range(2)]; b_ob = p.bufs(2, tag + "o")
    kb = 0
    for l in range(nlayers):
        i2 = l % 2
        p.dma("pool", lambda e, i2=i2, l=l: cdma(e, wt[i2], wr[l]), b_wt[i2], writes=[b_wt[i2]])
        p.dma("sp", lambda e, i2=i2, l=l: e.dma_start(out=bt[i2], in_=br[l]), b_bt[i2], writes=[b_bt[i2]])
        n0 = 0
        while n0 < NCOL:
            n = min(512, NCOL - n0)
            bk = kb % 2; kb += 1
            fns = []
            for c in range(DC):
                fns.append(lambda e, i2=i2, c=c, n0=n0, n=n, bk=bk: e.matmul(
                    cx.banks[bk][0:1, 0:n], lhsT=cb[:, c:c + 1], rhs=wt[i2][:, c * NCOL + n0:c * NCOL + n0 + n],
                    start=(c == 0), stop=(c == DC - 1)))
            p.op("pe", fns, reads=[b_c, b_wt[i2]], writes=[cx.bbufs[bk]])
            p.op("dve", lambda e, i2=i2, n0=n0, n=n, bk=bk: e.tensor_tensor(ob[i2][:, n0:n0 + n], cx.banks[bk][0:1, 0:n], bt[i2][:, n0:n0 + n], ALU.add),
                 reads=[b_bt[i2]], writes=[b_ob[i2], cx.bbufs[bk]])
            n0 += n
        p.dma("sp", lambda e, i2=i2, l=l: e.dma_start(out=outd[l], in_=ob[i2]), b_ob[i2], reads=[b_ob[i2]], writes=[b_out])


class Cfg:
    def __init__(self, D=2048, DFF=5632, S=16384, depth=2):
        self.D, self.DFF, self.S, self.depth = D, DFF, S, depth
        self.NT = S // NCORES
        self.T = min(1024, self.NT)
        self.alpha = (2 * depth) ** 0.25
        self.eps = 1e-5
        self.NMOD = 9 * D
        assert self.NMOD % NCORES == 0
        self.NCOL = self.NMOD // NCORES


_CACHE = {}


def _new_nc():
    return bass.Bass("TRN2", target_bir_lowering=False)


def _din(nc, name, shape, dt=F32):
    return nc.dram_tensor(name, list(shape), dt, kind="ExternalInput").ap()


def _dout(nc, name, shape, dt=F32):
    return nc.dram_tensor(name, list(shape), dt, kind="ExternalOutput").ap()


def _dscr(nc, name, shape, dt=F32):
    return nc.dram_tensor(name, list(shape), dt).ap()


def build_mod(cfg):
    nc = _new_nc(); p = Prog(nc); cx = Ctx(nc, p)
    DC = cfg.D // 128
    cl = _din(nc, "cl", [128, DC]); wr = _din(nc, "wr", [cfg.depth, 128, DC * cfg.NCOL]); br = _din(nc, "br", [cfg.depth, 1, cfg.NCOL])
    outd = _dout(nc, "mod", [cfg.depth, 1, cfg.NCOL])
    b_out = p.buf("out")
    emit_mod(cx, cl, wr, br, outd, b_out, cfg.D, cfg.NCOL, cfg.depth)
    p.wait_all("sp", [b_out]); p.emit()
    return nc


def build_ffn(cfg):
    nc = _new_nc(); p = Prog(nc); cx = Ctx(nc, p)
    D, DFF, NT = cfg.D, cfg.DFF, cfg.NT
    DC, HC = D // 128, DFF // 128
    xT = _din(nc, "xT", [D, NT]); w1r = _din(nc, "w1r", [HC, 128, DC * 256]); w2r = _din(nc, "w2r", [DC, 128, HC * 128])
    vecs = _din(nc, "vecs", [128, 5 * DC]); rT = _dscr(nc, "rT", [D, NT]); xoT = _dout(nc, "xoT", [D, NT])
    b_x, b_r, b_xo = p.bufs(3, "dram")
    emit_ffn(cx, xT, w1r, w2r, vecs, rT, xoT, b_x, b_r, b_xo, D, DFF, NT, cfg.T, 0.5, cfg.alpha, cfg.eps)
    p.wait_all("sp", [b_xo]); p.emit()
    return nc


def build_inproj(cfg):
    nc = _new_nc(); p = Prog(nc); cx = Ctx(nc, p)
    D, NT = cfg.D, cfg.NT
    DC = D // 128
    xT = _din(nc, "xT", [D, NT]); vecs = _din(nc, "vecs", [128, 2 * DC])
    wfm = _din(nc, "wfm", [33, 128, DC * 128]); wtm = _din(nc, "wtm", [4, 128, DC * 512])
    qkT = _dout(nc, "qkT", [16, 128, NT], BF16); vsb = _dout(nc, "vsb", [NT, 1024], BF16)
    mqkT = _dout(nc, "mqkT", [8, 128, NT]); vml = _dout(nc, "vml", [NT, 1024], BF16)
    oT = _dout(nc, "oT", [8, 128, NT]); gTd = _dout(nc, "gT", [8, NT])
    b_x, b_out = p.bufs(2, "dram")
    emit_inproj(cx, xT, vecs, wfm, wtm, qkT, vsb, mqkT, vml, oT, gTd, b_x, b_out, D, NT, 128 ** -0.5)
    p.wait_all("sp", [b_out]); p.emit()
    return nc


def build_attn(cfg):
    nc = _new_nc(); p = Prog(nc); cx = Ctx(nc, p, bf16_bank=7)
    S = cfg.S
    NCH = S // 128
    qT = _din(nc, "qT", [128, S], BF16); kT = _din(nc, "kT", [128, S], BF16); v = _din(nc, "v", [S, 128], BF16)
    uq = _din(nc, "uq", [128, S]); uk = _din(nc, "uk", [128, S]); cw = _din(nc, "cw", [128, 10])
    vm = _din(nc, "vm", [S, 128], BF16); ig = _din(nc, "ig", [NCH, 128]); fg = _din(nc, "fg", [NCH, 128]); gb = _din(nc, "gb", [128, 2])
    yT = _dout(nc, "yT", [128, S]); hT = _dout(nc, "hT", [128, S])
    bigQ = cx.sb([128, S], BF16, "bigQ"); bigK = cx.sb([128, S], BF16, "bigK"); bigV = cx.sb([128, S // 128, 128], BF16, "bigV")
    bQ, bK, bV, b_in, b_out = p.bufs(5, "x")
    emit_mlstm(cx, uq, uk, cw, vm, ig, fg, gb, hT, b_in, b_out, S, bigQ, bigK, bigV, bQ, bK, bV, cx.banks[7], cx.bbufs[7], 128 ** -0.5)
    emit_sb_attn(cx, qT, kT, v, yT, b_in, b_out, S, bigQ, bigK, bigV, bQ, bK, bV)
    p.wait_all("sp", [b_out]); p.emit()
    return nc


def build_outproj(cfg):
    nc = _new_nc(); p = Prog(nc); cx = Ctx(nc, p)
    D, NT = cfg.D, cfg.NT
    DC = D // 128
    yT_all = _din(nc, "yT_all", [2048, NT]); oT = _din(nc, "oT", [1024, NT]); xT = _din(nc, "xT", [D, NT])
    woutr = _din(nc, "woutr", [DC, 128, 16 * 128]); vecs = _din(nc, "vecs", [128, 3 * DC]); ngv = _din(nc, "ngv", [128, 32])
    ynT = _dscr(nc, "ynT", [2048, NT]); rT = _dscr(nc, "rT", [D, NT]); xoT = _dout(nc, "xoT", [D, NT])
    b_in, b_yn, b_r, b_xo = p.bufs(4, "dram")
    emit_outproj(cx, yT_all, oT, xT, woutr, vecs, ngv, ynT, rT, xoT, b_in, b_yn, b_r, b_xo, D, NT, NT, cfg.eps, cfg.alpha)
    p.wait_all("sp", [b_xo]); p.emit()
    return nc


def _get(cfg, name, builder):
    key = (name, cfg.D, cfg.DFF, cfg.S, cfg.depth)
    if key not in _CACHE:
        _CACHE[key] = builder(cfg)
    return _CACHE[key]


def _run(nc, in_maps):
    res = run_bass_kernel_spmd(nc, in_maps, core_ids=list(range(NCORES)))
    return res.results


def ffn_host_layout(w1, w2, D, DFF):
    DC, HC = D // 128, DFF // 128
    a = w1[:, :DFF].reshape(DC, 128, HC, 128)
    u = w1[:, DFF:].reshape(DC, 128, HC, 128)
    au = np.stack([a, u], axis=3)
    w1r = np.ascontiguousarray(au.transpose(2, 1, 0, 3, 4)).reshape(HC, 128, DC * 256)
    w2r = np.ascontiguousarray(w2.reshape(HC, 128, DC, 128).transpose(2, 1, 0, 3)).reshape(DC, 128, HC * 128)
    return w1r, w2r


def vec_layout(vs, D):
    DC = D // 128
    return np.ascontiguousarray(np.stack([np.asarray(v, np.float32).reshape(DC, 128).T for v in vs], axis=1)).reshape(128, len(vs) * DC)


def kernel_cfg(cfg, x, c, ada_w, ada_b, ln_g, ln_b, ffn1_w1, ffn1_w2, mix_w_in, mlstm_conv_w, mlstm_conv_b,
               mlstm_gate_b, mix_norm_g, mix_w_out, ffn2_w1, ffn2_w2):
    D, DFF, S, NT, depth = cfg.D, cfg.DFF, cfg.S, cfg.NT, cfg.depth
    DC = D // 128
    f32 = np.float32
    x = np.asarray(x, f32); c = np.asarray(c, f32)
    NCOL = cfg.NCOL
    cl = vec_layout([c[0]], D)
    aw = np.asarray(ada_w, f32).reshape(depth, DC, 128, NCORES, NCOL)
    in_maps = []
    for i in range(NCORES):
        wr = np.ascontiguousarray(aw[:, :, :, i, :].transpose(0, 2, 1, 3)).reshape(depth, 128, DC * NCOL)
        br = np.ascontiguousarray(np.asarray(ada_b, f32).reshape(depth, NCORES, 1, NCOL)[:, i])
        in_maps.append({"cl": cl, "wr": wr, "br": br})
    res = _run(_get(cfg, "mod", build_mod), in_maps)
    mod = np.concatenate([r["mod"].reshape(depth, NCOL) for r in res], axis=1).reshape(depth, 3, 3, D)
    xTs = [np.ascontiguousarray(x[0, i * NT:(i + 1) * NT, :].T) for i in range(NCORES)]
    nc_ffn = _get(cfg, "ffn", build_ffn)
    nc_ip = _get(cfg, "inproj", build_inproj)
    nc_at = _get(cfg, "attn", build_attn)
    nc_op = _get(cfg, "outproj", build_outproj)
    zeros = np.zeros(D, f32)

    def run_ffn(xTs, w1, w2, l, sub):
        w1r, w2r = ffn_host_layout(np.asarray(w1, f32), np.asarray(w2, f32), D, DFF)
        vecs = vec_layout([mod[l, sub, 1], mod[l, sub, 0], mod[l, sub, 2], ln_g[l, sub], ln_b[l, sub]], D)
        res = _run(nc_ffn, [{"xT": xTs[i], "w1r": w1r, "w2r": w2r, "vecs": vecs} for i in range(NCORES)])
        return [r["xoT"] for r in res]

    for l in range(depth):
        xTs = run_ffn(xTs, ffn1_w1[l], ffn1_w2[l], l, 0)
        W = np.asarray(mix_w_in[l], f32)
        fm_cols = [W[:, 0:2048], W[:, 3072:4096], W[:, 5120:6144]]
        gpad = np.zeros((D, 128), f32); gpad[:, 0:8] = W[:, 6144:6152]
        Wfm = np.concatenate(fm_cols + [gpad], axis=1)
        wfm = np.ascontiguousarray(Wfm.reshape(DC, 128, 33, 128).transpose(2, 1, 0, 3)).reshape(33, 128, DC * 128)
        Wtm = np.concatenate([W[:, 2048:3072], W[:, 4096:5120]], axis=1)
        wtm = np.ascontiguousarray(Wtm.reshape(DC, 128, 4, 512).transpose(2, 1, 0, 3)).reshape(4, 128, DC * 512)
        vecs = vec_layout([mod[l, 1, 1], mod[l, 1, 0]], D)
        rip = _run(nc_ip, [{"xT": xTs[i], "vecs": vecs, "wfm": wfm, "wtm": wtm} for i in range(NCORES)])
        qkT = np.concatenate([r["qkT"] for r in rip], axis=2)
        vsb = np.concatenate([r["vsb"] for r in rip], axis=0)
        mqkT = np.concatenate([r["mqkT"] for r in rip], axis=2)
        vml = np.concatenate([r["vml"] for r in rip], axis=0)
        gT = np.concatenate([r["gT"] for r in rip], axis=1)
        cwl = np.asarray(mlstm_conv_w[l], f32); cbl = np.asarray(mlstm_conv_b[l], f32); gbl = np.asarray(mlstm_gate_b[l], f32)
        in_maps = []
        for h in range(NCORES):
            hm, vh = h // 2, h % 2
            cw = np.concatenate([cwl[:, hm * 128:(hm + 1) * 128].T, cwl[:, 512 + hm * 128:512 + (hm + 1) * 128].T,
                                 cbl[hm * 128:(hm + 1) * 128, None], cbl[512 + hm * 128:512 + (hm + 1) * 128, None]], axis=1)
            gb = np.ascontiguousarray(np.broadcast_to(np.array([gbl[hm], gbl[4 + hm]], f32), (128, 2)))
            in_maps.append({
                "qT": np.ascontiguousarray(qkT[h]), "kT": np.ascontiguousarray(qkT[8 + h]),
                "v": np.ascontiguousarray(vsb[:, h * 128:(h + 1) * 128]),
                "uq": np.ascontiguousarray(mqkT[hm]), "uk": np.ascontiguousarray(mqkT[4 + hm]),
                "cw": np.ascontiguousarray(cw, dtype=f32),
                "vm": np.ascontiguousarray(vml[:, hm * 256 + vh * 128:hm * 256 + (vh + 1) * 128]),
                "ig": np.ascontiguousarray(gT[hm].reshape(S // 128, 128)), "fg": np.ascontiguousarray(gT[4 + hm].reshape(S // 128, 128)),
                "gb": gb})
        rat = _run(nc_at, in_maps)
        yall = np.concatenate([r["yT"] for r in rat] + [r["hT"] for r in rat], axis=0)
        ng = np.asarray(mix_norm_g[l], f32)
        ngv = np.concatenate([vec_layout([ng], 2048), np.zeros((128, 16), f32)], axis=1)
        woutr = np.ascontiguousarray(np.asarray(mix_w_out[l], f32).reshape(16, 128, DC, 128).transpose(2, 1, 0, 3)).reshape(DC, 128, 16 * 128)
        vecs = vec_layout([mod[l, 1, 2], ln_g[l, 1], ln_b[l, 1]], D)
        rop = _run(nc_op, [{"yT_all": np.ascontiguousarray(yall[:, i * NT:(i + 1) * NT]),
                            "oT": rip[i]["oT"].reshape(1024, NT), "xT": xTs[i], "woutr": woutr, "vecs": vecs, "ngv": ngv}
                           for i in range(NCORES)])
        xTs = [r["xoT"] for r in rop]
        xTs = run_ffn(xTs, ffn2_w1[l], ffn2_w2[l], l, 2)
    out = np.concatenate([xt.T for xt in xTs], axis=0)[None]
    return np.ascontiguousarray(out, dtype=f32)


def kernel(**inputs):
    cfg = Cfg()
    return kernel_cfg(cfg, **inputs)
```

```python
import numpy as np
import ml_dtypes
import concourse.bass as bass
import concourse.mybir as mybir
from concourse.bass_utils import run_bass_kernel_spmd

F32 = mybir.dt.float32
BF16 = mybir.dt.bfloat16
AF = mybir.ActivationFunctionType
ALU = mybir.AluOpType
NPBF = ml_dtypes.bfloat16

ENGS = ("pe", "act", "dve", "pool", "sp")
NCORES = 8


class Buf:
    __slots__ = ("name", "w", "r", "dsem", "dcnt")

    def __init__(self, name):
        self.name = name
        self.w = {}
        self.r = {}
        self.dsem = None
        self.dcnt = 0


class Prog:
    def __init__(self, nc):
        self.nc = nc
        self.ops = {e: [] for e in ENGS}
        self.sems = {}
        self.cnt = {}
        self.seen = {e: {} for e in ENGS}
        for e in ENGS:
            self.sems[e] = nc.alloc_semaphore("sem_" + e)
            self.cnt[e] = 0
        self.nbuf = 0

    def buf(self, name=None):
        self.nbuf += 1
        return Buf((name or "b") + "_" + str(self.nbuf))

    def bufs(self, n, name="b"):
        return [self.buf(f"{name}{i}") for i in range(n)]

    def _deps(self, eng, reads, writes):
        need = {}

        def add(d, same_ok):
            for k, v in d.items():
                if k == eng and same_ok:
                    continue
                if need.get(k, 0) < v:
                    need[k] = v
        for b in reads:
            add(b.w, False)
        for b in writes:
            add(b.w, True)
            add(b.r, True)
        waits = []
        seen = self.seen[eng]
        for k, v in need.items():
            if seen.get(k, 0) < v:
                seen[k] = v
                waits.append((k, v))
        return waits

    def _record(self, tok, reads, writes):
        k, v = tok
        for b in reads:
            if b.r.get(k, 0) < v:
                b.r[k] = v
        for b in writes:
            if b.w.get(k, 0) < v:
                b.w[k] = v

    def op(self, eng, fns, reads=(), writes=()):
        if callable(fns):
            fns = [fns]
        waits = self._deps(eng, reads, writes)
        self.cnt[eng] += 1
        tok = (eng, self.cnt[eng])
        self._record(tok, reads, writes)
        self.ops[eng].append((waits, fns, (eng, 1)))
        return tok

    def dma(self, q, fn, prim, reads=(), writes=()):
        waits = self._deps(q, reads, writes)
        if prim.dsem is None:
            prim.dsem = "d_" + prim.name
            self.sems[prim.dsem] = self.nc.alloc_semaphore(prim.dsem)
        prim.dcnt += 16
        tok = (prim.dsem, prim.dcnt)
        self._record(tok, reads, writes)
        self.ops[q].append((waits, [fn], (prim.dsem, 16)))
        return tok

    def wait_all(self, eng, bufs):
        waits = self._deps(eng, (), bufs)
        self.ops[eng].append((waits, [], None))

    def emit(self):
        nc, sems, ops = self.nc, self.sems, self.ops
        self.ops = {e: [] for e in ENGS}

        def run(e, lst):
            for waits, fns, inc in lst:
                for k, v in waits:
                    e.wait_ge(sems[k], v)
                n = len(fns)
                for i, fn in enumerate(fns):
                    ins = fn(e)
                    if i == n - 1 and inc is not None:
                        ins.then_inc(sems[inc[0]], inc[1])

        with nc.Block() as block:
            @block.tensor
            def _(e):
                run(e, ops["pe"])

            @block.scalar
            def _(e):
                run(e, ops["act"])

            @block.vector
            def _(e):
                run(e, ops["dve"])

            @block.gpsimd
            def _(e):
                run(e, ops["pool"])

            @block.sync
            def _(e):
                run(e, ops["sp"])


def cdma(e, out, in_):
    n = out.shape[-1]
    if n > 2048:
        for b in range(2048, 63, -1):
            if n % b == 0:
                break
        out = out.rearrange("p (a b) -> p a b", b=b)
        in_ = in_.rearrange("p (a b) -> p a b", b=b)
    return e.dma_start(out=out, in_=in_)


class Ctx:
    def __init__(self, nc, p, bf16_bank=None):
        self.nc, self.p = nc, p
        self.banks = [(nc.alloc_psum_tensor(f"bank{i}", [128, 1024], BF16).ap() if i == bf16_bank else
                       nc.alloc_psum_tensor(f"bank{i}", [128, 512], F32).ap()) for i in range(8)]
        self.bbufs = p.bufs(8, "bank")
        self.ones = nc.alloc_sbuf_tensor("ones_f32", [128, 128], F32).ap()
        self.onesB = nc.alloc_sbuf_tensor("ones_bf16", [128, 128], BF16).ap()
        self.b_ones = p.buf("ones")
        p.op("pool", lambda e: e.memset(self.ones, 1.0), writes=[self.b_ones])
        p.op("pool", lambda e: e.memset(self.onesB, 1.0), writes=[self.b_ones])
        self.n_sb = 0

    def sb(self, shape, dt, name=None):
        self.n_sb += 1
        return self.nc.alloc_sbuf_tensor((name or "sb") + f"_{self.n_sb}", shape, dt).ap()


def emit_ln_phase(cx, rT, outT, vec_g, vec_b, b_vec, b_r_dram, b_out_dram, GC, NT, LS, eps, ngroups=1,
                  banks=(6, 7), tag="ln"):
    nc, p = cx.nc, cx.p
    nsub = NT // LS
    Dg = GC * 128
    rl = [cx.sb([128, GC, LS], F32, f"{tag}_rl{i}") for i in range(2)]
    b_rl = p.bufs(2, tag + "rl")
    sq = [cx.sb([128, LS], F32, f"{tag}_sq{i}") for i in range(2)]
    b_sq = p.bufs(2, tag + "sq")
    mean = cx.sb([128, LS], F32, tag + "_mean"); b_mean = p.buf(tag + "mean")
    m2 = cx.sb([128, LS], F32, tag + "_m2"); b_m2 = p.buf(tag + "m2")
    var = cx.sb([128, LS], F32, tag + "_var"); b_var = p.buf(tag + "var")
    rstd = cx.sb([128, LS], F32, tag + "_rstd"); b_rstd = p.buf(tag + "rstd")
    nmr = cx.sb([128, LS], F32, tag + "_nmr"); b_nmr = p.buf(tag + "nmr")
    t1 = [cx.sb([128, LS], F32, f"{tag}_t1{i}") for i in range(2)]; b_t1 = p.bufs(2, tag + "t1")
    t2 = [cx.sb([128, LS], F32, f"{tag}_t2{i}") for i in range(2)]; b_t2 = p.bufs(2, tag + "t2")
    ot = [cx.sb([128, GC, LS], F32, f"{tag}_ot{i}") for i in range(2)]; b_ot = p.bufs(2, tag + "ot")
    bs, bq = banks
    k = 0
    it = 0
    for g in range(ngroups):
        rTg = rT[g * Dg:(g + 1) * Dg, :].rearrange("(c p) t -> p c t", p=128)
        oTg = outT[g * Dg:(g + 1) * Dg, :].rearrange("(c p) t -> p c t", p=128)
        for s in range(nsub):
            i2 = it % 2
            it += 1
            tsl = slice(s * LS, (s + 1) * LS)
            p.dma("sp", lambda e, i2=i2, tsl=tsl, rTg=rTg: e.dma_start(out=rl[i2], in_=rTg[:, :, tsl]),
                  b_rl[i2], reads=[b_r_dram], writes=[b_rl[i2]])
            for c in range(GC):
                q2 = k % 2
                k += 1
                p.op("act", lambda e, i2=i2, c=c, q2=q2: e.activation(out=sq[q2], in_=rl[i2][:, c, :], func=AF.Square),
                     reads=[b_rl[i2]], writes=[b_sq[q2]])
                p.op("pe", [lambda e, i2=i2, c=c: e.matmul(cx.banks[bs][:, 0:LS], lhsT=cx.ones, rhs=rl[i2][:, c, :],
                                                           start=(c == 0), stop=(c == GC - 1)),
                            lambda e, q2=q2, c=c: e.matmul(cx.banks[bq][:, 0:LS], lhsT=cx.ones, rhs=sq[q2],
                                                           start=(c == 0), stop=(c == GC - 1))],
                     reads=[b_rl[i2], b_sq[q2], cx.b_ones], writes=[cx.bbufs[bs], cx.bbufs[bq]])
            p.op("dve", lambda e: e.tensor_scalar(mean, cx.banks[bs][:, 0:LS], 1.0 / Dg, None, ALU.mult),
                 writes=[b_mean, cx.bbufs[bs]])
            p.op("dve", lambda e: e.tensor_tensor(m2, mean, mean, ALU.mult), reads=[b_mean], writes=[b_m2])
            p.op("dve", lambda e: e.tensor_scalar(var, cx.banks[bq][:, 0:LS], 1.0 / Dg, eps, ALU.mult, ALU.add),
                 writes=[b_var, cx.bbufs[bq]])
            p.op("dve", lambda e: e.tensor_tensor(m2, var, m2, ALU.subtract), reads=[b_var, b_m2], writes=[b_m2])
            p.op("act", lambda e: e.activation(out=var, in_=m2, func=AF.Sqrt), reads=[b_m2], writes=[b_var])
            p.op("dve", lambda e: e.reciprocal(rstd, var), reads=[b_var], writes=[b_rstd])
            p.op("dve", lambda e: e.scalar_tensor_tensor(nmr, mean, -1.0, rstd, ALU.mult, ALU.mult),
                 reads=[b_mean, b_rstd], writes=[b_nmr])
            for c in range(GC):
                q2 = k % 2
                k += 1
                p.op("dve", lambda e, i2=i2, c=c, q2=q2: e.tensor_tensor(t1[q2], rl[i2][:, c, :], rstd, ALU.mult),
                     reads=[b_rl[i2], b_rstd], writes=[b_t1[q2]])
                p.op("pool", lambda e, q2=q2: e.tensor_tensor(t2[q2], t1[q2], nmr, ALU.add),
                     reads=[b_t1[q2], b_nmr], writes=[b_t2[q2]])
                vi = g * GC + c
                p.op("act", lambda e, i2=i2, c=c, q2=q2, vi=vi: e.activation(
                    out=ot[i2][:, c, :], in_=t2[q2], func=AF.Identity,
                    bias=vec_b[:, vi:vi + 1], scale=vec_g[:, vi:vi + 1]),
                    reads=[b_t2[q2], b_vec], writes=[b_ot[i2]])
            p.dma("sp", lambda e, i2=i2, tsl=tsl, oTg=oTg: e.dma_start(out=oTg[:, :, tsl], in_=ot[i2]),
                  b_ot[i2], reads=[b_ot[i2]], writes=[b_out_dram])


class StageC:
    def __init__(self, cx, KC, DC, T, NS, tag, w2ext=None, NW2=3, extra_nl=None):
        p = cx.p
        self.cx, self.KC, self.DC, self.T, self.NS, self.tag = cx, KC, DC, T, NS, tag
        nts = T // NS
        self.nts = nts
        if w2ext is None:
            self.w2t = [cx.sb([128, KC * 128], BF16, f"{tag}_w2t{i}") for i in range(NW2)]; self.b_w2t = p.bufs(NW2, tag + "w2t")
            self.w2wr = [[b] for b in self.b_w2t]
        else:
            self.w2t, self.b_w2t, self.w2wr = w2ext
        self.NW2 = len(self.w2t)
        self.xres = [cx.sb([128, NS], F32, f"{tag}_xres{i}") for i in range(2)]; self.b_xres = p.bufs(2, tag + "xres")
        self.rt = [cx.sb([128, NS], F32, f"{tag}_rt{i}") for i in range(2)]; self.b_rt = p.bufs(2, tag + "rt")
        self.rbf = [cx.sb([128, NS], BF16, f"{tag}_rbf{i}") for i in range(2)]; self.b_rbf = p.bufs(2, tag + "rbf")
        self.sqbf = [cx.sb([128, NS], BF16, f"{tag}_sqbf{i}") for i in range(2)]; self.b_sqbf = p.bufs(2, tag + "sqbf")
        self.asum = cx.sb([128, T], F32, tag + "_asum"); self.b_asum = p.bufs(nts, tag + "asum")
        self.asq = cx.sb([128, T], F32, tag + "_asq"); self.b_asq = p.bufs(nts, tag + "asq")
        self.nmr = cx.sb([128, T], F32, tag + "_nmr"); self.b_nmr = p.bufs(nts, tag + "nmr")
        self.nl = [cx.sb([128, NS], F32, f"{tag}_nl{i}") for i in range(3)]; self.b_nl = p.bufs(3, tag + "nl")
        self.nl += self.xres; self.b_nl += self.b_xres
        for ap_, b_ in (extra_nl or []):
            self.nl.append(ap_); self.b_nl.append(b_)
        self.kw2 = self.ky = self.kr = self.kn = 0

    def run(self, gT, b_gT, w2r, xTv, rTv, oTv, gw, b_gw, vg, vb, b_vgb, t0, eps, b_x, b_r, b_xo,
            bank_y=(4, 5), bank_s=(6, 7), hook=None):
        cx, p = self.cx, self.cx.p
        KC, DC, NS, nts = self.KC, self.DC, self.NS, self.nts
        banks, bb = cx.banks, cx.bbufs
        Dtot = DC * 128
        pend = None

        def stats(c, ts, r2):
            sl = slice(ts * NS, (ts + 1) * NS)
            bs, bq = bank_s
            p.op("pe", [lambda e: e.matmul(banks[bs][:, 0:NS], lhsT=cx.onesB, rhs=self.rbf[r2], start=True, stop=True),
                        lambda e: e.matmul(banks[bq][:, 0:NS], lhsT=cx.onesB, rhs=self.sqbf[r2], start=True, stop=True)],
                 reads=[self.b_rbf[r2], self.b_sqbf[r2], cx.b_ones], writes=[bb[bs], bb[bq]])
            if c == 0:
                p.op("dve", lambda e: e.tensor_copy(self.asum[:, sl], banks[bs][:, 0:NS]), writes=[self.b_asum[ts], bb[bs]])
                p.op("dve", lambda e: e.tensor_copy(self.asq[:, sl], banks[bq][:, 0:NS]), writes=[self.b_asq[ts], bb[bq]])
            else:
                p.op("dve", lambda e: e.tensor_tensor(self.asum[:, sl], self.asum[:, sl], banks[bs][:, 0:NS], ALU.add),
                     reads=[self.b_asum[ts]], writes=[self.b_asum[ts], bb[bs]])
                p.op("dve", lambda e: e.tensor_tensor(self.asq[:, sl], self.asq[:, sl], banks[bq][:, 0:NS], ALU.add),
                     reads=[self.b_asq[ts]], writes=[self.b_asq[ts], bb[bq]])

        for c in range(DC):
            i2 = self.kw2 % self.NW2; self.kw2 += 1
            p.dma("pool", lambda e, i2=i2, c=c: cdma(e, self.w2t[i2], w2r[c]), self.b_w2t[i2], writes=self.w2wr[i2])
            for ts in range(nts):
                by = bank_y[self.ky % 2]; self.ky += 1
                sl = slice(ts * NS, (ts + 1) * NS)
                gsl = slice(t0 + ts * NS, t0 + (ts + 1) * NS)
                fns = []
                for h in range(KC):
                    fns.append(lambda e, i2=i2, h=h, sl=sl, by=by: e.matmul(
                        banks[by][:, 0:NS], lhsT=self.w2t[i2][:, h * 128:(h + 1) * 128], rhs=gT[:, h, sl],
                        start=(h == 0), stop=(h == KC - 1)))
                p.op("pe", fns, reads=[self.b_w2t[i2], b_gT[ts]], writes=[bb[by]])
                if pend is not None:
                    stats(*pend)
                r2 = self.kr % 2; self.kr += 1
                p.dma("sp", lambda e, r2=r2, c=c, gsl=gsl: e.dma_start(out=self.xres[r2], in_=xTv[:, c, gsl]),
                      self.b_xres[r2], reads=[b_x], writes=[self.b_xres[r2]])
                p.op("dve", lambda e, r2=r2, by=by, c=c: e.scalar_tensor_tensor(
                    self.rt[r2], banks[by][:, 0:NS], gw[:, c:c + 1], self.xres[r2], ALU.mult, ALU.add),
                    reads=[self.b_xres[r2], b_gw], writes=[self.b_rt[r2], bb[by]])
                p.op("act", lambda e, r2=r2: e.activation(out=self.rbf[r2], in_=self.rt[r2], func=AF.Copy), reads=[self.b_rt[r2]], writes=[self.b_rbf[r2]])
                p.op("act", lambda e, r2=r2: e.activation(out=self.sqbf[r2], in_=self.rt[r2], func=AF.Square),
                     reads=[self.b_rt[r2]], writes=[self.b_sqbf[r2]])
                p.dma("sp", lambda e, r2=r2, c=c, gsl=gsl: e.dma_start(out=rTv[:, c, gsl], in_=self.rt[r2]),
                      self.b_rt[r2], reads=[self.b_rt[r2]], writes=[b_r])
                pend = (c, ts, r2)
                if hook:
                    hook.pop(0)()
        stats(*pend)
        while hook:
            hook.pop(0)()
        for ts in range(nts):
            sl = slice(ts * NS, (ts + 1) * NS)
            A, Q, M = self.asum[:, sl], self.asq[:, sl], self.nmr[:, sl]
            rd = [self.b_asum[ts], self.b_asq[ts], self.b_nmr[ts]]
            p.op("dve", lambda e, A=A: e.tensor_scalar(A, A, 1.0 / Dtot, None, ALU.mult), reads=rd, writes=rd)
            p.op("dve", lambda e, A=A, M=M: e.tensor_tensor(M, A, A, ALU.mult), reads=rd, writes=rd)
            p.op("dve", lambda e, Q=Q: e.tensor_scalar(Q, Q, 1.0 / Dtot, eps, ALU.mult, ALU.add), reads=rd, writes=rd)
            p.op("dve", lambda e, Q=Q, M=M: e.tensor_tensor(Q, Q, M, ALU.subtract), reads=rd, writes=rd)
            p.op("act", lambda e, Q=Q: e.activation(out=Q, in_=Q, func=AF.Sqrt), reads=rd, writes=rd)
            p.op("dve", lambda e, Q=Q: e.reciprocal(Q, Q), reads=rd, writes=rd)
            p.op("dve", lambda e, A=A, Q=Q, M=M: e.scalar_tensor_tensor(M, A, -1.0, Q, ALU.mult, ALU.mult), reads=rd, writes=rd)
        items = []

        def norm_item(ts, c):
            sl = slice(ts * NS, (ts + 1) * NS)
            gsl = slice(t0 + ts * NS, t0 + (ts + 1) * NS)
            rd = [self.b_asum[ts], self.b_asq[ts], self.b_nmr[ts]]
            n3 = self.kn % len(self.nl); self.kn += 1
            nl = self.nl[n3]; bnl = self.b_nl[n3]
            p.dma("sp", lambda e: e.dma_start(out=nl, in_=rTv[:, c, gsl]), bnl, reads=[b_r], writes=[bnl])
            p.op("dve", lambda e: e.tensor_tensor(nl, nl, self.asq[:, sl], ALU.mult), reads=[bnl] + rd, writes=[bnl])
            p.op("dve", lambda e: e.tensor_tensor(nl, nl, self.nmr[:, sl], ALU.add), reads=[bnl] + rd, writes=[bnl])
            p.op("act", lambda e: e.activation(out=nl, in_=nl, func=AF.Identity, bias=vb[:, c:c + 1], scale=vg[:, c:c + 1]),
                 reads=[bnl, b_vgb], writes=[bnl])
            p.dma("act", lambda e: e.dma_start(out=oTv[:, c, gsl], in_=nl), bnl, reads=[bnl], writes=[b_xo])

        for ts in range(nts):
            for c in range(DC):
                items.append(lambda ts=ts, c=c: norm_item(ts, c))
        return items


def modulate_items(cx, xTv, xbf, b_xbf, s1, sh, b_vec, xin, b_xin, kx, t0, nts, NS, DC, b_x):
    p = cx.p
    items = []

    def one(ts, c):
        i3 = kx[0] % 3; kx[0] += 1
        sl = slice(t0 + ts * NS, t0 + (ts + 1) * NS)
        p.dma("sp", lambda e: e.dma_start(out=xin[i3], in_=xTv[:, c, sl]), b_xin[i3], reads=[b_x], writes=[b_xin[i3]])
        p.op("act", lambda e: e.activation(out=xbf[:, c, ts * NS:(ts + 1) * NS], in_=xin[i3], func=AF.Identity,
                                           bias=sh[:, c:c + 1], scale=s1[:, c:c + 1]),
             reads=[b_xin[i3], b_vec], writes=[b_xbf[ts]])

    for ts in range(nts):
        for c in range(DC):
            items.append(lambda ts=ts, c=c: one(ts, c))
    return items


def emit_modulate(cx, xTv, xbf, b_xbf, s1, sh, b_vec, xin, b_xin, kx, t0, nts, NS, DC, b_x):
    for it_ in modulate_items(cx, xTv, xbf, b_xbf, s1, sh, b_vec, xin, b_xin, kx, t0, nts, NS, DC, b_x):
        it_()


def emit_ffn(cx, xT, w1r, w2r, vecs, rT, xoT, b_x, b_r, b_xo, D, DFF, NT, T, resw, alpha, ln_eps, tag="f"):
    nc, p = cx.nc, cx.p
    DC, HC = D // 128, DFF // 128
    NS = min(512, T)
    assert T % NS == 0 and NT % T == 0
    nts = T // NS
    vec = cx.sb([128, 5 * DC], F32, tag + "_vec"); b_vec = p.buf(tag + "vec")
    der = cx.sb([128, 2 * DC], F32, tag + "_der")
    p.dma("sp", lambda e: e.dma_start(out=vec, in_=vecs), b_vec, writes=[b_vec])
    p.op("dve", lambda e: e.tensor_scalar(der[:, 0:DC], vec[:, 0:DC], 1.0, None, ALU.add), reads=[b_vec], writes=[b_vec])
    p.op("dve", lambda e: e.tensor_scalar(der[:, DC:2 * DC], vec[:, 2 * DC:3 * DC], 1.0, resw / alpha, ALU.add, ALU.mult),
         reads=[b_vec], writes=[b_vec])
    xin = [cx.sb([128, NS], F32, f"{tag}_xin{i}") for i in range(3)]; b_xin = p.bufs(3, tag + "xin")
    xbf = cx.sb([128, DC, T], BF16, tag + "_xbf"); b_xbf = p.bufs(nts, tag + "xbf")
    NW1, NW2 = 5, 3
    s1, s2 = DC * 256, HC * 128
    wreg = cx.sb([128, max(NW1 * s1, NW2 * s2)], BF16, tag + "_wreg")
    w1t = [wreg[:, i * s1:(i + 1) * s1] for i in range(NW1)]; b_w1t = p.bufs(NW1, tag + "w1t")
    w2v = [wreg[:, k * s2:(k + 1) * s2] for k in range(NW2)]; b_w2v = p.bufs(NW2, tag + "w2t")
    ov = lambda i, k: i * s1 < (k + 1) * s2 and k * s2 < (i + 1) * s1
    w1wr = [[b_w1t[i]] + [b_w2v[k] for k in range(NW2) if ov(i, k)] for i in range(NW1)]
    w2wr = [[b_w2v[k]] + [b_w1t[i] for i in range(NW1) if ov(i, k)] for k in range(NW2)]
    gT = cx.sb([128, HC, T], BF16, tag + "_gT"); b_gT = p.bufs(nts, tag + "gT")
    sa = [cx.sb([128, NS], F32, f"{tag}_sa{i}") for i in range(2)]; b_sa = p.bufs(2, tag + "sa")
    stc = StageC(cx, HC, DC, T, NS, tag + "C", w2ext=(w2v, b_w2v, w2wr), extra_nl=list(zip(xin, b_xin)))
    xTv = xT.rearrange("(c p) t -> p c t", p=128)
    rTv = rT.rearrange("(c p) t -> p c t", p=128)
    oTv = xoT.rearrange("(c p) t -> p c t", p=128)
    kx = [0]
    ks = kw1 = kb = 0
    bank_a, bank_u = (0, 1), (2, 3)
    ntt = NT // T
    emit_modulate(cx, xTv, xbf, b_xbf, der[:, 0:DC], vec[:, DC:2 * DC], b_vec, xin, b_xin, kx, 0, nts, NS, DC, b_x)
    pending = []
    for tt in range(ntt):
        t0 = tt * T
        for j in range(HC):
            for _ in range(2):
                if pending:
                    pending.pop(0)()
            i2 = (2 + j % 3) if (HC > 12 and j >= HC - 6) else (j % NW1)
            p.dma("pool", lambda e, i2=i2, j=j: cdma(e, w1t[i2], w1r[j]), b_w1t[i2], writes=w1wr[i2])
            for ts in range(nts):
                ba, bu = bank_a[kb % 2], bank_u[kb % 2]; kb += 1
                sl = slice(ts * NS, (ts + 1) * NS)
                fns = []
                for c in range(DC):
                    fns.append(lambda e, i2=i2, c=c, sl=sl, ba=ba: e.matmul(
                        cx.banks[ba][:, 0:NS], lhsT=w1t[i2][:, c * 256:c * 256 + 128], rhs=xbf[:, c, sl],
                        start=(c == 0), stop=(c == DC - 1)))
                p.op("pe", fns, reads=[b_w1t[i2], b_xbf[ts]], writes=[cx.bbufs[ba]])
                fns = []
                for c in range(DC):
                    fns.append(lambda e, i2=i2, c=c, sl=sl, bu=bu: e.matmul(
                        cx.banks[bu][:, 0:NS], lhsT=w1t[i2][:, c * 256 + 128:c * 256 + 256], rhs=xbf[:, c, sl],
                        start=(c == 0), stop=(c == DC - 1)))
                p.op("pe", fns, reads=[b_w1t[i2], b_xbf[ts]], writes=[cx.bbufs[bu]])
                s2 = ks % 2; ks += 1
                p.op("act", lambda e, s2=s2, ba=ba: e.activation(out=sa[s2], in_=cx.banks[ba][:, 0:NS], func=AF.Silu),
                     writes=[b_sa[s2], cx.bbufs[ba]])
                p.op("dve", lambda e, s2=s2, bu=bu, j=j, sl=sl: e.tensor_tensor(gT[:, j, sl], sa[s2], cx.banks[bu][:, 0:NS], ALU.mult),
                     reads=[b_sa[s2]], writes=[b_gT[ts], cx.bbufs[bu]])
        hook = None
        if tt + 1 < ntt:
            hook = modulate_items(cx, xTv, xbf, b_xbf, der[:, 0:DC], vec[:, DC:2 * DC], b_vec, xin, b_xin, kx,
                                  t0 + T, nts, NS, DC, b_x)
        while pending:
            pending.pop(0)()
        pending = stc.run(gT, b_gT, w2r, xTv, rTv, oTv, der[:, DC:2 * DC], b_vec, vec[:, 3 * DC:4 * DC], vec[:, 4 * DC:5 * DC], b_vec,
                          t0, ln_eps / (alpha * alpha), b_x, b_r, b_xo, hook=hook)
    while pending:
        pending.pop(0)()


def emit_inproj(cx, xT, vecs, wfm, wtm, qkT, vsb, mqkT, vml, oT, gTd, b_x, b_out, D, NT, qscale, tag="ip"):
    nc, p = cx.nc, cx.p
    DC = D // 128
    NS = min(512, NT)
    nts = NT // NS
    vec = cx.sb([128, 2 * DC], F32, tag + "_vec"); b_vec = p.buf(tag + "vec")
    s1 = cx.sb([128, DC], F32, tag + "_s1")
    p.dma("sp", lambda e: e.dma_start(out=vec, in_=vecs), b_vec, writes=[b_vec])
    p.op("dve", lambda e: e.tensor_scalar(s1, vec[:, 0:DC], 1.0, None, ALU.add), reads=[b_vec], writes=[b_vec])
    xin = [cx.sb([128, NS], F32, f"{tag}_xin{i}") for i in range(3)]; b_xin = p.bufs(3, tag + "xin")
    xbf = cx.sb([128, DC, NT], BF16, tag + "_xbf"); b_xbf = p.bufs(nts, tag + "xbf")
    xTv = xT.rearrange("(c p) t -> p c t", p=128)
    emit_modulate(cx, xTv, xbf, b_xbf, s1, vec[:, DC:2 * DC], b_vec, xin, b_xin, [0], 0, nts, NS, DC, b_x)
    wf = [cx.sb([128, DC * 128], BF16, f"{tag}_wf{i}") for i in range(3)]; b_wf = p.bufs(3, tag + "wf")
    obf = [cx.sb([128, NS], BF16, f"{tag}_obf{i}") for i in range(3)]; b_obf = p.bufs(3, tag + "obf")
    of32 = [cx.sb([128, NS], F32, f"{tag}_of{i}") for i in range(3)]; b_of = p.bufs(3, tag + "of")
    kbk = ko = 0
    for j in range(33):
        i3 = j % 3
        p.dma("pool", lambda e, i3=i3, j=j: cdma(e, wf[i3], wfm[j]), b_wf[i3], writes=[b_wf[i3]])
        for ts in range(nts):
            bk = kbk % 4; kbk += 1
            sl = slice(ts * NS, (ts + 1) * NS)
            fns = []
            for c in range(DC):
                fns.append(lambda e, i3=i3, c=c, sl=sl, bk=bk: e.matmul(
                    cx.banks[bk][:, 0:NS], lhsT=wf[i3][:, c * 128:(c + 1) * 128], rhs=xbf[:, c, sl],
                    start=(c == 0), stop=(c == DC - 1)))
            p.op("pe", fns, reads=[b_wf[i3], b_xbf[ts]], writes=[cx.bbufs[bk]])
            o3 = ko % 3; ko += 1
            eng = "act" if (ko % 2 == 0) else "dve"
            src = cx.banks[bk][:, 0:NS]
            if j < 16:
                dst, bd, ddst = obf[o3], b_obf[o3], qkT[j][:, sl]
                sc = qscale if j < 8 else 1.0
            elif j < 24:
                dst, bd, ddst, sc = of32[o3], b_of[o3], mqkT[j - 16][:, sl], 1.0
            elif j < 32:
                dst, bd, ddst, sc = of32[o3], b_of[o3], oT[j - 24][:, sl], 1.0
            else:
                dst, bd, ddst, sc = of32[o3][0:8, :], b_of[o3], gTd[:, sl], 1.0
                src = cx.banks[bk][0:8, 0:NS]
            if eng == "act":
                p.op("act", lambda e, dst=dst, src=src, sc=sc: e.activation(out=dst, in_=src, func=AF.Copy, scale=sc),
                     writes=[bd, cx.bbufs[bk]])
            else:
                p.op("dve", lambda e, dst=dst, src=src, sc=sc: e.tensor_scalar(dst, src, sc, None, ALU.mult),
                     writes=[bd, cx.bbufs[bk]])
            p.dma("sp", lambda e, dst=dst, ddst=ddst: e.dma_start(out=ddst, in_=dst), bd, reads=[bd], writes=[b_out])
    wt = [cx.sb([128, DC * 512], BF16, f"{tag}_wt{i}") for i in range(2)]; b_wt = p.bufs(2, tag + "wt")
    otm = [cx.sb([128, 512], BF16, f"{tag}_otm{i}") for i in range(3)]; b_otm = p.bufs(3, tag + "otm")
    for s in range(4):
        i2 = s % 2
        p.dma("pool", lambda e, i2=i2, s=s: cdma(e, wt[i2], wtm[s]), b_wt[i2], writes=[b_wt[i2]])
        dd = vsb if s < 2 else vml
        for tc in range(NT // 128):
            bk = 4 + kbk % 4; kbk += 1
            fns = []
            for c in range(DC):
                fns.append(lambda e, i2=i2, c=c, tc=tc, bk=bk: e.matmul(
                    cx.banks[bk], lhsT=xbf[:, c, tc * 128:(tc + 1) * 128], rhs=wt[i2][:, c * 512:(c + 1) * 512],
                    start=(c == 0), stop=(c == DC - 1)))
            p.op("pe", fns, reads=[b_wt[i2]] + b_xbf, writes=[cx.bbufs[bk]])
            o3 = ko % 3; ko += 1
            eng = "act" if (ko % 2 == 0) else "dve"
            if eng == "act":
                p.op("act", lambda e, o3=o3, bk=bk: e.activation(out=otm[o3], in_=cx.banks[bk], func=AF.Copy),
                     writes=[b_otm[o3], cx.bbufs[bk]])
            else:
                p.op("dve", lambda e, o3=o3, bk=bk: e.tensor_copy(otm[o3], cx.banks[bk]), writes=[b_otm[o3], cx.bbufs[bk]])
            p.dma("sp", lambda e, o3=o3, tc=tc, s=s, dd=dd: e.dma_start(
                out=dd[tc * 128:(tc + 1) * 128, (s % 2) * 512:(s % 2 + 1) * 512], in_=otm[o3]),
                b_otm[o3], reads=[b_otm[o3]], writes=[b_out])


def emit_sb_attn(cx, qT_d, kT_d, v_d, yT_d, b_in, b_out, S, bigQ, bigK, bigV, b_bigQ, b_bigK, b_bigV, tag="sb"):
    nc, p = cx.nc, cx.p
    NB, NQ = S // 128, S // 512
    Ui = cx.sb([128, 128], BF16, tag + "_Ui"); Ls = cx.sb([128, 128], BF16, tag + "_Ls")
    b_c = p.buf(tag + "const")
    p.op("pool", lambda e: e.memset(Ui, 1.0), writes=[b_c])
    p.op("pool", lambda e: e.memset(Ls, 1.0), writes=[b_c])
    p.op("pool", lambda e: e.affine_select(Ui, Ui, [[-1, 128]], ALU.is_ge, 0.0, base=0, channel_multiplier=1),
         reads=[b_c], writes=[b_c])
    p.op("pool", lambda e: e.affine_select(Ls, Ls, [[1, 128]], ALU.is_gt, 0.0, base=0, channel_multiplier=-1),
         reads=[b_c], writes=[b_c])
    npc = max(1, S // 4096)
    pc = S // npc
    for i in range(npc):
        sl = slice(i * pc, (i + 1) * pc)
        p.dma("sp", lambda e, sl=sl: e.dma_start(out=bigQ[:, sl], in_=qT_d[:, sl]), b_bigQ, reads=[b_in], writes=[b_bigQ])
        p.dma("sp", lambda e, sl=sl: e.dma_start(out=bigK[:, sl], in_=kT_d[:, sl]), b_bigK, reads=[b_in], writes=[b_bigK])
    v_v = v_d.rearrange("(b p) d -> p b d", p=128)
    nvb = max(1, NB // 32)
    for i in range(nvb):
        sl = slice(i * (NB // nvb), (i + 1) * (NB // nvb))
        p.dma("sp", lambda e, sl=sl: e.dma_start(out=bigV[:, sl, :], in_=v_v[:, sl, :]), b_bigV, reads=[b_in], writes=[b_bigV])
    NR = 4
    e32 = [cx.sb([128, 512], F32, f"{tag}_e{i}") for i in range(NR)]; b_e = p.bufs(NR, tag + "e")
    spb = [cx.sb([128, 512], BF16, f"{tag}_spb{i}") for i in range(NR)]; b_spb = p.bufs(NR, tag + "spb")
    t32 = [cx.sb([128, 512], F32, f"{tag}_t{i}") for i in range(NR)]; b_t = p.bufs(NR, tag + "t")
    ab = [cx.sb([128, 512], BF16, f"{tag}_ab{i}") for i in range(NR)]; b_ab = p.bufs(NR, tag + "ab")
    ot = [cx.sb([128, 512], F32, f"{tag}_ot{i}") for i in range(2)]; b_ot = p.bufs(2, tag + "ot")
    ZB = (0, 1, 2)
    RB = 3
    OB = (4, 5)
    banks, bb = cx.banks, cx.bbufs
    steps = []
    for tq in range(NQ):
        kbs = list(range(4 * tq + 3, -1, -1))
        for n, kb in enumerate(kbs):
            o = kb - 4 * tq
            steps.append(dict(tq=tq, kb=kb, first=(n == 0), last=(n == len(kbs) - 1), diag=(o >= 0), cs=max(o, 0) * 128))
    N = len(steps)

    def QK(i):
        s = steps[i]; z = ZB[i % 3]; cs = s["cs"]; tq, kb = s["tq"], s["kb"]
        p.op("pe", lambda e: e.matmul(banks[z][:, cs:512], lhsT=bigK[:, kb * 128:(kb + 1) * 128],
                                      rhs=bigQ[:, tq * 512 + cs:(tq + 1) * 512], start=True, stop=True),
             reads=[b_bigK, b_bigQ], writes=[bb[z]])

    def A(i):
        s = steps[i]; z = ZB[i % 3]; r = i % NR; cs = s["cs"]
        p.op("act", lambda e: e.activation(out=e32[r][:, cs:], in_=banks[z][:, cs:512], func=AF.Exp),
             writes=[b_e[r], bb[z]])
        p.op("act", lambda e: e.activation(out=spb[r][:, cs:], in_=e32[r][:, cs:], func=AF.Ln, bias=1.0),
             reads=[b_e[r]], writes=[b_spb[r]])
        if s["diag"]:
            p.op("pool", lambda e: e.affine_select(spb[r][:, cs:], spb[r][:, cs:], [[1, 512 - cs]], ALU.is_gt, 0.0,
                                                   base=0, channel_multiplier=-1),
                 reads=[b_spb[r]], writes=[b_spb[r]])

    def Pa(i):
        s = steps[i]; r = i % NR; cs = s["cs"]
        p.op("pe", lambda e: e.matmul(banks[RB][:, cs:512], lhsT=Ui, rhs=spb[r][:, cs:], start=s["first"], stop=False,
                                      skip_group_check=True),
             reads=[b_spb[r], b_c], writes=[bb[RB]])

    def B(i):
        s = steps[i]; r = i % NR; cs = s["cs"]
        p.op("act", lambda e: e.activation(out=t32[r][:, cs:], in_=banks[RB][:, cs:512], func=AF.Exp, scale=-1.0),
             writes=[b_t[r], bb[RB]])
        p.op("dve", lambda e: e.tensor_tensor(ab[r][:, cs:], e32[r][:, cs:], t32[r][:, cs:], ALU.mult),
             reads=[b_e[r], b_t[r]], writes=[b_ab[r]])
        if s["diag"]:
            p.op("pool", lambda e: e.affine_select(ab[r][:, cs:], ab[r][:, cs:], [[1, 512 - cs]], ALU.is_gt, 0.0,
                                                   base=0, channel_multiplier=-1),
                 reads=[b_ab[r]], writes=[b_ab[r]])

    def Pb(i):
        s = steps[i]; r = i % NR; cs = s["cs"]
        p.op("pe", lambda e: e.matmul(banks[RB][:, cs:512], lhsT=Ls, rhs=spb[r][:, cs:], start=False, stop=s["last"],
                                      skip_group_check=True),
             reads=[b_spb[r], b_c], writes=[bb[RB]])

    def AV(i):
        s = steps[i]; r = i % NR; cs = s["cs"]; tq, kb = s["tq"], s["kb"]
        ob = OB[tq % 2]
        p.op("pe", lambda e: e.matmul(banks[ob][:, cs:512], lhsT=bigV[:, kb, :], rhs=ab[r][:, cs:], start=s["first"],
                                      stop=s["last"], skip_group_check=True),
             reads=[b_bigV, b_ab[r]], writes=[bb[ob]])
        if s["last"]:
            o2 = tq % 2
            p.op("dve", lambda e: e.tensor_copy(ot[o2], banks[ob]), writes=[b_ot[o2], bb[ob]])
            p.dma("sp", lambda e: e.dma_start(out=yT_d[:, tq * 512:(tq + 1) * 512], in_=ot[o2]),
                  b_ot[o2], reads=[b_ot[o2]], writes=[b_out])

    QK(0)
    if N > 1:
        QK(1)
    A(0)
    for i in range(N):
        if i + 1 < N:
            A(i + 1)
        Pa(i)
        if i + 2 < N:
            QK(i + 2)
        B(i)
        Pb(i)
        if i >= 1:
            AV(i - 1)
    AV(N - 1)


def emit_mlstm(cx, uqT_d, ukT_d, cw_d, v_d, ig_d, fg_d, gb_d, hT_d, b_in, b_out, S,
               bigQ, bigK, bigV, b_bigQ, b_bigK, b_bigV, bank_bf, bbuf_bf, qscale, tag="ml"):
    nc, p = cx.nc, cx.p
    NCH = S // 128
    assert NCH <= 128
    banks, bb = cx.banks, cx.bbufs
    onesB = cx.onesB
    TriF = cx.sb([128, 128], F32, tag + "_TriF")
    identF = cx.sb([128, 128], F32, tag + "_idF")
    identB = cx.sb([128, 128], BF16, tag + "_idB")
    b_c = p.buf(tag + "const")
    for t_ in (TriF, identF, identB):
        p.op("pool", lambda e, t_=t_: e.memset(t_, 1.0), writes=[b_c])
    p.op("pool", lambda e: e.affine_select(TriF, TriF, [[1, 128]], ALU.is_ge, 0.0, base=0, channel_multiplier=-1),
         reads=[b_c], writes=[b_c])
    p.op("pool", lambda e: e.affine_select(identF, identF, [[1, 128]], ALU.is_equal, 0.0, base=0, channel_multiplier=-1),
         reads=[b_c], writes=[b_c])
    p.op("pool", lambda e: e.affine_select(identB, identB, [[1, 128]], ALU.is_equal, 0.0, base=0, channel_multiplier=-1),
         reads=[b_c], writes=[b_c])
    cw = cx.sb([128, 10], F32, tag + "_cw"); gb = cx.sb([128, 2], F32, tag + "_gb"); ngb = cx.sb([128, 1], F32, tag + "_ngb")
    b_small = p.buf(tag + "small")
    p.dma("sp", lambda e: e.dma_start(out=cw, in_=cw_d), b_small, reads=[b_in], writes=[b_small])
    p.dma("sp", lambda e: e.dma_start(out=gb, in_=gb_d), b_small, reads=[b_in], writes=[b_small])
    p.op("dve", lambda e: e.tensor_scalar(ngb, gb[:, 1:2], -1.0, None, ALU.mult), reads=[b_small], writes=[b_small])
    Gi = cx.sb([128, 128], F32, tag + "_Gi"); Gf = cx.sb([128, 128], F32, tag + "_Gf")
    b_G = p.buf(tag + "G")
    if NCH < 128:
        p.op("pool", lambda e: e.memset(Gi, 0.0), writes=[b_G])
        p.op("pool", lambda e: e.memset(Gf, 0.0), writes=[b_G])
    p.dma("sp", lambda e: e.dma_start(out=Gi[0:NCH, :], in_=ig_d), b_G, reads=[b_in], writes=[b_G])
    p.dma("sp", lambda e: e.dma_start(out=Gf[0:NCH, :], in_=fg_d), b_G, reads=[b_in], writes=[b_G])
    p.op("act", lambda e: e.activation(out=Gf, in_=Gf, func=AF.Exp, bias=ngb[:, 0:1], scale=-1.0), reads=[b_G, b_small], writes=[b_G])
    p.op("act", lambda e: e.activation(out=Gf, in_=Gf, func=AF.Ln, bias=1.0), reads=[b_G], writes=[b_G])
    p.op("dve", lambda e: e.tensor_scalar(Gf, Gf, -1.0, None, ALU.mult), reads=[b_G], writes=[b_G])
    p.op("dve", lambda e: e.tensor_scalar(Gi, Gi, gb[:, 0:1], None, ALU.add), reads=[b_G, b_small], writes=[b_G])
    FT = cx.sb([128, 128], F32, tag + "_FT"); IT = cx.sb([128, 128], F32, tag + "_IT")
    b_FT = p.buf(tag + "FT")
    GB = 6
    p.op("pe", lambda e: e.transpose(banks[GB][:, 0:128], Gf, identF), reads=[b_G, b_c], writes=[bb[GB]])
    p.op("pe", lambda e: e.transpose(banks[GB][:, 128:256], Gi, identF), reads=[b_G, b_c], writes=[bb[GB]])
    p.op("dve", lambda e: e.tensor_copy(FT, banks[GB][:, 0:128]), writes=[b_FT, bb[GB]])
    p.op("dve", lambda e: e.tensor_copy(IT, banks[GB][:, 128:256]), writes=[b_FT, bb[GB]])
    p.op("pe", lambda e: e.matmul(banks[GB][:, 256:384], lhsT=TriF, rhs=FT, start=True, stop=True), reads=[b_FT, b_c], writes=[bb[GB]])
    p.op("pe", lambda e: e.matmul(banks[GB][:, 384:512], lhsT=cx.ones, rhs=FT, start=True, stop=True),
         reads=[b_FT, b_c, cx.b_ones], writes=[bb[GB]])
    biasS = cx.sb([128, 128], F32, tag + "_biasS"); Wcol = cx.sb([128, 128], F32, tag + "_Wcol")
    lnqc = cx.sb([128, 1], F32, tag + "_lnqc")
    decay = cx.sb([128, 128], F32, tag + "_decay")
    b_gs = p.buf(tag + "gs")
    p.op("dve", lambda e: e.tensor_tensor(biasS, IT, banks[GB][:, 256:384], ALU.subtract), reads=[b_FT], writes=[b_gs, bb[GB]])
    p.op("dve", lambda e: e.tensor_tensor(Wcol, biasS, banks[GB][:, 384:512], ALU.add), reads=[b_gs], writes=[b_gs, bb[GB]])
    p.op("act", lambda e: e.activation(out=Wcol, in_=Wcol, func=AF.Exp), reads=[b_gs], writes=[b_gs])
    lnq = float(np.log(qscale))
    p.op("dve", lambda e: e.memset(lnqc, lnq), writes=[b_gs])
    p.op("dve", lambda e: e.tensor_scalar(biasS, biasS, lnq, None, ALU.add), reads=[b_gs], writes=[b_gs])
    p.op("act", lambda e: e.activation(out=decay, in_=banks[GB][:, 384:512], func=AF.Exp), writes=[b_gs, bb[GB]])
    PC = min(2048, S)
    stg = [cx.sb([128, PC + 3], F32, f"{tag}_stg{i}") for i in range(2)]; b_stg = p.bufs(2, tag + "stg")
    acc = [cx.sb([128, PC], F32, f"{tag}_acc{i}") for i in range(2)]; b_acc = p.bufs(2, tag + "acc")
    k2 = 0
    for which, (u_d, big, b_big) in enumerate(((uqT_d, bigQ, b_bigQ), (ukT_d, bigK, b_bigK))):
        wo = which * 4
        for pi in range(S // PC):
            i2 = k2 % 2; k2 += 1
            t0 = pi * PC
            if pi == 0:
                p.op("pool", lambda e, i2=i2: e.memset(stg[i2][:, 0:3], 0.0), writes=[b_stg[i2]])
                p.dma("sp", lambda e, i2=i2, u_d=u_d: e.dma_start(out=stg[i2][:, 3:], in_=u_d[:, 0:PC]),
                      b_stg[i2], reads=[b_in], writes=[b_stg[i2]])
            else:
                p.dma("sp", lambda e, i2=i2, u_d=u_d, t0=t0: e.dma_start(out=stg[i2], in_=u_d[:, t0 - 3:t0 + PC]),
                      b_stg[i2], reads=[b_in], writes=[b_stg[i2]])
            p.op("dve", lambda e, i2=i2, wo=wo, which=which: e.tensor_scalar(
                acc[i2], stg[i2][:, 3:3 + PC], cw[:, wo + 3:wo + 4], cw[:, 8 + which:9 + which], ALU.mult, ALU.add),
                reads=[b_stg[i2], b_small], writes=[b_acc[i2]])
            for kk in (2, 1, 0):
                p.op("dve", lambda e, i2=i2, wo=wo, kk=kk: e.scalar_tensor_tensor(
                    acc[i2], stg[i2][:, kk:kk + PC], cw[:, wo + kk:wo + kk + 1], acc[i2], ALU.mult, ALU.add),
                    reads=[b_stg[i2], b_small, b_acc[i2]], writes=[b_acc[i2]])
            p.op("act", lambda e, i2=i2, t0=t0, big=big: e.activation(out=big[:, t0:t0 + PC], in_=acc[i2], func=AF.Silu),
                 reads=[b_acc[i2]], writes=[b_big])
    v_v = v_d.rearrange("(b p) d -> p b d", p=128)
    nvb = max(1, NCH // 32)
    for i in range(nvb):
        sl = slice(i * (NCH // nvb), (i + 1) * (NCH // nvb))
        p.dma("sp", lambda e, sl=sl: e.dma_start(out=bigV[:, sl, :], in_=v_v[:, sl, :]), b_bigV, reads=[b_in], writes=[b_bigV])
    kw = [cx.sb([128, 128], BF16, f"{tag}_kw{i}") for i in range(2)]; b_kw = p.bufs(2, tag + "kw")
    FTri = [cx.sb([128, 128], F32, f"{tag}_FTri{i}") for i in range(2)]; b_FTri = p.bufs(2, tag + "FTri")
    DT = [cx.sb([128, 128], F32, f"{tag}_DT{i}") for i in range(2)]; b_DT = p.bufs(2, tag + "DT")
    PT = [cx.sb([128, 128], BF16, f"{tag}_PT{i}") for i in range(2)]; b_PT = p.bufs(2, tag + "PT")
    Gx = [cx.sb([128, 128], F32, f"{tag}_Gx{i}") for i in range(2)]; b_Gx = p.bufs(2, tag + "Gx")
    qg = [cx.sb([128, 128], BF16, f"{tag}_qg{i}") for i in range(2)]; b_qg = p.bufs(2, tag + "qg")
    CN = cx.sb([128, 256], F32, tag + "_CN"); b_CN = p.buf(tag + "CN")
    CNb = [cx.sb([128, 256], BF16, f"{tag}_CNb{i}") for i in range(2)]; b_CNb = p.bufs(2, tag + "CNb")
    dn = [cx.sb([128, 128], F32, f"{tag}_dn{i}") for i in range(2)]; b_dn = p.bufs(2, tag + "dn")
    hst = [cx.sb([128, 512], F32, f"{tag}_hst{i}") for i in range(2)]; b_hst = p.bufs(2, tag + "hst")
    SB_, NB_, CB_ = (0, 1), (2, 3), (4, 5)
    p.op("pool", lambda e: e.memset(CN, 0.0), writes=[b_CN])

    def pre(c):
        i2 = c % 2
        csl = slice(c * 128, (c + 1) * 128)
        p.op("pe", lambda e: e.transpose(bank_bf[:, 0:128], bigK[:, csl], identB), reads=[b_bigK, b_c], writes=[bbuf_bf])
        p.op("act", lambda e: e.activation(out=kw[i2], in_=bank_bf[:, 0:128], func=AF.Identity, scale=Wcol[:, c:c + 1]),
             reads=[b_gs], writes=[b_kw[i2], bbuf_bf])
        cb = CB_[i2]
        p.op("pe", [lambda e: e.matmul(banks[cb][:, 0:128], lhsT=kw[i2], rhs=bigV[:, c, :], start=True, stop=True),
                    lambda e: e.matmul(banks[cb][:, 128:256], lhsT=kw[i2], rhs=onesB, start=True, stop=True)],
             reads=[b_kw[i2], b_bigV, cx.b_ones], writes=[bb[cb]])
        p.op("dve", lambda e: e.tensor_scalar(FTri[i2], TriF, FT[:, c:c + 1], None, ALU.mult), reads=[b_c, b_FT], writes=[b_FTri[i2]])
        sb_ = SB_[i2]
        p.op("pe", [lambda e: e.matmul(banks[sb_][:, 0:128], lhsT=bigK[:, csl], rhs=bigQ[:, csl], start=True, stop=True),
                    lambda e: e.matmul(banks[sb_][:, 128:256], lhsT=cx.ones, rhs=FTri[i2], start=True, stop=True)],
             reads=[b_bigK, b_bigQ, b_FTri[i2], cx.b_ones], writes=[bb[sb_]])
        p.op("act", lambda e: e.activation(out=DT[i2], in_=banks[sb_][:, 128:256], func=AF.Exp, bias=biasS[:, c:c + 1]),
             reads=[b_gs], writes=[b_DT[i2], bb[sb_]])
        p.op("act", lambda e: e.activation(out=Gx[i2], in_=banks[sb_][:, 128:256], func=AF.Exp, bias=lnqc[:, 0:1]),
             reads=[b_gs], writes=[b_Gx[i2], bb[sb_]])
        p.op("pool", lambda e: e.affine_select(DT[i2], DT[i2], [[1, 128]], ALU.is_ge, 0.0, base=0, channel_multiplier=-1),
             reads=[b_DT[i2]], writes=[b_DT[i2]])
        p.op("dve", lambda e: e.tensor_tensor(PT[i2], banks[sb_][:, 0:128], DT[i2], ALU.mult),
             reads=[b_DT[i2]], writes=[b_PT[i2], bb[sb_]])
        p.op("pool", lambda e: e.tensor_tensor(qg[i2], bigQ[:, csl], Gx[i2], ALU.mult),
             reads=[b_bigQ, b_Gx[i2]], writes=[b_qg[i2]])

    def post(c):
        i2 = c % 2
        nb_ = NB_[i2]
        sprev = (c - 1) % 2
        fns = [lambda e: e.matmul(banks[nb_][:, 0:128], lhsT=bigV[:, c, :], rhs=PT[i2], start=True, stop=(c == 0))]
        if c > 0:
            fns.append(lambda e: e.matmul(banks[nb_][:, 0:128], lhsT=CNb[sprev][:, 0:128], rhs=qg[i2], start=False, stop=True))
        fns.append(lambda e: e.matmul(banks[nb_][:, 128:256], lhsT=onesB, rhs=PT[i2], start=True, stop=(c == 0)))
        if c > 0:
            fns.append(lambda e: e.matmul(banks[nb_][:, 128:256], lhsT=CNb[sprev][:, 128:256], rhs=qg[i2], start=False, stop=True))
        p.op("pe", fns, reads=[b_bigV, b_PT[i2], b_qg[i2], b_CNb[sprev], cx.b_ones], writes=[bb[nb_]])
        p.op("act", lambda e: e.activation(out=dn[i2], in_=banks[nb_][:, 128:256], func=AF.Abs),
             writes=[b_dn[i2], bb[nb_]])
        p.op("dve", lambda e: e.tensor_scalar(dn[i2], dn[i2], 1.0, None, ALU.max), reads=[b_dn[i2]], writes=[b_dn[i2]])
        p.op("dve", lambda e: e.reciprocal(dn[i2], dn[i2]), reads=[b_dn[i2]], writes=[b_dn[i2]])
        h2 = (c // 4) % 2
        hs = (c % 4) * 128
        p.op("dve", lambda e: e.tensor_tensor(hst[h2][:, hs:hs + 128], banks[nb_][:, 0:128], dn[i2], ALU.mult),
             reads=[b_dn[i2]], writes=[b_hst[h2], bb[nb_]])
        if c % 4 == 3 or c == NCH - 1:
            g0 = (c // 4) * 512
            n = (c % 4 + 1) * 128
            p.dma("sp", lambda e: e.dma_start(out=hT_d[:, g0:g0 + n], in_=hst[h2][:, 0:n]), b_hst[h2],
                  reads=[b_hst[h2]], writes=[b_out])
        cb = CB_[i2]
        p.op("dve", lambda e: e.scalar_tensor_tensor(CN, CN, decay[:, c:c + 1], banks[cb][:, 0:256], ALU.mult, ALU.add),
             reads=[b_CN, b_gs], writes=[b_CN, bb[cb]])
        p.op("act", lambda e: e.activation(out=CNb[i2], in_=CN, func=AF.Copy), reads=[b_CN], writes=[b_CNb[i2]])

    pre(0)
    for c in range(NCH):
        if c + 1 < NCH:
            pre(c + 1)
        post(c)


def emit_outproj(cx, yT_all, oT, xT, woutr, vecs, ngv, ynT, rT, xoT, b_in, b_yn, b_r, b_xo, D, NT, T, ln_eps, alpha, tag="op"):
    nc, p = cx.nc, cx.p
    DC = D // 128
    NS = min(512, T)
    nts = T // NS
    banks, bb = cx.banks, cx.bbufs
    vec = cx.sb([128, 3 * DC], F32, tag + "_vec"); b_vec = p.buf(tag + "vec")
    gw = cx.sb([128, DC], F32, tag + "_gw")
    ng = cx.sb([128, 32], F32, tag + "_ng")
    p.dma("sp", lambda e: e.dma_start(out=vec, in_=vecs), b_vec, writes=[b_vec])
    p.dma("sp", lambda e: e.dma_start(out=ng, in_=ngv), b_vec, writes=[b_vec])
    p.op("dve", lambda e: e.tensor_scalar(gw, vec[:, 0:DC], 1.0, 1.0 / alpha, ALU.add, ALU.mult), reads=[b_vec], writes=[b_vec])
    ybf = cx.sb([128, 16, T], BF16, tag + "_ybf"); b_ybf = p.bufs(nts, tag + "ybf")
    NY = 4
    yin = [cx.sb([128, NS], F32, f"{tag}_yin{i}") for i in range(NY)]; b_yin = p.bufs(NY, tag + "yin")
    yb = [cx.sb([128, NS], BF16, f"{tag}_yb{i}") for i in range(NY)]; b_yb = p.bufs(NY, tag + "yb")
    ysq = [cx.sb([128, NS], BF16, f"{tag}_ysq{i}") for i in range(NY)]; b_ysq = p.bufs(NY, tag + "ysq")
    oin = [cx.sb([128, NS], F32, f"{tag}_oin{i}") for i in range(3)]; b_oin = p.bufs(3, tag + "oin")
    st = [[cx.sb([128, NS], F32, f"{tag}_st{j}_{i}") for i in range(2)] for j in range(3)]
    b_st = p.bufs(2, tag + "st")
    stc = StageC(cx, 16, DC, T, NS, tag + "C", NW2=6)
    yv = yT_all.rearrange("(c p) t -> p c t", p=128)
    ov = oT.rearrange("(c p) t -> p c t", p=128)
    xTv = xT.rearrange("(c p) t -> p c t", p=128)
    rTv = rT.rearrange("(c p) t -> p c t", p=128)
    oTv = xoT.rearrange("(c p) t -> p c t", p=128)
    groups = [(c, 1) for c in range(8)] + [(8 + 2 * h, 2) for h in range(4)]
    ky = ko = kg = 0
    SBK = (0, 1, 2, 3)
    for tt in range(NT // T):
        t0 = tt * T
        for ts in range(nts):
            gsl = slice(t0 + ts * NS, t0 + (ts + 1) * NS)
            sl = slice(ts * NS, (ts + 1) * NS)
            for (c0, GC) in groups:
                g2 = kg % 2; kg += 1
                bs, bq = SBK[2 * g2], SBK[2 * g2 + 1]
                mean, rstd, nmr = st[0][g2], st[1][g2], st[2][g2]
                Dg = GC * 128
                slots = []
                for cc in range(GC):
                    c = c0 + cc
                    i4 = ky % NY; ky += 1
                    slots.append(i4)
                    p.dma("sp", lambda e, i4=i4, c=c, gsl=gsl: e.dma_start(out=yin[i4], in_=yv[:, c, gsl]), b_yin[i4],
                          reads=[b_in], writes=[b_yin[i4]])
                    p.op("act", lambda e, i4=i4: e.activation(out=yb[i4], in_=yin[i4], func=AF.Copy), reads=[b_yin[i4]], writes=[b_yb[i4]])
                    p.op("act", lambda e, i4=i4: e.activation(out=ysq[i4], in_=yin[i4], func=AF.Square), reads=[b_yin[i4]], writes=[b_ysq[i4]])
                    p.op("pe", [lambda e, i4=i4, cc=cc, GC=GC, bs=bs: e.matmul(banks[bs][:, 0:NS], lhsT=cx.onesB, rhs=yb[i4], start=(cc == 0), stop=(cc == GC - 1)),
                                lambda e, i4=i4, cc=cc, GC=GC, bq=bq: e.matmul(banks[bq][:, 0:NS], lhsT=cx.onesB, rhs=ysq[i4], start=(cc == 0), stop=(cc == GC - 1))],
                         reads=[b_yb[i4], b_ysq[i4], cx.b_ones], writes=[bb[bs], bb[bq]])
                bst = b_st[g2]
                p.op("dve", lambda e, mean=mean, bs=bs, Dg=Dg: e.tensor_scalar(mean, banks[bs][:, 0:NS], 1.0 / Dg, None, ALU.mult),
                     writes=[bst, bb[bs]])
                p.op("dve", lambda e, nmr=nmr, mean=mean: e.tensor_tensor(nmr, mean, mean, ALU.mult), reads=[bst], writes=[bst])
                p.op("dve", lambda e, rstd=rstd, bq=bq, Dg=Dg: e.tensor_scalar(rstd, banks[bq][:, 0:NS], 1.0 / Dg, ln_eps, ALU.mult, ALU.add),
                     reads=[bst], writes=[bst, bb[bq]])
                p.op("dve", lambda e, rstd=rstd, nmr=nmr: e.tensor_tensor(rstd, rstd, nmr, ALU.subtract), reads=[bst], writes=[bst])
                p.op("act", lambda e, rstd=rstd: e.activation(out=rstd, in_=rstd, func=AF.Ln), reads=[bst], writes=[bst])
                p.op("act", lambda e, rstd=rstd: e.activation(out=rstd, in_=rstd, func=AF.Exp, scale=-0.5), reads=[bst], writes=[bst])
                p.op("dve", lambda e, nmr=nmr, mean=mean, rstd=rstd: e.scalar_tensor_tensor(nmr, mean, -1.0, rstd, ALU.mult, ALU.mult),
                     reads=[bst], writes=[bst])
                for cc in range(GC):
                    c = c0 + cc
                    i4 = slots[cc]
                    p.op("dve", lambda e, i4=i4, rstd=rstd: e.tensor_tensor(yin[i4], yin[i4], rstd, ALU.mult), reads=[b_yin[i4], bst], writes=[b_yin[i4]])
                    p.op("dve", lambda e, i4=i4, nmr=nmr: e.tensor_tensor(yin[i4], yin[i4], nmr, ALU.add), reads=[b_yin[i4], bst], writes=[b_yin[i4]])
                    if c < 8:
                        p.op("act", lambda e, i4=i4, c=c, sl=sl: e.activation(out=ybf[:, c, sl], in_=yin[i4], func=AF.Identity, scale=ng[:, c:c + 1]),
                             reads=[b_yin[i4], b_vec], writes=[b_ybf[ts]])
                    else:
                        o3 = ko % 3; ko += 1
                        p.dma("sp", lambda e, o3=o3, c=c, gsl=gsl: e.dma_start(out=oin[o3], in_=ov[:, c - 8, gsl]), b_oin[o3],
                              reads=[b_in], writes=[b_oin[o3]])
                        p.op("act", lambda e, o3=o3: e.activation(out=oin[o3], in_=oin[o3], func=AF.Sigmoid), reads=[b_oin[o3]], writes=[b_oin[o3]])
                        p.op("act", lambda e, i4=i4, c=c: e.activation(out=yin[i4], in_=yin[i4], func=AF.Identity, scale=ng[:, c:c + 1]),
                             reads=[b_yin[i4], b_vec], writes=[b_yin[i4]])
                        p.op("dve", lambda e, i4=i4, o3=o3, c=c, sl=sl: e.tensor_tensor(ybf[:, c, sl], yin[i4], oin[o3], ALU.mult),
                             reads=[b_yin[i4], b_oin[o3]], writes=[b_ybf[ts]])
        for it_ in stc.run(ybf, b_ybf, woutr, xTv, rTv, oTv, gw, b_vec, vec[:, DC:2 * DC], vec[:, 2 * DC:3 * DC], b_vec,
                           t0, ln_eps / (alpha * alpha), b_in, b_r, b_xo):
            it_()


def emit_mod(cx, cl, wr, br, outd, b_out, D, NCOL, nlayers, tag="md"):
    nc, p = cx.nc, cx.p
    DC = D // 128
    cs = cx.sb([128, DC], F32, tag + "_c"); cb = cx.sb([128, DC], BF16, tag + "_cb"); b_c = p.buf(tag + "c")
    p.dma("sp", lambda e: e.dma_start(out=cs, in_=cl), b_c, writes=[b_c])
    p.op("act", lambda e: e.activation(out=cb, in_=cs, func=AF.Silu), reads=[b_c], writes=[b_c])
    wt = [cx.sb([128, DC * NCOL], BF16, f"{tag}_w{i}") for i in range(2)]; b_wt = p.bufs(2, tag + "w")
    bt = [cx.sb([1, NCOL], F32, f"{tag}_b{i}") for i in range(2)]; b_bt = p.bufs(2, tag + "b")
    ob = [cx.sb([1, NCOL], F32, f"{tag}_o{i}") for i in range(2)]; b_ob = p.bufs(2, tag + "o")
    kb = 0
    for l in range(nlayers):
        i2 = l % 2
        p.dma("pool", lambda e, i2=i2, l=l: cdma(e, wt[i2], wr[l]), b_wt[i2], writes=[b_wt[i2]])
        p.dma("sp", lambda e, i2=i2, l=l: e.dma_start(out=bt[i2], in_=br[l]), b_bt[i2], writes=[b_bt[i2]])
        n0 = 0
        while n0 < NCOL:
            n = min(512, NCOL - n0)
            bk = kb % 2; kb += 1
            fns = []
            for c in range(DC):
                fns.append(lambda e, i2=i2, c=c, n0=n0, n=n, bk=bk: e.matmul(
                    cx.banks[bk][0:1, 0:n], lhsT=cb[:, c:c + 1], rhs=wt[i2][:, c * NCOL + n0:c * NCOL + n0 + n],
                    start=(c == 0), stop=(c == DC - 1)))
            p.op("pe", fns, reads=[b_c, b_wt[i2]], writes=[cx.bbufs[bk]])
            p.op("dve", lambda e, i2=i2, n0=n0, n=n, bk=bk: e.tensor_tensor(ob[i2][:, n0:n0 + n], cx.banks[bk][0:1, 0:n], bt[i2][:, n0:n0 + n], ALU.add),
                 reads=[b_bt[i2]], writes=[b_ob[i2], cx.bbufs[bk]])
            n0 += n
        p.dma("sp", lambda e, i2=i2, l=l: e.dma_start(out=outd[l], in_=ob[i2]), b_ob[i2], reads=[b_ob[i2]], writes=[b_out])


class Cfg:
    def __init__(self, D=2048, DFF=5632, S=16384, depth=2):
        self.D, self.DFF, self.S, self.depth = D, DFF, S, depth
        self.NT = S // NCORES
        self.T = min(1024, self.NT)
        self.alpha = (2 * depth) ** 0.25
        self.eps = 1e-5
        self.NMOD = 9 * D
        assert self.NMOD % NCORES == 0
        self.NCOL = self.NMOD // NCORES


_CACHE = {}


def _new_nc():
    return bass.Bass("TRN2", target_bir_lowering=False)


def _din(nc, name, shape, dt=F32):
    return nc.dram_tensor(name, list(shape), dt, kind="ExternalInput").ap()


def _dout(nc, name, shape, dt=F32):
    return nc.dram_tensor(name, list(shape), dt, kind="ExternalOutput").ap()


def _dscr(nc, name, shape, dt=F32):
    return nc.dram_tensor(name, list(shape), dt).ap()


def build_mod(cfg):
    nc = _new_nc(); p = Prog(nc); cx = Ctx(nc, p)
    DC = cfg.D // 128
    cl = _din(nc, "cl", [128, DC]); wr = _din(nc, "wr", [cfg.depth, 128, DC * cfg.NCOL]); br = _din(nc, "br", [cfg.depth, 1, cfg.NCOL])
    outd = _dout(nc, "mod", [cfg.depth, 1, cfg.NCOL])
    b_out = p.buf("out")
    emit_mod(cx, cl, wr, br, outd, b_out, cfg.D, cfg.NCOL, cfg.depth)
    p.wait_all("sp", [b_out]); p.emit()
    return nc


def build_ffn(cfg):
    nc = _new_nc(); p = Prog(nc); cx = Ctx(nc, p)
    D, DFF, NT = cfg.D, cfg.DFF, cfg.NT
    DC, HC = D // 128, DFF // 128
    xT = _din(nc, "xT", [D, NT]); w1r = _din(nc, "w1r", [HC, 128, DC * 256]); w2r = _din(nc, "w2r", [DC, 128, HC * 128])
    vecs = _din(nc, "vecs", [128, 5 * DC]); rT = _dscr(nc, "rT", [D, NT]); xoT = _dout(nc, "xoT", [D, NT])
    b_x, b_r, b_xo = p.bufs(3, "dram")
    emit_ffn(cx, xT, w1r, w2r, vecs, rT, xoT, b_x, b_r, b_xo, D, DFF, NT, cfg.T, 0.5, cfg.alpha, cfg.eps)
    p.wait_all("sp", [b_xo]); p.emit()
    return nc


def build_inproj(cfg):
    nc = _new_nc(); p = Prog(nc); cx = Ctx(nc, p)
    D, NT = cfg.D, cfg.NT
    DC = D // 128
    xT = _din(nc, "xT", [D, NT]); vecs = _din(nc, "vecs", [128, 2 * DC])
    wfm = _din(nc, "wfm", [33, 128, DC * 128]); wtm = _din(nc, "wtm", [4, 128, DC * 512])
    qkT = _dout(nc, "qkT", [16, 128, NT], BF16); vsb = _dout(nc, "vsb", [NT, 1024], BF16)
    mqkT = _dout(nc, "mqkT", [8, 128, NT]); vml = _dout(nc, "vml", [NT, 1024], BF16)
    oT = _dout(nc, "oT", [8, 128, NT]); gTd = _dout(nc, "gT", [8, NT])
    b_x, b_out = p.bufs(2, "dram")
    emit_inproj(cx, xT, vecs, wfm, wtm, qkT, vsb, mqkT, vml, oT, gTd, b_x, b_out, D, NT, 128 ** -0.5)
    p.wait_all("sp", [b_out]); p.emit()
    return nc


def build_attn(cfg):
    nc = _new_nc(); p = Prog(nc); cx = Ctx(nc, p, bf16_bank=7)
    S = cfg.S
    NCH = S // 128
    qT = _din(nc, "qT", [128, S], BF16); kT = _din(nc, "kT", [128, S], BF16); v = _din(nc, "v", [S, 128], BF16)
    uq = _din(nc, "uq", [128, S]); uk = _din(nc, "uk", [128, S]); cw = _din(nc, "cw", [128, 10])
    vm = _din(nc, "vm", [S, 128], BF16); ig = _din(nc, "ig", [NCH, 128]); fg = _din(nc, "fg", [NCH, 128]); gb = _din(nc, "gb", [128, 2])
    yT = _dout(nc, "yT", [128, S]); hT = _dout(nc, "hT", [128, S])
    bigQ = cx.sb([128, S], BF16, "bigQ"); bigK = cx.sb([128, S], BF16, "bigK"); bigV = cx.sb([128, S // 128, 128], BF16, "bigV")
    bQ, bK, bV, b_in, b_out = p.bufs(5, "x")
    emit_mlstm(cx, uq, uk, cw, vm, ig, fg, gb, hT, b_in, b_out, S, bigQ, bigK, bigV, bQ, bK, bV, cx.banks[7], cx.bbufs[7], 128 ** -0.5)
    emit_sb_attn(cx, qT, kT, v, yT, b_in, b_out, S, bigQ, bigK, bigV, bQ, bK, bV)
    p.wait_all("sp", [b_out]); p.emit()
    return nc


def build_outproj(cfg):
    nc = _new_nc(); p = Prog(nc); cx = Ctx(nc, p)
    D, NT = cfg.D, cfg.NT
    DC = D // 128
    yT_all = _din(nc, "yT_all", [2048, NT]); oT = _din(nc, "oT", [1024, NT]); xT = _din(nc, "xT", [D, NT])
    woutr = _din(nc, "woutr", [DC, 128, 16 * 128]); vecs = _din(nc, "vecs", [128, 3 * DC]); ngv = _din(nc, "ngv", [128, 32])
    ynT = _dscr(nc, "ynT", [2048, NT]); rT = _dscr(nc, "rT", [D, NT]); xoT = _dout(nc, "xoT", [D, NT])
    b_in, b_yn, b_r, b_xo = p.bufs(4, "dram")
    emit_outproj(cx, yT_all, oT, xT, woutr, vecs, ngv, ynT, rT, xoT, b_in, b_yn, b_r, b_xo, D, NT, NT, cfg.eps, cfg.alpha)
    p.wait_all("sp", [b_xo]); p.emit()
    return nc


def _get(cfg, name, builder):
    key = (name, cfg.D, cfg.DFF, cfg.S, cfg.depth)
    if key not in _CACHE:
        _CACHE[key] = builder(cfg)
    return _CACHE[key]


def _run(nc, in_maps):
    res = run_bass_kernel_spmd(nc, in_maps, core_ids=list(range(NCORES)))
    return res.results


def ffn_host_layout(w1, w2, D, DFF):
    DC, HC = D // 128, DFF // 128
    a = w1[:, :DFF].reshape(DC, 128, HC, 128)
    u = w1[:, DFF:].reshape(DC, 128, HC, 128)
    au = np.stack([a, u], axis=3)
    w1r = np.ascontiguousarray(au.transpose(2, 1, 0, 3, 4)).reshape(HC, 128, DC * 256)
    w2r = np.ascontiguousarray(w2.reshape(HC, 128, DC, 128).transpose(2, 1, 0, 3)).reshape(DC, 128, HC * 128)
    return w1r, w2r


def vec_layout(vs, D):
    DC = D // 128
    return np.ascontiguousarray(np.stack([np.asarray(v, np.float32).reshape(DC, 128).T for v in vs], axis=1)).reshape(128, len(vs) * DC)


def kernel_cfg(cfg, x, c, ada_w, ada_b, ln_g, ln_b, ffn1_w1, ffn1_w2, mix_w_in, mlstm_conv_w, mlstm_conv_b,
               mlstm_gate_b, mix_norm_g, mix_w_out, ffn2_w1, ffn2_w2):
    D, DFF, S, NT, depth = cfg.D, cfg.DFF, cfg.S, cfg.NT, cfg.depth
    DC = D // 128
    f32 = np.float32
    x = np.asarray(x, f32); c = np.asarray(c, f32)
    NCOL = cfg.NCOL
    cl = vec_layout([c[0]], D)
    aw = np.asarray(ada_w, f32).reshape(depth, DC, 128, NCORES, NCOL)
    in_maps = []
    for i in range(NCORES):
        wr = np.ascontiguousarray(aw[:, :, :, i, :].transpose(0, 2, 1, 3)).reshape(depth, 128, DC * NCOL)
        br = np.ascontiguousarray(np.asarray(ada_b, f32).reshape(depth, NCORES, 1, NCOL)[:, i])
        in_maps.append({"cl": cl, "wr": wr, "br": br})
    res = _run(_get(cfg, "mod", build_mod), in_maps)
    mod = np.concatenate([r["mod"].reshape(depth, NCOL) for r in res], axis=1).reshape(depth, 3, 3, D)
    xTs = [np.ascontiguousarray(x[0, i * NT:(i + 1) * NT, :].T) for i in range(NCORES)]
    nc_ffn = _get(cfg, "ffn", build_ffn)
    nc_ip = _get(cfg, "inproj", build_inproj)
    nc_at = _get(cfg, "attn", build_attn)
    nc_op = _get(cfg, "outproj", build_outproj)
    zeros = np.zeros(D, f32)

    def run_ffn(xTs, w1, w2, l, sub):
        w1r, w2r = ffn_host_layout(np.asarray(w1, f32), np.asarray(w2, f32), D, DFF)
        vecs = vec_layout([mod[l, sub, 1], mod[l, sub, 0], mod[l, sub, 2], ln_g[l, sub], ln_b[l, sub]], D)
        res = _run(nc_ffn, [{"xT": xTs[i], "w1r": w1r, "w2r": w2r, "vecs": vecs} for i in range(NCORES)])
        return [r["xoT"] for r in res]

    for l in range(depth):
        xTs = run_ffn(xTs, ffn1_w1[l], ffn1_w2[l], l, 0)
        W = np.asarray(mix_w_in[l], f32)
        fm_cols = [W[:, 0:2048], W[:, 3072:4096], W[:, 5120:6144]]
        gpad = np.zeros((D, 128), f32); gpad[:, 0:8] = W[:, 6144:6152]
        Wfm = np.concatenate(fm_cols + [gpad], axis=1)
        wfm = np.ascontiguousarray(Wfm.reshape(DC, 128, 33, 128).transpose(2, 1, 0, 3)).reshape(33, 128, DC * 128)
        Wtm = np.concatenate([W[:, 2048:3072], W[:, 4096:5120]], axis=1)
        wtm = np.ascontiguousarray(Wtm.reshape(DC, 128, 4, 512).transpose(2, 1, 0, 3)).reshape(4, 128, DC * 512)
        vecs = vec_layout([mod[l, 1, 1], mod[l, 1, 0]], D)
        rip = _run(nc_ip, [{"xT": xTs[i], "vecs": vecs, "wfm": wfm, "wtm": wtm} for i in range(NCORES)])
        qkT = np.concatenate([r["qkT"] for r in rip], axis=2)
        vsb = np.concatenate([r["vsb"] for r in rip], axis=0)
        mqkT = np.concatenate([r["mqkT"] for r in rip], axis=2)
        vml = np.concatenate([r["vml"] for r in rip], axis=0)
        gT = np.concatenate([r["gT"] for r in rip], axis=1)
        cwl = np.asarray(mlstm_conv_w[l], f32); cbl = np.asarray(mlstm_conv_b[l], f32); gbl = np.asarray(mlstm_gate_b[l], f32)
        in_maps = []
        for h in range(NCORES):
            hm, vh = h // 2, h % 2
            cw = np.concatenate([cwl[:, hm * 128:(hm + 1) * 128].T, cwl[:, 512 + hm * 128:512 + (hm + 1) * 128].T,
                                 cbl[hm * 128:(hm + 1) * 128, None], cbl[512 + hm * 128:512 + (hm + 1) * 128, None]], axis=1)
            gb = np.ascontiguousarray(np.broadcast_to(np.array([gbl[hm], gbl[4 + hm]], f32), (128, 2)))
            in_maps.append({
                "qT": np.ascontiguousarray(qkT[h]), "kT": np.ascontiguousarray(qkT[8 + h]),
                "v": np.ascontiguousarray(vsb[:, h * 128:(h + 1) * 128]),
                "uq": np.ascontiguousarray(mqkT[hm]), "uk": np.ascontiguousarray(mqkT[4 + hm]),
                "cw": np.ascontiguousarray(cw, dtype=f32),
                "vm": np.ascontiguousarray(vml[:, hm * 256 + vh * 128:hm * 256 + (vh + 1) * 128]),
                "ig": np.ascontiguousarray(gT[hm].reshape(S // 128, 128)), "fg": np.ascontiguousarray(gT[4 + hm].reshape(S // 128, 128)),
                "gb": gb})
        rat = _run(nc_at, in_maps)
        yall = np.concatenate([r["yT"] for r in rat] + [r["hT"] for r in rat], axis=0)
        ng = np.asarray(mix_norm_g[l], f32)
        ngv = np.concatenate([vec_layout([ng], 2048), np.zeros((128, 16), f32)], axis=1)
        woutr = np.ascontiguousarray(np.asarray(mix_w_out[l], f32).reshape(16, 128, DC, 128).transpose(2, 1, 0, 3)).reshape(DC, 128, 16 * 128)
        vecs = vec_layout([mod[l, 1, 2], ln_g[l, 1], ln_b[l, 1]], D)
        rop = _run(nc_op, [{"yT_all": np.ascontiguousarray(yall[:, i * NT:(i + 1) * NT]),
                            "oT": rip[i]["oT"].reshape(1024, NT), "xT": xTs[i], "woutr": woutr, "vecs": vecs, "ngv": ngv}
                           for i in range(NCORES)])
        xTs = [r["xoT"] for r in rop]
        xTs = run_ffn(xTs, ffn2_w1[l], ffn2_w2[l], l, 2)
    out = np.concatenate([xt.T for xt in xTs], axis=0)[None]
    return np.ascontiguousarray(out, dtype=f32)


def kernel(**inputs):
    cfg = Cfg()
    return kernel_cfg(cfg, **inputs)
```
